# Optimizing a Trainium2 kernel written in Bass

```python
import math
import jax
import jax.numpy as jnp
from jax import lax
import numpy as np

D_MODEL = 1024
BATCH = 8
SEQ = 2048
DEPTH = 4

GRID_W = 64
CTX_LEN = 256
HEAD_DIM = 64
ROPE_THETA = 10000.0
RMS_EPS = 1e-6
LN_EPS = 1e-5
NEG_INF = -1e30
Q_BLOCK = 128

MLA_HEADS = 4
MLA_Q_LORA = 256
MLA_KV_LORA = 128
MLA_NOPE = 64
MLA_ROPE = 32
MLA_V = 64

GQA_Q_HEADS = 4
GQA_KV_HEADS = 2
WINDOW = 128
W_BLOCK = 128

S5_CHANNELS = 256
S5_GROUP = 16
S5_GROUPS = S5_CHANNELS // S5_GROUP
S5_STATE = 64
S5_DT_MIN = 1e-3
S5_DT_MAX = 1e-1

GMLP_WIDTH = 256
GMLP_CHUNK = 128
GMLP_GROUPS = 4
GMLP_GROUP_DIM = GMLP_WIDTH // GMLP_GROUPS

N_BRANCH = 4
BRANCH_WIDTH = 256
D_FF = 2816
MACARON_WEIGHT = 0.5
N_MOD = 9
MOD_CTX_LAST = 5

SEG_WIDTHS = (MLA_KV_LORA, MLA_ROPE, GQA_KV_HEADS * HEAD_DIM, GQA_KV_HEADS * HEAD_DIM, S5_CHANNELS, MLA_Q_LORA, GQA_Q_HEADS * HEAD_DIM, 2 * GMLP_WIDTH, N_BRANCH * D_MODEL)
N_CTX_SEGS = 5
IN_COLS = MLA_KV_LORA + MLA_ROPE + 2 * GQA_KV_HEADS * HEAD_DIM + S5_CHANNELS + MLA_Q_LORA + GQA_Q_HEADS * HEAD_DIM + 2 * GMLP_WIDTH + N_BRANCH * D_MODEL

kernel_name = 'hybrid_prefix_dit_trunk'

F32 = jnp.float32


def split_cols(p, widths):
    parts, start = [], 0
    for w in widths:
        parts.append(p[..., start:start + w])
        start += w
    return parts


def rms_norm(x, gain):
    xf = x.astype(F32)
    y = xf * lax.rsqrt(jnp.mean(xf * xf, axis=-1, keepdims=True) + RMS_EPS)
    return (y * gain.astype(F32)).astype(x.dtype)


def layer_norm(x, gain):
    xf = x.astype(F32)
    xc = xf - jnp.mean(xf, axis=-1, keepdims=True)
    y = xc * lax.rsqrt(jnp.mean(xc * xc, axis=-1, keepdims=True) + LN_EPS)
    return (y * gain.astype(F32)).astype(x.dtype)


def axial_rope_tables(rows, rot_dim):
    axis_dim = rot_dim // 2
    inv_freq = ROPE_THETA ** (-jnp.arange(0, axis_dim, 2, dtype=F32) / axis_dim)
    row = jnp.repeat(jnp.arange(rows, dtype=F32), GRID_W)
    col = jnp.tile(jnp.arange(GRID_W, dtype=F32), rows)
    ang_r = row[:, None] * inv_freq[None, :]
    ang_c = col[:, None] * inv_freq[None, :]
    return (jnp.cos(ang_r), jnp.sin(ang_r), jnp.cos(ang_c), jnp.sin(ang_c))


def _rotate_half(x, cos, sin):
    x1, x2 = jnp.split(x, 2, axis=-1)
    cos = cos[None, :, None, :]
    sin = sin[None, :, None, :]
    return jnp.concatenate([x1 * cos - x2 * sin, x2 * cos + x1 * sin], axis=-1)


def axial_rope(x, tables):
    cos_r, sin_r, cos_c, sin_c = tables
    x_row, x_col = jnp.split(x.astype(F32), 2, axis=-1)
    out = jnp.concatenate([_rotate_half(x_row, cos_r, sin_r), _rotate_half(x_col, cos_c, sin_c)], axis=-1)
    return out.astype(x.dtype)


def blocked_attention(q, k, v, scale):
    b, lq, h, d = q.shape
    nb = lq // Q_BLOCK
    qb = q.reshape(b, nb, Q_BLOCK, h, d).transpose(1, 0, 2, 3, 4)

    def one_block(q_blk):
        s = jnp.einsum('bqhd,bkhd->bhqk', q_blk, k).astype(F32) * scale
        p = jax.nn.softmax(s, axis=-1).astype(v.dtype)
        return jnp.einsum('bhqk,bkhd->bqhd', p, v)

    o = lax.map(one_block, qb)
    return o.transpose(1, 0, 2, 3, 4).reshape(b, lq, h, v.shape[-1])


def mla_queries(q_lora, norm_g, w_uq, rope):
    b, n, _ = q_lora.shape
    q = (rms_norm(q_lora, norm_g) @ w_uq).reshape(b, n, MLA_HEADS, MLA_NOPE + MLA_ROPE)
    q_nope, q_pe = q[..., :MLA_NOPE], q[..., MLA_NOPE:]
    if rope is not None:
        q_pe = axial_rope(q_pe, rope)
    return jnp.concatenate([q_nope, q_pe], axis=-1)


def mla_keys_values(kv_lora, k_pe, norm_g, w_ukv, rope):
    b, n, _ = kv_lora.shape
    kv = (rms_norm(kv_lora, norm_g) @ w_ukv).reshape(b, n, MLA_HEADS, MLA_NOPE + MLA_V)
    k_nope, v = kv[..., :MLA_NOPE], kv[..., MLA_NOPE:]
    k_pe = k_pe[:, :, None, :]
    if rope is not None:
        k_pe = axial_rope(k_pe, rope)
    k = jnp.concatenate([k_nope, jnp.broadcast_to(k_pe, (b, n, MLA_HEADS, MLA_ROPE))], axis=-1)
    return k, v


def window_gqa_latent(q, k, v, kc, vc, sink):
    b, n, hq, d = q.shape
    hkv = k.shape[2]
    g = hq // hkv
    nb = n // W_BLOCK
    nc = kc.shape[1]
    scale = d ** -0.5

    def band(t):
        tp = jnp.pad(t, ((0, 0), (W_BLOCK, W_BLOCK), (0, 0), (0, 0)))
        tp = tp.reshape(b, nb + 2, W_BLOCK, hkv, d)
        return jnp.concatenate([tp[:, :-2], tp[:, 1:-1], tp[:, 2:]], axis=2)

    kb, vb = band(k), band(v)
    qb = q.reshape(b, nb, W_BLOCK, hkv, g, d)
    s_band = jnp.einsum('bnqkgd,bnjkd->bnkgqj', qb, kb).astype(F32) * scale
    qpos = jnp.arange(nb)[:, None] * W_BLOCK + jnp.arange(W_BLOCK)[None, :]
    kpos = jnp.arange(nb)[:, None] * W_BLOCK - W_BLOCK + jnp.arange(3 * W_BLOCK)[None, :]
    valid = ((jnp.abs(qpos[:, :, None] - kpos[:, None, :]) <= WINDOW)
             & (kpos[:, None, :] >= 0) & (kpos[:, None, :] < n))
    s_band = jnp.where(valid[None, :, None, None], s_band, NEG_INF)
    s_ctx = jnp.einsum('bnqkgd,bjkd->bnkgqj', qb, kc).astype(F32) * scale
    sink_l = jnp.broadcast_to(sink.astype(F32).reshape(1, 1, hkv, g, 1, 1), s_band.shape[:-1] + (1,))
    probs = jax.nn.softmax(jnp.concatenate([s_band, s_ctx, sink_l], axis=-1), axis=-1).astype(v.dtype)
    nbk = 3 * W_BLOCK
    o = (jnp.einsum('bnkgqj,bnjkd->bnqkgd', probs[..., :nbk], vb)
         + jnp.einsum('bnkgqj,bjkd->bnqkgd', probs[..., nbk:nbk + nc], vc))
    return o.reshape(b, n, hq, d)


def sink_gqa_context(q, k, v, sink):
    b, n, hq, d = q.shape
    hkv = k.shape[2]
    g = hq // hkv
    qg = q.reshape(b, n, hkv, g, d)
    s = jnp.einsum('bqkgd,bjkd->bkgqj', qg, k).astype(F32) * d ** -0.5
    sink_l = jnp.broadcast_to(sink.astype(F32).reshape(1, hkv, g, 1, 1), s.shape[:-1] + (1,))
    p = jax.nn.softmax(jnp.concatenate([s, sink_l], axis=-1), axis=-1)[..., :-1].astype(v.dtype)
    return jnp.einsum('bkgqj,bjkd->bqkgd', p, v).reshape(b, n, hq, d)


def s5_discretize(lam_re, lam_im, log_dt, b_re, b_im):
    lam_re = jnp.minimum(lam_re.astype(F32), -1e-4)
    lam_im = lam_im.astype(F32)
    dt = jnp.exp(log_dt.astype(F32))[:, None]
    mag = jnp.exp(lam_re * dt)
    a_re = mag * jnp.cos(lam_im * dt)
    a_im = mag * jnp.sin(lam_im * dt)
    nr, ni = a_re - 1.0, a_im
    den = lam_re * lam_re + lam_im * lam_im
    coef_re = (nr * lam_re + ni * lam_im) / den
    coef_im = (ni * lam_re - nr * lam_im) / den
    b_re, b_im = b_re.astype(F32), b_im.astype(F32)
    bb_re = coef_re[..., None] * b_re - coef_im[..., None] * b_im
    bb_im = coef_re[..., None] * b_im + coef_im[..., None] * b_re
    return a_re, a_im, bb_re, bb_im


def _complex_affine_combine(e1, e2):
    a1r, a1i, b1r, b1i = e1
    a2r, a2i, b2r, b2i = e2
    return (a2r * a1r - a2i * a1i, a2r * a1i + a2i * a1r,
            a2r * b1r - a2i * b1i + b2r, a2r * b1i + a2i * b1r + b2i)


def s5_states(u, disc, s0, reverse):
    a_re, a_im, bb_re, bb_im = disc
    bu_re = jnp.einsum('blgh,gph->blgp', u, bb_re)
    bu_im = jnp.einsum('blgh,gph->blgp', u, bb_im)
    if s0 is not None:
        s0_re, s0_im = s0
        first = -1 if reverse else 0
        bu_re = bu_re.at[:, first].add(a_re * s0_re - a_im * s0_im)
        bu_im = bu_im.at[:, first].add(a_re * s0_im + a_im * s0_re)
    elems = (jnp.broadcast_to(a_re, bu_re.shape), jnp.broadcast_to(a_im, bu_re.shape), bu_re, bu_im)
    _, _, s_re, s_im = lax.associative_scan(_complex_affine_combine, elems, reverse=reverse, axis=1)
    return s_re, s_im


def s5_readout(states, c_re, c_im):
    s_re, s_im = states
    return jnp.einsum('blgp,ghp->blgh', s_re, c_re) - jnp.einsum('blgp,ghp->blgh', s_im, c_im)


def s5_output(u, fwd, bwd, c_re, c_im, d_skip, w_glu, b_glu, out_dtype):
    c_re, c_im = c_re.astype(F32), c_im.astype(F32)
    y = (s5_readout(fwd, c_re[0], c_im[0]) + s5_readout(bwd, c_re[1], c_im[1])
         + d_skip.astype(F32).reshape(S5_GROUPS, S5_GROUP) * u)
    y = jax.nn.gelu(y.reshape(u.shape[0], u.shape[1], S5_CHANNELS))
    a, g = jnp.split(y @ w_glu.astype(F32) + b_glu.astype(F32), 2, axis=-1)
    return (a * jax.nn.sigmoid(g)).astype(out_dtype)


def chunk_gmlp(z, norm_g, w_s, b_s):
    b, n, _ = z.shape
    u, v = jnp.split(jax.nn.gelu(z), 2, axis=-1)
    v = layer_norm(v, norm_g).reshape(b, n // GMLP_CHUNK, GMLP_CHUNK, GMLP_GROUPS, GMLP_GROUP_DIM)
    mixed = jnp.einsum('gij,bcjgd->bcigd', w_s, v) + b_s.T[:, :, None]
    return u * mixed.reshape(b, n, GMLP_WIDTH)


def merge_branches(branches, gate_logits, w_branch, w_out):
    gates = jax.nn.sigmoid(gate_logits.astype(F32)).astype(gate_logits.dtype)
    gates = gates.reshape(gate_logits.shape[:-1] + (N_BRANCH, D_MODEL))
    merged = sum(gates[..., i, :] * (y @ w_branch[i]) for i, y in enumerate(branches))
    return merged @ w_out


def token_mix(hl, hc, tab_mla, tab_gqa, lp, need_ctx):
    b, n, _ = hl.shape
    nc = hc.shape[1]
    kvl_l, kpe_l, gk_l, gv_l, u_l, ql_l, gq_l, z_l, gate_l = split_cols(hl @ lp['w_in'], SEG_WIDTHS)
    ctx_widths = SEG_WIDTHS if need_ctx else SEG_WIDTHS[:N_CTX_SEGS]
    parts_c = split_cols(hc @ lp['w_in'][:, :sum(ctx_widths)], ctx_widths)
    kvl_c, kpe_c, gk_c, gv_c, u_c = parts_c[:N_CTX_SEGS]

    mla_scale = (MLA_NOPE + MLA_ROPE) ** -0.5
    k_lat, v_lat = mla_keys_values(kvl_l, kpe_l, lp['mla_kv_norm'], lp['mla_w_ukv'], tab_mla)
    k_ctx, v_ctx = mla_keys_values(kvl_c, kpe_c, lp['mla_kv_norm'], lp['mla_w_ukv'], None)
    q_lat = mla_queries(ql_l, lp['mla_q_norm'], lp['mla_w_uq'], tab_mla)
    a_l = blocked_attention(q_lat, jnp.concatenate([k_lat, k_ctx], axis=1),
                            jnp.concatenate([v_lat, v_ctx], axis=1), mla_scale).reshape(b, n, BRANCH_WIDTH)

    gq_lat = axial_rope(gq_l.reshape(b, n, GQA_Q_HEADS, HEAD_DIM), tab_gqa)
    gk_lat = axial_rope(gk_l.reshape(b, n, GQA_KV_HEADS, HEAD_DIM), tab_gqa)
    gv_lat = gv_l.reshape(b, n, GQA_KV_HEADS, HEAD_DIM)
    gk_ctx = gk_c.reshape(b, nc, GQA_KV_HEADS, HEAD_DIM)
    gv_ctx = gv_c.reshape(b, nc, GQA_KV_HEADS, HEAD_DIM)
    b_l = window_gqa_latent(gq_lat, gk_lat, gv_lat, gk_ctx, gv_ctx, lp['gqa_sink']).reshape(b, n, BRANCH_WIDTH)

    disc_f = s5_discretize(lp['s5_lam_re'][0], lp['s5_lam_im'][0], lp['s5_log_dt'][0], lp['s5_b_re'][0], lp['s5_b_im'][0])
    disc_b = s5_discretize(lp['s5_lam_re'][1], lp['s5_lam_im'][1], lp['s5_log_dt'][1], lp['s5_b_re'][1], lp['s5_b_im'][1])
    uc = u_c.astype(F32).reshape(b, nc, S5_GROUPS, S5_GROUP)
    ul = u_l.astype(F32).reshape(b, n, S5_GROUPS, S5_GROUP)
    fwd_c = s5_states(uc, disc_f, None, False)
    bwd_c = s5_states(uc, disc_b, None, True)
    fwd_l = s5_states(ul, disc_f, (fwd_c[0][:, -1], fwd_c[1][:, -1]), False)
    bwd_l = s5_states(ul, disc_b, (bwd_c[0][:, 0], bwd_c[1][:, 0]), True)
    c_l = s5_output(ul, fwd_l, bwd_l, lp['s5_c_re'], lp['s5_c_im'], lp['s5_d'], lp['s5_w_glu'], lp['s5_b_glu'], hl.dtype)

    d_l = chunk_gmlp(z_l, lp['gmlp_norm'], lp['gmlp_w_s'], lp['gmlp_b_s'])

    yl = merge_branches((a_l, b_l, c_l, d_l), gate_l, lp['w_branch'], lp['w_out'])
    if not need_ctx:
        return yl, None

    ql_c, gq_c, z_c, gate_c = parts_c[N_CTX_SEGS:]
    q_ctx = mla_queries(ql_c, lp['mla_q_norm'], lp['mla_w_uq'], None)
    a_c = blocked_attention(q_ctx, k_ctx, v_ctx, mla_scale).reshape(b, nc, BRANCH_WIDTH)
    b_c = sink_gqa_context(gq_c.reshape(b, nc, GQA_Q_HEADS, HEAD_DIM), gk_ctx, gv_ctx, lp['gqa_sink']).reshape(b, nc, BRANCH_WIDTH)
    c_c = s5_output(uc, fwd_c, bwd_c, lp['s5_c_re'], lp['s5_c_im'], lp['s5_d'], lp['s5_w_glu'], lp['s5_b_glu'], hc.dtype)
    d_c = chunk_gmlp(z_c, lp['gmlp_norm'], lp['gmlp_w_s'], lp['gmlp_b_s'])
    yc = merge_branches((a_c, b_c, c_c, d_c), gate_c, lp['w_branch'], lp['w_out'])
    return yl, yc


def mod_vec(mod, i):
    return mod[:, i][:, None, :]


def modulated_norm(xs, gain, mod, s):
    return rms_norm(xs, gain) * (1.0 + mod_vec(mod, 3 * s + 1)) + mod_vec(mod, 3 * s)


def gated_residual(xs, y, gain, mod, s, weight):
    return xs + weight * mod_vec(mod, 3 * s + 2) * rms_norm(y, gain)


def ffn_sublayer(xs, mod, s, g_pre, g_post, w_in_f, w_out_f):
    a, g = jnp.split(modulated_norm(xs, g_pre, mod, s) @ w_in_f, 2, axis=-1)
    return gated_residual(xs, (jax.nn.silu(a) * g) @ w_out_f, g_post, mod, s, MACARON_WEIGHT)


def setup_inputs(seed: int = 0) -> dict:
    key = jax.random.key(seed)
    keys = iter(jax.random.split(key, 40))

    def nrm(shape, scale):
        return scale * jax.random.normal(next(keys), shape, F32)

    def gain(shape):
        return 1.0 + nrm(shape, 0.05)

    hg, p, g = S5_GROUP, S5_STATE, S5_GROUPS
    lam_im0 = jnp.broadcast_to(jnp.pi * jnp.arange(p, dtype=F32), (DEPTH, 2, g, p))
    return {
        'x': nrm((BATCH, SEQ, D_MODEL), 1.0),
        'c': nrm((BATCH, D_MODEL), 1.0),
        'ctx': nrm((BATCH, CTX_LEN, D_MODEL), 1.0),
        'c_ctx': nrm((D_MODEL,), 1.0),
        'w_ada': nrm((DEPTH, D_MODEL, N_MOD * D_MODEL), D_MODEL ** -0.5),
        'b_ada': nrm((DEPTH, N_MOD * D_MODEL), 0.01),
        'norm_pre': gain((DEPTH, 3, D_MODEL)),
        'norm_post': gain((DEPTH, 3, D_MODEL)),
        'w_ffn_in': nrm((DEPTH, 2, D_MODEL, 2 * D_FF), D_MODEL ** -0.5),
        'w_ffn_out': nrm((DEPTH, 2, D_FF, D_MODEL), D_FF ** -0.5),
        'w_in': nrm((DEPTH, D_MODEL, IN_COLS), D_MODEL ** -0.5),
        'mla_q_norm': gain((DEPTH, MLA_Q_LORA)),
        'mla_w_uq': nrm((DEPTH, MLA_Q_LORA, MLA_HEADS * (MLA_NOPE + MLA_ROPE)), MLA_Q_LORA ** -0.5),
        'mla_kv_norm': gain((DEPTH, MLA_KV_LORA)),
        'mla_w_ukv': nrm((DEPTH, MLA_KV_LORA, MLA_HEADS * (MLA_NOPE + MLA_V)), MLA_KV_LORA ** -0.5),
        'gqa_sink': nrm((DEPTH, GQA_Q_HEADS), 0.5),
        's5_lam_re': -0.5 + nrm((DEPTH, 2, g, p), 0.01),
        's5_lam_im': lam_im0 + nrm((DEPTH, 2, g, p), 0.01),
        's5_log_dt': jax.random.uniform(next(keys), (DEPTH, 2, g), F32, math.log(S5_DT_MIN), math.log(S5_DT_MAX)),
        's5_b_re': nrm((DEPTH, 2, g, p, hg), (2 * hg) ** -0.5),
        's5_b_im': nrm((DEPTH, 2, g, p, hg), (2 * hg) ** -0.5),
        's5_c_re': nrm((DEPTH, 2, g, hg, p), p ** -0.5),
        's5_c_im': nrm((DEPTH, 2, g, hg, p), p ** -0.5),
        's5_d': nrm((DEPTH, S5_CHANNELS), 1.0),
        's5_w_glu': nrm((DEPTH, S5_CHANNELS, 2 * S5_CHANNELS), S5_CHANNELS ** -0.5),
        's5_b_glu': nrm((DEPTH, 2 * S5_CHANNELS), 0.01),
        'gmlp_norm': gain((DEPTH, GMLP_WIDTH)),
        'gmlp_w_s': nrm((DEPTH, GMLP_GROUPS, GMLP_CHUNK, GMLP_CHUNK), 0.5 * GMLP_CHUNK ** -0.5),
        'gmlp_b_s': 1.0 + nrm((DEPTH, GMLP_GROUPS, GMLP_CHUNK), 0.01),
        'w_branch': nrm((DEPTH, N_BRANCH, BRANCH_WIDTH, D_MODEL), BRANCH_WIDTH ** -0.5),
        'w_out': nrm((DEPTH, D_MODEL, D_MODEL), D_MODEL ** -0.5),
    }


def reference(x, c, ctx, c_ctx, w_ada, b_ada, norm_pre, norm_post, w_ffn_in, w_ffn_out, w_in,
              mla_q_norm, mla_w_uq, mla_kv_norm, mla_w_ukv, gqa_sink,
              s5_lam_re, s5_lam_im, s5_log_dt, s5_b_re, s5_b_im, s5_c_re, s5_c_im, s5_d, s5_w_glu, s5_b_glu,
              gmlp_norm, gmlp_w_s, gmlp_b_s, w_branch, w_out):
    b, n, _ = x.shape
    rows = n // GRID_W
    tab_mla = axial_rope_tables(rows, MLA_ROPE)
    tab_gqa = axial_rope_tables(rows, HEAD_DIM)
    silu_c = jax.nn.silu(c)
    silu_cc = jax.nn.silu(c_ctx)[None, :]
    xl, xc = x, ctx
    for l in range(DEPTH):
        last = l == DEPTH - 1
        mod_l = (silu_c @ w_ada[l] + b_ada[l]).reshape(b, N_MOD, D_MODEL)
        n_mod_c = MOD_CTX_LAST if last else N_MOD
        mod_c = (silu_cc @ w_ada[l][:, :n_mod_c * D_MODEL] + b_ada[l][:n_mod_c * D_MODEL]).reshape(1, n_mod_c, D_MODEL)
        lp = dict(w_in=w_in[l], mla_q_norm=mla_q_norm[l], mla_w_uq=mla_w_uq[l],
                  mla_kv_norm=mla_kv_norm[l], mla_w_ukv=mla_w_ukv[l], gqa_sink=gqa_sink[l],
                  s5_lam_re=s5_lam_re[l], s5_lam_im=s5_lam_im[l], s5_log_dt=s5_log_dt[l],
                  s5_b_re=s5_b_re[l], s5_b_im=s5_b_im[l], s5_c_re=s5_c_re[l], s5_c_im=s5_c_im[l],
                  s5_d=s5_d[l], s5_w_glu=s5_w_glu[l], s5_b_glu=s5_b_glu[l],
                  gmlp_norm=gmlp_norm[l], gmlp_w_s=gmlp_w_s[l], gmlp_b_s=gmlp_b_s[l],
                  w_branch=w_branch[l], w_out=w_out[l])
        xl = ffn_sublayer(xl, mod_l, 0, norm_pre[l, 0], norm_post[l, 0], w_ffn_in[l, 0], w_ffn_out[l, 0])
        xc = ffn_sublayer(xc, mod_c, 0, norm_pre[l, 0], norm_post[l, 0], w_ffn_in[l, 0], w_ffn_out[l, 0])
        hl = modulated_norm(xl, norm_pre[l, 1], mod_l, 1)
        hc = modulated_norm(xc, norm_pre[l, 1], mod_c, 1)
        yl, yc = token_mix(hl, hc, tab_mla, tab_gqa, lp, not last)
        xl = gated_residual(xl, yl, norm_post[l, 1], mod_l, 1, 1.0)
        xl = ffn_sublayer(xl, mod_l, 2, norm_pre[l, 2], norm_post[l, 2], w_ffn_in[l, 1], w_ffn_out[l, 1])
        if not last:
            xc = gated_residual(xc, yc, norm_post[l, 1], mod_c, 1, 1.0)
            xc = ffn_sublayer(xc, mod_c, 2, norm_pre[l, 2], norm_post[l, 2], w_ffn_in[l, 1], w_ffn_out[l, 1])
    return xl
```

```python
import numpy as np
import ml_dtypes
from contextlib import ExitStack, contextmanager
import concourse.bass as bass
import concourse.mybir as mybir
from concourse.bass_utils import run_bass_kernel_spmd

F32 = mybir.dt.float32
BF16 = mybir.dt.bfloat16
ALU = mybir.AluOpType
AF = mybir.ActivationFunctionType
AX = mybir.AxisListType

D = 1024
T = 2304
NCTX = 256
NLAT = 2048
DFF = 2816
DEPTH = 4
EPS = 1e-6
SAME_ENG_SYNC = True

E_KVL, E_QL, E_KPA, E_KPB = 0, 128, 384, 416
E_GQA, E_GQB, E_GKA, E_GKB, E_GV, E_U, E_Z, E_GATE = 512, 768, 1024, 1152, 1280, 1408, 1664, 2176
NEXT = 6272
TBLK = [(0, 256), (256, 768), (768, 1280), (1280, 1792), (1792, 2304)]
GC = 1.5957691216057308


class Res:
    __slots__ = ("w", "r")

    def __init__(self):
        self.w = None
        self.r = {}


class Eng:
    def __init__(self, obj, pe=False):
        self.obj = obj
        self.pe = pe
        self.sem = None
        self.cnt = 0
        self.known = {}
        self.pend = []


class Chan:
    def __init__(self, sem):
        self.sem = sem
        self.total = 0


class Tl:
    def __init__(self, h):
        self.h = h
        self._r = {}

    def __getitem__(self, k):
        return self.h[k]

    def R(self, key=0):
        r = self._r.get(key)
        if r is None:
            r = self._r[key] = Res()
        return r


class KB:
    def __init__(self, nc, es):
        self.nc = nc
        self.es = es
        self.stack = [es]
        self.E = {"pe": Eng(nc.tensor, True), "act": Eng(nc.scalar), "dve": Eng(nc.vector),
                  "pool": Eng(nc.gpsimd), "sp": Eng(nc.sync)}
        self.nsem = 0
        self.nname = 0
        self.rotate()
        self.wch = [Chan(self.newsem()) for _ in range(6)]
        self.sch = [Chan(self.newsem()) for _ in range(4)]
        self.wi = 0
        self.si = 0
        self.ps = Tl(es.enter_context(nc.psum_tensor("ps", [128, 8, 512], F32)))
        self.bank_i = 0
        self.banks_rot = list(range(8))

    def newsem(self):
        self.nsem += 1
        return self.es.enter_context(self.nc.semaphore(f"sem{self.nsem}"))

    def rotate(self):
        for e in self.E.values():
            assert not e.pend
            e.sem = self.newsem()
            e.cnt = 0

    def sb(self, shape, dt, name=None):
        self.nname += 1
        return Tl(self.stack[-1].enter_context(self.nc.sbuf_tensor(f"{name or 't'}_{self.nname}", list(shape), dt)))

    @contextmanager
    def scope(self):
        st = ExitStack()
        self.stack.append(st)
        try:
            yield
        finally:
            self.barrier()
            self.stack.pop()
            st.close()

    def barrier(self):
        chans = self.wch + self.sch
        for e in self.E.values():
            assert not e.pend
        for e in self.E.values():
            for f in self.E.values():
                if f is e or f.cnt == 0:
                    continue
                if e.known.get(id(f.sem), 0) < f.cnt:
                    e.obj.wait_ge(f.sem, f.cnt)
                    e.known[id(f.sem)] = f.cnt
            for c in chans:
                if c.total and e.known.get(id(c.sem), 0) < c.total:
                    e.obj.wait_ge(c.sem, c.total)
                    e.known[id(c.sem)] = c.total

    def bank(self):
        b = self.banks_rot[self.bank_i % len(self.banks_rot)]
        self.bank_i += 1
        return b

    def _waits(self, eng, reads, writes):
        deps = {}

        def add(sv):
            sem, val = sv
            k = id(sem)
            if k not in deps or deps[k][1] < val:
                deps[k] = (sem, val)

        for r in reads:
            if r.w is not None:
                add(r.w)
        for w in writes:
            if w.w is not None:
                add(w.w)
            for sv in w.r.values():
                add(sv)
        for k, (sem, val) in deps.items():
            if sem is eng.sem and (eng.pe or not SAME_ENG_SYNC):
                continue
            if eng.known.get(k, 0) >= val:
                continue
            eng.obj.wait_ge(sem, val)
            eng.known[k] = val

    def emit(self, en, fn, reads=(), writes=(), sig=True):
        eng = self.E[en]
        self._waits(eng, reads, writes)
        ins = fn(eng.obj)
        eng.pend.append((reads, writes))
        if sig:
            eng.cnt += 1
            ins.then_inc(eng.sem, 1)
            st = (eng.sem, eng.cnt)
            for rs, ws in eng.pend:
                for w in ws:
                    w.w = st
                    w.r = {}
                for r in rs:
                    r.r[id(eng.sem)] = st
            eng.pend = []
        return ins

    def dma(self, q, out, in_, reads=(), writes=(), weight=False):
        eng = self.E[q]
        if weight:
            ch = self.wch[self.wi % len(self.wch)]
            self.wi += 1
        else:
            ch = self.sch[self.si % len(self.sch)]
            self.si += 1
        self._waits(eng, reads, writes)
        if ch.total and eng.known.get(id(ch.sem), 0) < ch.total:
            eng.obj.wait_ge(ch.sem, ch.total)
            eng.known[id(ch.sem)] = ch.total
        ins = eng.obj.dma_start(out=out, in_=in_)
        ins.then_inc(ch.sem, 16)
        ch.total += 16
        st = (ch.sem, ch.total)
        for w in writes:
            w.w = st
            w.r = {}
        for r in reads:
            r.r[id(ch.sem)] = st

    def mm(self, out, lhsT, rhs, start, stop, r, w, sig=None):
        if sig is None:
            sig = stop
        self.emit("pe", lambda e: e.matmul(out, lhsT, rhs, start=start, stop=stop), r, w, sig)

    def act(self, out, in_, func, r, w, bias=None, scale=1.0):
        kw = {}
        if bias is not None:
            kw["bias"] = bias
        self.emit("act", lambda e: e.activation(out=out, in_=in_, func=func, scale=scale, **kw), r, w)

    def tt(self, out, in0, in1, op, r, w, en="dve"):
        self.emit(en, lambda e: e.tensor_tensor(out=out, in0=in0, in1=in1, op=op), r, w)

    def ts(self, out, in0, s1, s2, op0, op1, r, w, en="dve"):
        if s2 is None:
            self.emit(en, lambda e: e.tensor_scalar(out=out, in0=in0, scalar1=s1, scalar2=None, op0=op0), r, w)
        else:
            self.emit(en, lambda e: e.tensor_scalar(out=out, in0=in0, scalar1=s1, scalar2=s2, op0=op0, op1=op1), r, w)

    def stt(self, out, in0, sc, in1, op0, op1, r, w):
        self.emit("dve", lambda e: e.scalar_tensor_tensor(out=out, in0=in0, scalar=sc, in1=in1, op0=op0, op1=op1), r, w)

    def cp(self, out, in_, r, w, en="dve"):
        self.emit(en, lambda e: e.tensor_copy(out=out, in_=in_), r, w)

    def recip(self, out, in_, r, w):
        self.emit("dve", lambda e: e.reciprocal(out=out, in_=in_), r, w)

    def memset(self, ap, val, w, en="dve"):
        self.emit(en, lambda e: e.memset(ap, val), (), w)


def split_tok(t0, t1):
    out = []
    if t0 < NCTX:
        out.append((t0, min(t1, NCTX), 1))
    if t1 > NCTX:
        out.append((max(t0, NCTX), t1, 0))
    return out


def build(depth_run, dbg=None):
    nc = bass.Bass("TRN2", target_bir_lowering=False)
    es = ExitStack()
    dr = lambda name, shape, dt=F32: nc.dram_tensor(name, list(shape), dt, kind="ExternalInput").ap()
    I = {}
    I["xT"] = dr("xT", [128, 8, T])
    I["cT"] = dr("cT", [128, 8, 2])
    I["w_ada"] = dr("w_ada", [DEPTH, D, 9 * D])
    I["vecs"] = dr("vecs", [DEPTH, 128, 120])
    I["w_ffn_in"] = dr("w_ffn_in", [DEPTH, 2, D, 2 * DFF])
    I["w_ffn_out"] = dr("w_ffn_out", [DEPTH, 2, DFF, D])
    I["w_in"] = dr("w_in", [DEPTH, D, NEXT])
    I["w_uq"] = dr("w_uq", [DEPTH, 256, 512])
    I["w_ukv"] = dr("w_ukv", [DEPTH, 128, 512])
    I["mvec"] = dr("mvec", [DEPTH, 128, 16])
    I["ropeC"] = dr("ropeC", [128, T])
    I["ropeS"] = dr("ropeS", [128, T])
    I["cmat"] = dr("cmat", [128, 3, 128])
    I["w_branch"] = dr("w_branch", [DEPTH, 4, 256, D])
    I["w_out"] = dr("w_out", [DEPTH, D, D])
    I["wsT"] = dr("wsT", [DEPTH, 128, 4, 128])
    I["bsT"] = dr("bsT", [DEPTH, 128, 2, 128])
    I["s5p"] = dr("s5p", [DEPTH, 2, 128, 8, 4])
    I["s5b"] = dr("s5b", [DEPTH, 2, 2, 128, 8, 128])
    I["s5c"] = dr("s5c", [DEPTH, 2, 2, 128, 8, 128])
    I["w_glu"] = dr("w_glu", [DEPTH, 256, 512])
    outT = nc.dram_tensor("outT", [128, 8, NLAT], F32, kind="ExternalOutput").ap()
    dbg_aps = {}
    if dbg:
        for name, shape in dbg.items():
            dbg_aps[name] = nc.dram_tensor(name, list(shape), F32, kind="ExternalOutput").ap()

    kb = KB(nc, es)
    ps = kb.ps
    PR = [ps.R(b) for b in range(8)]

    xT = kb.sb([128, 8, T], F32, "xT")
    ones = kb.sb([128, 128], BF16, "ones")
    onesf = kb.sb([128, 512], F32, "onesf")
    cmat_f = kb.sb([128, 3, 128], F32, "cmatf")
    cmat = kb.sb([128, 3, 128], BF16, "cmat")
    ropeC = kb.sb([128, T], BF16, "ropeC")
    ropeS = kb.sb([128, T], BF16, "ropeS")
    sc = kb.sb([128, 8, 2], BF16, "sc")
    vecs = kb.sb([128, 120], F32, "vecs")
    mvec = kb.sb([128, 16], F32, "mvec")
    modT = kb.sb([128, 72, 2], F32, "modT")
    modD = kb.sb([128, 3, 2, 3, 8], F32, "modD")
    BR = {}

    XR = lambda blk: xT.R(blk)

    def xres(t0, t1):
        return [xT.R(c) for c in range(t0 // 288, (t1 - 1) // 288 + 1)]

    for c in range(8):
        kb.dma("sp", xT[:, :, c * 288:(c + 1) * 288], I["xT"][:, :, c * 288:(c + 1) * 288], (), [xT.R(c)])
    kb.memset(ones[:], 1.0, [ones.R()])
    kb.memset(onesf[:], 1.0, [onesf.R()])
    kb.dma("sp", cmat_f[:], I["cmat"], (), [cmat_f.R()])
    kb.cp(cmat[:], cmat_f[:], [cmat_f.R()], [cmat.R()])
    kb.dma("pool", ropeC[:], I["ropeC"], (), [ropeC.R()], weight=True)
    kb.dma("pool", ropeS[:], I["ropeS"], (), [ropeS.R()], weight=True)
    with kb.scope():
        ctf = kb.sb([128, 8, 2], F32, "ctf")
        kb.dma("sp", ctf[:], I["cT"], (), [ctf.R()])
        kb.act(sc[:], ctf[:], AF.Silu, [ctf.R()], [sc.R()])
    ident = cmat_f[:, 0, :]

    def rstd_from_ps(pb, n, out_ap, out_res, inv_n, eps, tmp):
        kb.act(tmp[:, :n], ps[:, pb, :n], AF.Sqrt, [PR[pb]], [tmp.R()], bias=epsT[:, 0:1] if eps == EPS else epsT[:, 1:2], scale=inv_n)
        kb.recip(out_ap, tmp[:, :n], [tmp.R()], [out_res])

    epsT = kb.sb([128, 2], F32, "epsT")
    kb.memset(epsT[:, 0:1], EPS, [epsT.R()])
    kb.memset(epsT[:, 1:2], 1e-5, [epsT.R()])

    def compute_h(s, t0, t1, hout, hres, W):
        n = t1 - t0
        sq, tmp, rstd, tmp2 = W["sq"], W["tmp"], W["rstd"], W["tmp2"]
        kb.act(sq[:, :, :n], xT[:, :, t0:t1], AF.Square, xres(t0, t1), [sq.R()])
        pb = kb.bank()
        for kt in range(8):
            kb.mm(ps[:, pb, :n], ones[:, :], sq[:, kt, :n], kt == 0, kt == 7, [ones.R(), sq.R()], [PR[pb]])
        rstd_from_ps(pb, n, rstd[:, :n], rstd.R(), 1.0 / D, EPS, tmp)
        for kt in range(8):
            for lo, hi, j in split_tok(t0, t1):
                a, b = lo - t0, hi - t0
                kb.stt(tmp2[:, a:b], xT[:, kt, lo:hi], modD[:, s, j, 0, kt:kt + 1], rstd[:, a:b], ALU.mult, ALU.mult,
                       xres(lo, hi) + [modD.R(), rstd.R()], [tmp2.R()])
                kb.act(hout[:, kt, a:b], tmp2[:, a:b], AF.Identity, [tmp2.R(), modD.R()], [hres],
                       bias=modD[:, s, j, 1, kt:kt + 1])

    def hwork(n):
        return {"sq": kb.sb([128, 8, n], BF16, "sq"), "tmp": kb.sb([128, 512], F32, "tmp"),
                "rstd": kb.sb([128, 512], F32, "rstd"), "tmp2": kb.sb([128, 512], F32, "tmp2")}

    def gelu(out_ap, in_ap, n_part, shape_free, G, rin, wout):
        g1, g2 = G["g1"], G["g2"]
        sl = (slice(0, n_part),) + tuple(slice(0, f) for f in shape_free)
        kb.act(g1[sl], in_ap, AF.Square, rin, [g1.R()])
        kb.ts(g1[sl], g1[sl], 0.044715, 1.0, ALU.mult, ALU.add, [g1.R()], [g1.R()])
        kb.tt(g1[sl], g1[sl], in_ap, ALU.mult, [g1.R()] + rin, [g1.R()])
        kb.act(g2[sl], g1[sl], AF.Sigmoid, [g1.R()], [g2.R()], scale=GC)
        kb.tt(out_ap, g2[sl], in_ap, ALU.mult, [g2.R()] + rin, wout)

    def ada(l):
        kb.dma("sp", vecs[:], I["vecs"][l], (), [vecs.R()])
        kb.dma("sp", mvec[:], I["mvec"][l], (), [mvec.R()])
        with kb.scope():
            wa = [kb.sb([128, 8, 1024], BF16, "wada") for _ in range(2)]
            pb = kb.bank()
            src = I["w_ada"][l].rearrange("(kt p) n -> p kt n", p=128)
            for m in range(9):
                w = wa[m % 2]
                for hh in range(2):
                    kb.dma("pool", w[:, hh * 4:(hh + 1) * 4, :], src[:, hh * 4:(hh + 1) * 4, m * 1024:(m + 1) * 1024], (), [w.R(hh)], weight=True)
                for ot in range(8):
                    col = (m * 8 + ot) * 2
                    for kt in range(8):
                        kb.mm(ps[:, pb, col:col + 2], w[:, kt, ot * 128:(ot + 1) * 128], sc[:, kt, :], kt == 0, kt == 7,
                              [w.R(kt // 4), sc.R()], [PR[pb]])
            for j in range(2):
                kb.tt(modT[:, :, j], ps[:, pb, 0:144].rearrange("p (m j) -> p m j", j=2)[:, :, j], vecs[:, 0:72], ALU.add,
                      [PR[pb], vecs.R()], [modT.R()])
            for s in range(3):
                for j in range(2):
                    wgt = 1.0 if s == 1 else 0.5
                    kb.stt(modD[:, s, j, 0, :], modT[:, (3 * s + 1) * 8:(3 * s + 2) * 8, j], 1.0, vecs[:, 72 + s * 8:72 + (s + 1) * 8],
                           ALU.add, ALU.mult, [modT.R(), vecs.R()], [modD.R()])
                    kb.cp(modD[:, s, j, 1, :], modT[:, (3 * s) * 8:(3 * s + 1) * 8, j], [modT.R()], [modD.R()])
                    kb.stt(modD[:, s, j, 2, :], modT[:, (3 * s + 2) * 8:(3 * s + 3) * 8, j], wgt, vecs[:, 96 + s * 8:96 + (s + 1) * 8],
                           ALU.mult, ALU.mult, [modT.R(), vecs.R()], [modD.R()])

    def post_norm_residual(s, t0, n, ybuf, ssb, W, ykeys=None):
        tmp, rstd, tmp2 = W["tmp"], W["rstd"], W["tmp2"]
        rstd_from_ps(ssb, n, rstd[:, :n], rstd.R(), 1.0 / D, EPS, tmp)
        for kt in range(8):
            kb.tt(tmp2[:, :n], ybuf[:, kt, :n], rstd[:, :n], ALU.mult, [ybuf.R(kt if ykeys else 0), rstd.R()], [tmp2.R()])
            for lo, hi, j in split_tok(t0, t0 + n):
                a, b = lo - t0, hi - t0
                kb.stt(xT[:, kt, lo:hi], tmp2[:, a:b], modD[:, s, j, 2, kt:kt + 1], xT[:, kt, lo:hi], ALU.mult, ALU.add,
                       [tmp2.R(), modD.R()] + xres(lo, hi), xres(lo, hi))

    def ffn(l, jf, s):
        TB, NC_ = 576, 288
        with kb.scope():
            W = hwork(NC_)
            hT = kb.sb([128, 8, TB], BF16, "hT")
            hid = kb.sb([128, 22, TB], BF16, "hid")
            ybuf = [kb.sb([128, 8, NC_], F32, "ybuf") for _ in range(2)]
            win = [kb.sb([128, 8, 2, 256], BF16, "win") for _ in range(2)]
            wout = [kb.sb([128, 22, 256], BF16, "wout") for _ in range(2)]
            sa = [kb.sb([128, NC_], F32, "sa") for _ in range(2)]
            sqy = [kb.sb([128, NC_], BF16, "sqy") for _ in range(2)]
            src_in = I["w_ffn_in"][l, jf].rearrange("(kt p) (two f) -> p kt two f", p=128, two=2)
            src_out = I["w_ffn_out"][l, jf].rearrange("(ft p) d -> p ft d", p=128)
            wi = wo = si = 0
            for blk in range(4):
                t0 = blk * TB
                kb.banks_rot = list(range(8))
                for c in range(2):
                    compute_h(s, t0 + c * NC_, t0 + (c + 1) * NC_, hT[:, :, c * NC_:(c + 1) * NC_], hT.R(c), W)
                for fg in range(11):
                    w = win[wi % 2]
                    wi += 1
                    for gi in range(2):
                        kb.dma("pool", w[:, :, gi, :], src_in[:, :, gi, fg * 256:(fg + 1) * 256], (), [w.R(gi)], weight=True)
                    for fi in range(2):
                        f = fg * 2 + fi
                        for c in range(2):
                            ba, bg = kb.bank(), kb.bank()
                            for gi, pb in ((0, ba), (1, bg)):
                                for kt in range(8):
                                    kb.mm(ps[:, pb, :NC_], w[:, kt, gi, fi * 128:(fi + 1) * 128], hT[:, kt, c * NC_:(c + 1) * NC_],
                                          kt == 0, kt == 7, [w.R(gi), hT.R(c)], [PR[pb]])
                            sab = sa[si % 2]
                            si += 1
                            kb.act(sab[:], ps[:, ba, :NC_], AF.Silu, [PR[ba]], [sab.R()])
                            kb.tt(hid[:, f, c * NC_:(c + 1) * NC_], sab[:], ps[:, bg, :NC_], ALU.mult, [sab.R(), PR[bg]], [hid.R((f, c))])
                kb.banks_rot = list(range(6))
                ssb = (6, 7)
                for dg in range(4):
                    w = wout[wo % 2]
                    wo += 1
                    for hh in range(2):
                        kb.dma("pool", w[:, hh * 11:(hh + 1) * 11, :], src_out[:, hh * 11:(hh + 1) * 11, dg * 256:(dg + 1) * 256], (), [w.R(hh)], weight=True)
                    for di in range(2):
                        dt = dg * 2 + di
                        for c in range(2):
                            pb = kb.bank()
                            for ft in range(22):
                                kb.mm(ps[:, pb, :NC_], w[:, ft, di * 128:(di + 1) * 128], hid[:, ft, c * NC_:(c + 1) * NC_],
                                      ft == 0, ft == 21, [w.R(ft // 11), hid.R((ft, c))], [PR[pb]])
                            kb.act(ybuf[c][:, dt, :], ps[:, pb, :NC_], AF.Copy, [PR[pb]], [ybuf[c].R()])
                            sq_ = sqy[si % 2]
                            si += 1
                            kb.act(sq_[:], ps[:, pb, :NC_], AF.Square, [PR[pb]], [sq_.R()])
                            kb.mm(ps[:, ssb[c], :NC_], ones[:, :], sq_[:], dt == 0, dt == 7, [ones.R(), sq_.R()], [PR[ssb[c]]])
                for c in range(2):
                    post_norm_residual(s, t0 + c * NC_, NC_, ybuf[c], ssb[c], W)
            kb.banks_rot = list(range(8))

    def load_w(dst_tl, dst_ap, src_ap, key=0):
        kb.dma("pool", dst_ap, src_ap, (), [dst_tl.R(key)], weight=True)

    def mla(l):
        scale = 96.0 ** -0.5
        win_src = I["w_in"][l].rearrange("(kt p) n -> p kt n", p=128)
        with kb.scope():
            wuq = kb.sb([128, 2, 512], BF16, "wuq")
            wukv = kb.sb([128, 512], BF16, "wukv")
            kvn = kb.sb([128, T], BF16, "kvn")
            qn = kb.sb([128, 2, T], BF16, "qn")
            kpeR = kb.sb([128, T], BF16, "kpeR")
            sqb = kb.sb([128, 3, 512], BF16, "sqb")
            t1 = kb.sb([128, 512], F32, "t1")
            t2 = kb.sb([128, 512], F32, "t2")
            rs = kb.sb([128, 512], F32, "rs")
            kmax = kb.sb([128, 4], F32, "kmax")
            load_w(wuq, wuq[:], I["w_uq"][l].rearrange("(kt p) n -> p kt n", p=128))
            load_w(wukv, wukv[:], I["w_ukv"][l])
            mla_p1(l, win_src, kvn, qn, kpeR, sqb, t1, t2, rs)
            mla_p2(l, scale, wuq, wukv, kvn, qn, kpeR, sqb, t1, t2, kmax)

    def mla_p1(l, win_src, kvn, qn, kpeR, sqb, t1, t2, rs):
        with kb.scope():
            W = hwork(512)
            hb = kb.sb([128, 8, 512], BF16, "hb")
            wst = kb.sb([128, 8, 448], BF16, "wmla")
            raw = kb.sb([128, 3, 512], F32, "raw")
            for hh in range(2):
                load_w(wst, wst[:, hh * 4:(hh + 1) * 4, :], win_src[:, hh * 4:(hh + 1) * 4, 0:448], hh)
            WST = [wst.R(0), wst.R(1)]
            for bi, (t0, t1_) in enumerate(TBLK):
                n = t1_ - t0
                compute_h(1, t0, t1_, hb, hb.R(), W)
                pbs = [kb.bank() for _ in range(3)]
                for oi, (c0, pb) in enumerate(zip((E_KVL, E_QL, E_QL + 128), pbs)):
                    for kt in range(8):
                        kb.mm(ps[:, pb, :n], wst[:, kt, c0:c0 + 128], hb[:, kt, :n], kt == 0, kt == 7, WST + [hb.R()], [PR[pb]])
                    kb.act(raw[:, oi, :n], ps[:, pb, :n], AF.Copy, [PR[pb]], [raw.R(oi)])
                    kb.act(sqb[:, oi, :n], ps[:, pb, :n], AF.Square, [PR[pb]], [sqb.R(oi)])
                pb = kb.bank()
                kb.mm(ps[:, pb, :n], ones[:, :], sqb[:, 0, :n], True, True, [ones.R(), sqb.R(0)], [PR[pb]])
                rstd_from_ps(pb, n, rs[:, :n], rs.R(), 1.0 / 128, EPS, t1)
                kb.stt(kvn[:, t0:t1_], raw[:, 0, :n], mvec[:, 0:1], rs[:, :n], ALU.mult, ALU.mult, [raw.R(0), mvec.R(), rs.R()], [kvn.R(bi)])
                pb = kb.bank()
                for oi in (1, 2):
                    kb.mm(ps[:, pb, :n], ones[:, :], sqb[:, oi, :n], oi == 1, oi == 2, [ones.R(), sqb.R(oi)], [PR[pb]])
                rstd_from_ps(pb, n, rs[:, :n], rs.R(), 1.0 / 256, EPS, t1)
                for oi in (1, 2):
                    kb.stt(qn[:, oi - 1, t0:t1_], raw[:, oi, :n], mvec[:, oi:oi + 1], rs[:, :n], ALU.mult, ALU.mult,
                           [raw.R(oi), mvec.R(), rs.R()], [qn.R(bi)])
                pa, pbb = kb.bank(), kb.bank()
                for c0, pb in ((E_KPA, pa), (E_KPB, pbb)):
                    for kt in range(8):
                        kb.mm(ps[64:96, pb, :n], wst[:, kt, c0:c0 + 32], hb[:, kt, :n], kt == 0, kt == 7, WST + [hb.R()], [PR[pb]])
                kb.tt(t1[64:96, :n], ps[64:96, pa, :n], ropeC[64:96, t0:t1_], ALU.mult, [PR[pa], ropeC.R()], [t1.R()])
                kb.tt(t2[64:96, :n], ps[64:96, pbb, :n], ropeS[64:96, t0:t1_], ALU.mult, [PR[pbb], ropeS.R()], [t2.R()])
                kb.tt(kpeR[64:96, t0:t1_], t1[64:96, :n], t2[64:96, :n], ALU.add, [t1.R(), t2.R()], [kpeR.R(bi)])

    def mla_p2(l, scale, wuq, wukv, kvn, qn, kpeR, sqb, t1, t2, kmax):
        with kb.scope():
            KTt = [kb.sb([128, T], BF16, "KT") for _ in range(2)]
            QTt = [kb.sb([128, T], BF16, "QT") for _ in range(2)]
            V = kb.sb([128, 18, 256], BF16, "V")
            pT = [kb.sb([128, 512], BF16, "pT") for _ in range(2)]
            rd = kb.sb([128, 512], F32, "rd")
            for kbk in range(18):
                pb = kb.bank()
                kb.mm(ps[:, pb, :256], kvn[:, kbk * 128:(kbk + 1) * 128], wukv[:, 256:512], True, True,
                      [kvn.R(b) for b in range(5)] + [wukv.R()], [PR[pb]])
                kb.act(V[:, kbk, :], ps[:, pb, :256], AF.Copy, [PR[pb]], [V.R()])
            KVN = [kvn.R(b) for b in range(5)]
            QN = [qn.R(b) for b in range(5)]
            for pair in range(2):
                for hi_ in range(2):
                    kb.memset(KTt[hi_][96:97, :], 1.0, [KTt[hi_].R()])
                    kb.memset(kmax[:, 2 * pair + hi_:2 * pair + hi_ + 1], 0.0, [kmax.R()])
                for bi, (t0, t1_) in enumerate(TBLK):
                    n = t1_ - t0
                    pb = kb.bank()
                    kb.mm(ps[:, pb, :n], wukv[:, pair * 128:(pair + 1) * 128], kvn[:, t0:t1_], True, True, [wukv.R()] + KVN, [PR[pb]])
                    for hi_ in range(2):
                        KTh = KTt[hi_]
                        kb.act(KTh[0:64, t0:t1_], ps[hi_ * 64:(hi_ + 1) * 64, pb, :n], AF.Copy, [PR[pb]], [KTh.R()])
                        kb.cp(KTh[64:96, t0:t1_], kpeR[64:96, t0:t1_], [kpeR.R(bi)], [KTh.R()])
                        kb.act(sqb[0:96, 0, :n], KTh[0:96, t0:t1_], AF.Square, [KTh.R()], [sqb.R(0)])
                        p2 = kb.bank()
                        kb.mm(ps[:, p2, :n], ones[0:96, :], sqb[0:96, 0, :n], True, True, [ones.R(), sqb.R(0)], [PR[p2]])
                        kb.emit("dve", lambda e: e.tensor_reduce(out=t1[:, 0:1], in_=ps[:, p2, :n], axis=AX.X, op=ALU.max), [PR[p2]], [t1.R()])
                        hcol = 2 * pair + hi_
                        kb.tt(kmax[:, hcol:hcol + 1], kmax[:, hcol:hcol + 1], t1[:, 0:1], ALU.max, [kmax.R(), t1.R()], [kmax.R()])
                for hi_ in range(2):
                    hcol = 2 * pair + hi_
                    kb.act(kmax[:, hcol:hcol + 1], kmax[:, hcol:hcol + 1], AF.Sqrt, [kmax.R()], [kmax.R()])
                    kb.ts(kmax[:, hcol:hcol + 1], kmax[:, hcol:hcol + 1], -1.0, None, ALU.mult, None, [kmax.R()], [kmax.R()])
                for bi, (t0, t1_) in enumerate(TBLK):
                    n = t1_ - t0
                    for hi_ in range(2):
                        h = 2 * pair + hi_
                        QTh = QTt[hi_]
                        pb = kb.bank()
                        for k2 in range(2):
                            kb.mm(ps[:, pb, :n], wuq[:, k2, h * 128:(h + 1) * 128], qn[:, k2, t0:t1_], k2 == 0, k2 == 1, [wuq.R()] + QN, [PR[pb]])
                        kb.act(QTh[0:64, t0:t1_], ps[0:64, pb, :n], AF.Copy, [PR[pb]], [QTh.R()])
                        kb.tt(t1[64:96, :n], ps[64:96, pb, :n], ropeC[64:96, t0:t1_], ALU.mult, [PR[pb], ropeC.R()], [t1.R()])
                        kb.tt(t2[64:96, :n], ps[96:128, pb, :n], ropeS[96:128, t0:t1_], ALU.mult, [PR[pb], ropeS.R()], [t2.R()])
                        kb.tt(QTh[64:96, t0:t1_], t1[64:96, :n], t2[64:96, :n], ALU.add, [t1.R(), t2.R()], [QTh.R()])
                        kb.act(sqb[0:96, 0, :n], QTh[0:96, t0:t1_], AF.Square, [QTh.R()], [sqb.R(0)])
                        p2 = kb.bank()
                        kb.mm(ps[:, p2, :n], ones[0:96, :], sqb[0:96, 0, :n], True, True, [ones.R(), sqb.R(0)], [PR[p2]])
                        kb.act(t1[96:97, :n], ps[96:97, p2, :n], AF.Sqrt, [PR[p2]], [t1.R()])
                        kb.ts(QTh[96:97, t0:t1_], t1[96:97, :n], kmax[96:97, h:h + 1], None, ALU.mult, None, [t1.R(), kmax.R()], [QTh.R()])
                pti = 0
                for hi_ in range(2):
                    h = 2 * pair + hi_
                    KTh, QTh = KTt[hi_], QTt[hi_]
                    for gi, (q0, q1) in enumerate(TBLK):
                        nq = q1 - q0
                        kbs = list(range(2)) if gi == 0 else list(range(18))
                        po = kb.bank()
                        for ki, kbk in enumerate(kbs):
                            pS = kb.bank()
                            while pS == po:
                                pS = kb.bank()
                            kb.mm(ps[:, pS, :nq], KTh[0:97, kbk * 128:(kbk + 1) * 128], QTh[0:97, q0:q1], True, True, [KTh.R(), QTh.R()], [PR[pS]])
                            p_ = pT[pti % 2]
                            pti += 1
                            kb.act(p_[:, :nq], ps[:, pS, :nq], AF.Exp, [PR[pS]], [p_.R()], scale=scale)
                            last = ki == len(kbs) - 1
                            kb.mm(ps[0:64, po, :nq], V[:, kbk, h * 64:(h + 1) * 64], p_[:, :nq], ki == 0, last, [V.R(), p_.R()], [PR[po]], sig=last)
                            kb.mm(ps[64:128, po, :nq], ones[:, 0:64], p_[:, :nq], ki == 0, last, [ones.R(), p_.R()], [PR[po]], sig=True)
                        kb.recip(rd[0:64, :nq], ps[64:128, po, :nq], [PR[po]], [rd.R()])
                        kb.tt(BR['t'][hi_ * 64:(hi_ + 1) * 64, 0, pair, q0:q1], ps[0:64, po, :nq], rd[0:64, :nq], ALU.mult, [PR[po], rd.R()], [BR['t'].R((0, pair, gi))])

    def gqa(l):
        scale = 0.125
        win_src = I["w_in"][l].rearrange("(kt p) n -> p kt n", p=128)
        with kb.scope():
            W = hwork(512)
            hb = kb.sb([128, 8, 512], BF16, "hb")
            wst = kb.sb([128, 8, 896], BF16, "wgqa")
            QG = [kb.sb([128, T], BF16, "QG") for _ in range(4)]
            KG = [kb.sb([128, T], BF16, "KG") for _ in range(2)]
            VG = kb.sb([128, 18, 128], BF16, "VG")
            sqb = kb.sb([128, 512], BF16, "sqb")
            t1 = kb.sb([128, 512], F32, "t1")
            t2 = kb.sb([128, 512], F32, "t2")
            kmax = kb.sb([128, 2], F32, "kmax")
            sst = kb.sb([128, 4], F32, "sst")
            ksink = kb.sb([128, 4], BF16, "ksink")
            pT = [kb.sb([128, 5, 128], BF16, "pT") for _ in range(2)]
            psk = [kb.sb([128, 128], BF16, "psk") for _ in range(2)]
            rd = kb.sb([128, 128], F32, "rd")
            for hh in range(2):
                load_w(wst, wst[:, hh * 4:(hh + 1) * 4, :], win_src[:, hh * 4:(hh + 1) * 4, E_GQA:E_GQA + 896], hh)
            WST = [wst.R(0), wst.R(1)]
            kb.memset(sst[64:66, :], scale, [sst.R()])
            kb.dma("sp", sst[65:66, :], I["mvec"][l, 65:66, 8:12], (), [sst.R()])
            kb.memset(ksink[0:66, :], 0.0, [ksink.R()])
            kb.ts(ksink[64:66, :], sst[64:66, :], 1.0 / scale, None, ALU.mult, None, [sst.R()], [ksink.R()])
            for h in range(4):
                kb.memset(QG[h][64:66, :], 1.0, [QG[h].R()])
            for kv in range(2):
                kb.memset(KG[kv][64:66, :], 0.0, [KG[kv].R()])
                kb.memset(KG[kv][64:65, :], 1.0, [KG[kv].R()])
            kb.memset(kmax[:], 0.0, [kmax.R()])

            def rope_proj(cA, cB, dst, t0, t1_, n):
                pa, pbb = kb.bank(), kb.bank()
                for c0, pb in ((cA, pa), (cB, pbb)):
                    for kt in range(8):
                        kb.mm(ps[0:64, pb, :n], wst[:, kt, c0:c0 + 64], hb[:, kt, :n], kt == 0, kt == 7, WST + [hb.R()], [PR[pb]])
                kb.tt(t1[0:64, :n], ps[0:64, pa, :n], ropeC[0:64, t0:t1_], ALU.mult, [PR[pa], ropeC.R()], [t1.R()])
                kb.tt(t2[0:64, :n], ps[0:64, pbb, :n], ropeS[0:64, t0:t1_], ALU.mult, [PR[pbb], ropeS.R()], [t2.R()])
                kb.tt(dst[0:64, t0:t1_], t1[0:64, :n], t2[0:64, :n], ALU.add, [t1.R(), t2.R()], [dst.R()])

            def sumsq64(src, t0, t1_, n):
                kb.act(sqb[0:64, :n], src[0:64, t0:t1_], AF.Square, [src.R()], [sqb.R()])
                p2 = kb.bank()
                kb.mm(ps[:, p2, :n], ones[0:64, :], sqb[0:64, :n], True, True, [ones.R(), sqb.R()], [PR[p2]])
                return p2

            for bi, (t0, t1_) in enumerate(TBLK):
                n = t1_ - t0
                compute_h(1, t0, t1_, hb, hb.R(), W)
                for kv in range(2):
                    rope_proj(E_GKA - E_GQA + kv * 64, E_GKB - E_GQA + kv * 64, KG[kv], t0, t1_, n)
                    p2 = sumsq64(KG[kv], t0, t1_, n)
                    kb.emit("dve", lambda e: e.tensor_reduce(out=t1[:, 0:1], in_=ps[:, p2, :n], axis=AX.X, op=ALU.max), [PR[p2]], [t1.R()])
                    kb.tt(kmax[:, kv:kv + 1], kmax[:, kv:kv + 1], t1[:, 0:1], ALU.max, [kmax.R(), t1.R()], [kmax.R()])
                for cb in range(n // 128):
                    kbk = t0 // 128 + cb
                    pb = kb.bank()
                    for kt in range(8):
                        kb.mm(ps[:, pb, :128], hb[:, kt, cb * 128:(cb + 1) * 128], wst[:, kt, E_GV - E_GQA:E_GV - E_GQA + 128], kt == 0, kt == 7,
                              WST + [hb.R()], [PR[pb]])
                    kb.act(VG[:, kbk, :], ps[:, pb, :128], AF.Copy, [PR[pb]], [VG.R()])
            kb.act(kmax[:], kmax[:], AF.Sqrt, [kmax.R()], [kmax.R()])
            kb.ts(kmax[:], kmax[:], -1.0, None, ALU.mult, None, [kmax.R()], [kmax.R()])
            for bi, (t0, t1_) in enumerate(TBLK):
                n = t1_ - t0
                compute_h(1, t0, t1_, hb, hb.R(), W)
                for h in range(4):
                    rope_proj(h * 64, E_GQB - E_GQA + h * 64, QG[h], t0, t1_, n)
                    p2 = sumsq64(QG[h], t0, t1_, n)
                    kb.act(t1[64:65, :n], ps[64:65, p2, :n], AF.Sqrt, [PR[p2]], [t1.R()])
                    kb.ts(QG[h][64:65, t0:t1_], t1[64:65, :n], kmax[64:65, h // 2:h // 2 + 1], None, ALU.mult, None, [t1.R(), kmax.R()], [QG[h].R()])
            it = 0
            for h in range(4):
                kv = h // 2
                for qb in range(18):
                    q0 = qb * 128
                    if qb < 2:
                        band = []
                    else:
                        nb = qb - 2
                        band = [(2 + nb + d, d) for d in (-1, 0, 1) if 0 <= nb + d < 16]
                    pband, pctx, po = kb.bank(), kb.bank(), kb.bank()
                    p_ = pT[it % 2]
                    pk = psk[it % 2]
                    it += 1
                    for i, (kbk, d) in enumerate(band):
                        kb.mm(ps[:, pband, i * 128:(i + 1) * 128], KG[kv][0:66, kbk * 128:(kbk + 1) * 128], QG[h][0:66, q0:q0 + 128], True, d == 0,
                              [KG[kv].R(), QG[h].R()], [PR[pband]], sig=(d == 0))
                        if d != 0:
                            mi = 1 if d < 0 else 2
                            kb.mm(ps[:, pband, i * 128:(i + 1) * 128], cmat[:, mi, :], cmat[:, 0, :], False, True, [cmat.R()], [PR[pband]], sig=True)
                    for i in range(2):
                        kb.mm(ps[:, pctx, i * 128:(i + 1) * 128], KG[kv][0:66, i * 128:(i + 1) * 128], QG[h][0:66, q0:q0 + 128], True, True,
                              [KG[kv].R(), QG[h].R()], [PR[pctx]])
                    kb.mm(ps[0:1, pctx, 256:384], ksink[0:66, h:h + 1], QG[h][0:66, q0:q0 + 128], True, True, [ksink.R(), QG[h].R()], [PR[pctx]])
                    nb_ = len(band)
                    if nb_:
                        kb.act(p_[:, 0:nb_, :], ps[:, pband, 0:nb_ * 128].rearrange("p (a b) -> p a b", b=128), AF.Exp, [PR[pband]], [p_.R()], scale=scale)
                    kb.act(p_[:, 3:5, :], ps[:, pctx, 0:256].rearrange("p (a b) -> p a b", b=128), AF.Exp, [PR[pctx]], [p_.R()], scale=scale)
                    kb.act(pk[0:1, :], ps[0:1, pctx, 256:384], AF.Exp, [PR[pctx]], [pk.R()], scale=scale)
                    items = [(kbk, i) for i, (kbk, d) in enumerate(band)] + [(0, 3), (1, 4)]
                    for ii, (kbk, pi) in enumerate(items):
                        kb.mm(ps[0:64, po, :128], VG[:, kbk, kv * 64:(kv + 1) * 64], p_[:, pi, :], ii == 0, ii == len(items) - 1, [VG.R(), p_.R()], [PR[po]], sig=False)
                        kb.mm(ps[64:128, po, :128], ones[:, 0:64], p_[:, pi, :], ii == 0, False, [ones.R(), p_.R()], [PR[po]], sig=False)
                    kb.mm(ps[64:128, po, :128], ones[0:1, 0:64], pk[0:1, :], False, True, [ones.R(), pk.R()], [PR[po]], sig=True)
                    kb.recip(rd[0:64, :], ps[64:128, po, :128], [PR[po]], [rd.R()])
                    kb.tt(BR['t'][(h % 2) * 64:(h % 2 + 1) * 64, 1, h // 2, q0:q0 + 128], ps[0:64, po, :128], rd[0:64, :], ALU.mult, [PR[po], rd.R()],
                          [BR['t'].R((1, h // 2, qb))])

    def gmlp(l):
        win_src = I["w_in"][l].rearrange("(kt p) n -> p kt n", p=128)
        with kb.scope():
            W = hwork(512)
            hb = kb.sb([128, 8, 512], BF16, "hb")
            wst = kb.sb([128, 8, 512], BF16, "wz")
            wsT = kb.sb([128, 4, 128], BF16, "wsT")
            bsT = kb.sb([128, 2, 128], F32, "bsT")
            zu = kb.sb([128, 2, 512], BF16, "zu")
            G = {"g1": kb.sb([128, 512], F32, "g1"), "g2": kb.sb([128, 512], F32, "g2")}
            vg = kb.sb([128, 256], F32, "vg")
            xn = kb.sb([128, 256], BF16, "xn")
            st6 = kb.sb([128, 6], F32, "st6")
            mv = kb.sb([128, 4], F32, "mv")
            mx = kb.sb([128, 128], F32, "mx")
            for hh in range(2):
                load_w(wst, wst[:, hh * 4:(hh + 1) * 4, :], win_src[:, hh * 4:(hh + 1) * 4, E_Z:E_Z + 512], hh)
            load_w(wsT, wsT[:], I["wsT"][l])
            kb.dma("sp", bsT[:], I["bsT"][l], (), [bsT.R()])
            WST = [wst.R(0), wst.R(1)]
            for bi, (t0, t1_) in enumerate(TBLK):
                n = t1_ - t0
                compute_h(1, t0, t1_, hb, hb.R(), W)
                for ot in range(2):
                    pb = kb.bank()
                    for kt in range(8):
                        kb.mm(ps[:, pb, :n], wst[:, kt, ot * 128:(ot + 1) * 128], hb[:, kt, :n], kt == 0, kt == 7, WST + [hb.R()], [PR[pb]])
                    gelu(zu[:, ot, :n], ps[:, pb, :n], 128, (n,), G, [PR[pb]], [zu.R(ot)])
                for cb in range(n // 128):
                    c0 = t0 + cb * 128
                    pb = kb.bank()
                    for kt in range(8):
                        kb.mm(ps[:, pb, :256], hb[:, kt, cb * 128:(cb + 1) * 128], wst[:, kt, 256:512], kt == 0, kt == 7, WST + [hb.R()], [PR[pb]])
                    gelu(vg[:, :], ps[:, pb, :256], 128, (256,), G, [PR[pb]], [vg.R()])
                    kb.emit("dve", lambda e: e.bn_stats(out=st6[:, :], in_=vg[:, :]), [vg.R()], [st6.R()])
                    kb.emit("dve", lambda e: e.bn_aggr(out=mv[:, 0:2], in_=st6[:, :]), [st6.R()], [mv.R()])
                    kb.act(mv[:, 2:3], mv[:, 1:2], AF.Sqrt, [mv.R()], [mv.R()], bias=epsT[:, 1:2])
                    kb.recip(mv[:, 3:4], mv[:, 2:3], [mv.R()], [mv.R()])
                    kb.ts(xn[:, :], vg[:, :], mv[:, 0:1], mv[:, 3:4], ALU.subtract, ALU.mult, [vg.R(), mv.R()], [xn.R()])
                    for gp in range(2):
                        pm = kb.bank()
                        for gg in range(2):
                            g = gp * 2 + gg
                            kb.mm(ps[gg * 64:(gg + 1) * 64, pm, :128], xn[:, g * 64:(g + 1) * 64], wsT[:, g, :], True, True, [xn.R(), wsT.R()], [PR[pm]])
                        kb.stt(mx[:, :], ps[:, pm, :128], mvec[:, 3 + gp:4 + gp], bsT[:, gp, :], ALU.mult, ALU.add, [PR[pm], mvec.R(), bsT.R()], [mx.R()])
                        kb.tt(BR['t'][:, 3, gp, c0:c0 + 128], mx[:, :], zu[:, gp, cb * 128:(cb + 1) * 128], ALU.mult, [mx.R(), zu.R(gp)], [BR['t'].R((3, gp, c0 // 128))])

    def s5(l):
        win_src = I["w_in"][l].rearrange("(kt p) n -> p kt n", p=128)
        with kb.scope():
            wglu = kb.sb([128, 2, 512], BF16, "wglu")
            uT = kb.sb([128, 2, T], BF16, "uT")
            uR = kb.sb([128, 2, T], BF16, "uR")
            yacc = kb.sb([128, 2, T], BF16, "yacc")
            load_w(wglu, wglu[:], I["w_glu"][l].rearrange("(kt p) n -> p kt n", p=128))
            with kb.scope():
                W = hwork(512)
                hb = kb.sb([128, 8, 512], BF16, "hb")
                wst = kb.sb([128, 8, 256], BF16, "wu")
                for hh in range(2):
                    load_w(wst, wst[:, hh * 4:(hh + 1) * 4, :], win_src[:, hh * 4:(hh + 1) * 4, E_U:E_U + 256], hh)
                WST = [wst.R(0), wst.R(1)]
                for bi, (t0, t1_) in enumerate(TBLK):
                    n = t1_ - t0
                    compute_h(1, t0, t1_, hb, hb.R(), W)
                    for ot in range(2):
                        pb = kb.bank()
                        for kt in range(8):
                            kb.mm(ps[:, pb, :n], wst[:, kt, ot * 128:(ot + 1) * 128], hb[:, kt, :n], kt == 0, kt == 7, WST + [hb.R()], [PR[pb]])
                        kb.act(uT[:, ot, t0:t1_], ps[:, pb, :n], AF.Copy, [PR[pb]], [uT.R()])
            kb.cp(uR[:, :, 0:NCTX], uT[:, :, 0:NCTX][:, :, ::-1], [uT.R()], [uR.R()])
            kb.cp(uR[:, :, NCTX:T], uT[:, :, NCTX:T][:, :, ::-1], [uT.R()], [uR.R()])
            with kb.scope():
                pp = kb.sb([128, 8, 4], F32, "pp")
                tabs = kb.sb([128, 4, 8, 128], F32, "tabs")
                sc1 = kb.sb([128, 12, 8], F32, "sc1")
                bT = kb.sb([128, 2, 8, 128], BF16, "bT")
                cP = kb.sb([128, 2, 8, 128], BF16, "cP")
                kc = kb.sb([128, 8, 2], F32, "kc")
                kt_ = kb.sb([128, 4], F32, "kt_")
                PI = float(np.pi)

                def sin_of(out_ap, in_ap, shift, r, w, s_a, s_b):
                    kb.ts(s_a, in_ap, shift + PI, 1.0 / (2 * PI), ALU.add, ALU.mult, r, [sc1.R()])
                    kb.ts(s_b, s_a, -0.5, 12582912.0, ALU.add, ALU.add, [sc1.R()], [sc1.R()])
                    kb.ts(s_b, s_b, -12582912.0, None, ALU.add, None, [sc1.R()], [sc1.R()])
                    kb.tt(s_a, s_a, s_b, ALU.subtract, [sc1.R()], [sc1.R()])
                    kb.ts(s_a, s_a, 2 * PI, -PI, ALU.mult, ALU.add, [sc1.R()], [sc1.R()])
                    kb.ts(s_a, s_a, PI, -PI, ALU.min, ALU.max, [sc1.R()], [sc1.R()])
                    kb.act(out_ap, s_a, AF.Sin, [sc1.R()], w)

                for d_ in range(2):
                  usrc = uT if d_ == 0 else uR
                  with kb.scope():
                    bst = kb.sb([128, 2, 8, 128], F32, "bst")
                    bb = kb.sb([128, 2, 8, 128], F32, "bb")
                    tA = kb.sb([128, 8, 128], F32, "tA")
                    tB = kb.sb([128, 8, 128], F32, "tB")
                    kb.dma("sp", pp[:], I["s5p"][l, d_], (), [pp.R()])
                    kb.dma("sp", bst[:], I["s5b"][l, d_].rearrange("r n k c -> n r k c"), (), [bst.R()])
                    S = lambda i: sc1[:, i, :]
                    R1 = [sc1.R()]
                    lr, li, ldt = pp[:, :, 0], pp[:, :, 1], pp[:, :, 2]
                    kb.ts(S(0), lr, -1e-4, None, ALU.min, None, [pp.R()], R1)
                    kb.act(S(1), ldt, AF.Exp, [pp.R()], R1)
                    kb.tt(S(2), S(0), S(1), ALU.mult, R1, R1)
                    kb.act(S(2), S(2), AF.Exp, R1, R1)
                    kb.tt(S(3), li, S(1), ALU.mult, [pp.R()] + R1, R1)
                    sin_of(S(4), S(3), 0.0, R1, R1, S(10), S(11))
                    sin_of(S(5), S(3), PI / 2, R1, R1, S(10), S(11))
                    kb.tt(S(4), S(4), S(2), ALU.mult, R1, R1)
                    kb.tt(S(5), S(5), S(2), ALU.mult, R1, R1)
                    kb.ts(S(6), S(5), -1.0, None, ALU.add, None, R1, R1)
                    kb.tt(S(7), S(0), S(0), ALU.mult, R1, R1)
                    kb.tt(S(8), li, li, ALU.mult, [pp.R()], R1)
                    kb.tt(S(7), S(7), S(8), ALU.add, R1, R1)
                    kb.recip(S(7), S(7), R1, R1)
                    kb.tt(S(8), S(6), S(0), ALU.mult, R1, R1)
                    kb.tt(S(9), S(4), li, ALU.mult, R1 + [pp.R()], R1)
                    kb.tt(S(8), S(8), S(9), ALU.add, R1, R1)
                    kb.tt(S(8), S(8), S(7), ALU.mult, R1, R1)
                    kb.tt(S(9), S(4), S(0), ALU.mult, R1, R1)
                    kb.tt(S(6), S(6), li, ALU.mult, R1 + [pp.R()], R1)
                    kb.tt(S(9), S(9), S(6), ALU.subtract, R1, R1)
                    kb.tt(S(9), S(9), S(7), ALU.mult, R1, R1)
                    bc = lambda i: sc1[:, i, :].unsqueeze(2).broadcast_to([128, 8, 128])
                    kb.tt(tA[:], bst[:, 0], bc(8), ALU.mult, [bst.R()] + R1, [tA.R()])
                    kb.tt(tB[:], bst[:, 1], bc(9), ALU.mult, [bst.R()] + R1, [tB.R()])
                    kb.tt(bb[:, 0], tA[:], tB[:], ALU.subtract, [tA.R(), tB.R()], [bb.R()])
                    kb.tt(tA[:], bst[:, 1], bc(8), ALU.mult, [bst.R()] + R1, [tA.R()])
                    kb.tt(tB[:], bst[:, 0], bc(9), ALU.mult, [bst.R()] + R1, [tB.R()])
                    kb.tt(bb[:, 1], tA[:], tB[:], ALU.add, [tA.R(), tB.R()], [bb.R()])
                    for ri in range(2):
                        for k in range(8):
                            pb = kb.bank()
                            kb.emit("pe", lambda e: e.transpose(ps[:, pb, 0:128], bb[:, ri, k, :], ident), [bb.R(), cmat_f.R()], [PR[pb]])
                            kb.act(bT[:, ri, k, :], ps[:, pb, 0:128], AF.Copy, [PR[pb]], [bT.R()])
                    kb.dma("pool", cP[:], I["s5c"][l, d_].rearrange("r n k c -> n r k c"), (), [cP.R()], weight=True)
                    kb.tt(S(6), S(2), S(2), ALU.mult, R1, R1)
                    kb.recip(S(6), S(6), R1, R1)
                    kb.tt(S(7), S(5), S(6), ALU.mult, R1, R1)
                    kb.tt(S(6), S(4), S(6), ALU.mult, R1, R1)
                    kb.ts(S(6), S(6), -1.0, None, ALU.mult, None, R1, R1)
                    TR = [tabs.R()]
                    for (tr, ti, pr0, pi0) in ((0, 1, 5, 4), (2, 3, 7, 6)):
                        kb.memset(tabs[:, tr, :, 0:1], 1.0, TR)
                        kb.memset(tabs[:, ti, :, 0:1], 0.0, TR)
                        kb.cp(S(10), S(pr0), R1, R1)
                        kb.cp(S(11), S(pi0), R1, R1)
                        m = 1
                        while m < 256:
                            mm_ = min(m, 128) if m < 128 else 1
                            if m < 128:
                                pr = sc1[:, 10, :].unsqueeze(2).broadcast_to([128, 8, m])
                                pi_ = sc1[:, 11, :].unsqueeze(2).broadcast_to([128, 8, m])
                                src_r, src_i = tabs[:, tr, :, 0:m], tabs[:, ti, :, 0:m]
                                kb.tt(tA[:, :, 0:m], src_r, pr, ALU.mult, TR + R1, [tA.R()])
                                kb.tt(tB[:, :, 0:m], src_i, pi_, ALU.mult, TR + R1, [tB.R()])
                                kb.tt(tabs[:, tr, :, m:2 * m], tA[:, :, 0:m], tB[:, :, 0:m], ALU.subtract, [tA.R(), tB.R()], TR)
                                kb.tt(tA[:, :, 0:m], src_r, pi_, ALU.mult, TR + R1, [tA.R()])
                                kb.tt(tB[:, :, 0:m], src_i, pr, ALU.mult, TR + R1, [tB.R()])
                                kb.tt(tabs[:, ti, :, m:2 * m], tA[:, :, 0:m], tB[:, :, 0:m], ALU.add, [tA.R(), tB.R()], TR)
                            if m < 128:
                                kb.tt(tA[:, :, 0], S(10), S(10), ALU.mult, R1, [tA.R()])
                                kb.tt(tB[:, :, 0], S(11), S(11), ALU.mult, R1, [tB.R()])
                                kb.tt(S(11), S(10), S(11), ALU.mult, R1, R1)
                                kb.ts(S(11), S(11), 2.0, None, ALU.mult, None, R1, R1)
                                kb.tt(S(10), tA[:, :, 0], tB[:, :, 0], ALU.subtract, [tA.R(), tB.R()], R1)
                            m *= 2
                        if tr == 0:
                            kb.cp(sc1[:, 0, :], S(10), R1, R1)
                            kb.cp(sc1[:, 1, :], S(11), R1, R1)
                  with kb.scope():
                    Wt = kb.sb([128, 2, 512], F32, "Wt")
                    Zt = kb.sb([128, 2, 512], F32, "Zt")
                    Ss = [kb.sb([128, 2, 512], BF16, "Ss") for _ in range(2)]
                    w4 = [kb.sb([128, 512], F32, "w4") for _ in range(4)]
                    S = lambda i: sc1[:, i, :]
                    R1 = [sc1.R()]
                    TR = [tabs.R()]
                    kb.memset(kc[:], 0.0, [kc.R()])
                    si_ = 0
                    for bi, (t0, t1_) in enumerate(TBLK):
                        n = t1_ - t0
                        nch = n // 128
                        pY = [kb.bank(), kb.bank()]
                        for k in range(8):
                            o = k // 4
                            pr_, pi_ = kb.bank(), kb.bank()
                            while pr_ in pY:
                                pr_ = kb.bank()
                            while pi_ in pY or pi_ == pr_:
                                pi_ = kb.bank()
                            kb.mm(ps[:, pr_, :n], bT[:, 0, k, :], usrc[:, o, t0:t1_], True, True, [bT.R(), usrc.R()], [PR[pr_]])
                            kb.mm(ps[:, pi_, :n], bT[:, 1, k, :], usrc[:, o, t0:t1_], True, True, [bT.R(), usrc.R()], [PR[pi_]])
                            tb = lambda i: tabs[:, i, k:k + 1, :].broadcast_to([128, nch, 128])
                            v3 = lambda ap: ap.rearrange("p (c j) -> p c j", j=128)
                            P_r, P_i = v3(ps[:, pr_, :n]), v3(ps[:, pi_, :n])
                            a0, a1, a2, a3 = [v3(w4[i][:, :n]) for i in range(4)]
                            kb.tt(a0, P_r, tb(2), ALU.mult, [PR[pr_]] + TR, [w4[0].R()])
                            kb.tt(a1, P_i, tb(3), ALU.mult, [PR[pi_]] + TR, [w4[1].R()])
                            kb.tt(a2, P_i, tb(2), ALU.mult, [PR[pi_]] + TR, [w4[2].R()])
                            kb.tt(a3, P_r, tb(3), ALU.mult, [PR[pr_]] + TR, [w4[3].R()])
                            kb.tt(Wt[:, 0, :n], w4[0][:, :n], w4[1][:, :n], ALU.subtract, [w4[0].R(), w4[1].R()], [Wt.R()])
                            kb.tt(Wt[:, 1, :n], w4[2][:, :n], w4[3][:, :n], ALU.add, [w4[2].R(), w4[3].R()], [Wt.R()])
                            for c in range(nch):
                                cs = slice(c * 128, (c + 1) * 128)
                                for ri in range(2):
                                    kb.emit("dve", lambda e: e.tensor_tensor_scan(out=Zt[:, ri, cs], data0=onesf[:, 0:128], data1=Wt[:, ri, cs],
                                                                                  initial=kc[:, k, ri:ri + 1], op0=ALU.mult, op1=ALU.add),
                                            [onesf.R(), Wt.R(), kc.R()], [Zt.R()])
                                e0 = c * 128 + 127
                                kb.ts(kt_[:, 0:1], Zt[:, 0, e0:e0 + 1], sc1[:, 0, k:k + 1], None, ALU.mult, None, [Zt.R()] + R1, [kt_.R()])
                                kb.ts(kt_[:, 1:2], Zt[:, 1, e0:e0 + 1], sc1[:, 1, k:k + 1], None, ALU.mult, None, [Zt.R()] + R1, [kt_.R()])
                                kb.ts(kt_[:, 2:3], Zt[:, 1, e0:e0 + 1], sc1[:, 0, k:k + 1], None, ALU.mult, None, [Zt.R()] + R1, [kt_.R()])
                                kb.ts(kt_[:, 3:4], Zt[:, 0, e0:e0 + 1], sc1[:, 1, k:k + 1], None, ALU.mult, None, [Zt.R()] + R1, [kt_.R()])
                                kb.tt(kc[:, k, 0:1], kt_[:, 0:1], kt_[:, 1:2], ALU.subtract, [kt_.R()], [kc.R()])
                                kb.tt(kc[:, k, 1:2], kt_[:, 2:3], kt_[:, 3:4], ALU.add, [kt_.R()], [kc.R()])
                            Sb = Ss[si_ % 2]
                            si_ += 1
                            Z_r, Z_i = v3(Zt[:, 0, :n]), v3(Zt[:, 1, :n])
                            kb.tt(a0, Z_r, tb(0), ALU.mult, [Zt.R()] + TR, [w4[0].R()])
                            kb.tt(a1, Z_i, tb(1), ALU.mult, [Zt.R()] + TR, [w4[1].R()])
                            kb.tt(a2, Z_i, tb(0), ALU.mult, [Zt.R()] + TR, [w4[2].R()])
                            kb.tt(a3, Z_r, tb(1), ALU.mult, [Zt.R()] + TR, [w4[3].R()])
                            kb.tt(Sb[:, 0, :n], w4[0][:, :n], w4[1][:, :n], ALU.subtract, [w4[0].R(), w4[1].R()], [Sb.R()])
                            kb.stt(Sb[:, 1, :n], w4[2][:, :n], -1.0, w4[3][:, :n], ALU.mult, ALU.subtract, [w4[2].R(), w4[3].R()], [Sb.R()])
                            first, last = (k % 4 == 0), (k % 4 == 3)
                            kb.mm(ps[:, pY[o], :n], cP[:, 0, k, :], Sb[:, 0, :n], first, False, [cP.R(), Sb.R()], [PR[pY[o]]], sig=False)
                            kb.mm(ps[:, pY[o], :n], cP[:, 1, k, :], Sb[:, 1, :n], False, last, [cP.R(), Sb.R()], [PR[pY[o]]], sig=True)
                        for o in range(2):
                            if d_ == 0:
                                kb.stt(yacc[:, o, t0:t1_], uT[:, o, t0:t1_], mvec[:, 5 + o:6 + o], ps[:, pY[o], :n], ALU.mult, ALU.add,
                                       [uT.R(), mvec.R(), PR[pY[o]]], [yacc.R()])
                            else:
                                if bi == 0:
                                    dst = yacc[:, o, 0:NCTX][:, ::-1]
                                else:
                                    a_, b_ = t0 - NCTX, t1_ - NCTX
                                    lo_ = NCTX + (NLAT - b_)
                                    hi_ = NCTX + (NLAT - a_)
                                    dst = yacc[:, o, lo_:hi_][:, ::-1]
                                kb.tt(dst, dst, ps[:, pY[o], :n], ALU.add, [yacc.R(), PR[pY[o]]], [yacc.R()])
            with kb.scope():
                G = {"g1": kb.sb([128, 512], F32, "g1"), "g2": kb.sb([128, 512], F32, "g2")}
                yg = kb.sb([128, 2, 512], BF16, "yg")
                sg = kb.sb([128, 512], F32, "sg")
                for bi, (t0, t1_) in enumerate(TBLK):
                    n = t1_ - t0
                    for o in range(2):
                        gelu(yg[:, o, :n], yacc[:, o, t0:t1_], 128, (n,), G, [yacc.R()], [yg.R()])
                    for ot in range(2):
                        pa, pg = kb.bank(), kb.bank()
                        for (pb, cc) in ((pa, ot * 128), (pg, 256 + ot * 128)):
                            for k2 in range(2):
                                kb.mm(ps[:, pb, :n], wglu[:, k2, cc:cc + 128], yg[:, k2, :n], k2 == 0, k2 == 1, [wglu.R(), yg.R()], [PR[pb]])
                        kb.act(sg[:, :n], ps[:, pg, :n], AF.Sigmoid, [PR[pg], mvec.R()], [sg.R()], bias=mvec[:, 14 + ot:15 + ot])
                        kb.stt(BR['t'][:, 2, ot, t0:t1_], ps[:, pa, :n], mvec[:, 12 + ot:13 + ot], sg[:, :n], ALU.add, ALU.mult,
                               [PR[pa], mvec.R(), sg.R()], [BR['t'].R((2, ot, bi))])

    def merge(l):
        win_src = I["w_in"][l].rearrange("(kt p) n -> p kt n", p=128)
        wo_src = I["w_out"][l].rearrange("(kt p) n -> p kt n", p=128)
        br = BR['t']
        with kb.scope():
            W = hwork(512)
            hb = kb.sb([128, 8, 512], BF16, "hb")
            wg = [kb.sb([128, 8, 512], BF16, "wg") for _ in range(2)]
            wb = [kb.sb([128, 2, 1024], BF16, "wb") for _ in range(2)]
            mg = kb.sb([128, 8, 512], F32, "mg")
            mgb = kb.sb([128, 8, 512], BF16, "mgb")
            sg = [kb.sb([128, 512], F32, "sg") for _ in range(2)]
            tq = kb.sb([128, 512], F32, "tq")
            sqy = [kb.sb([128, 512], BF16, "sqy") for _ in range(2)]
            BRALL = [r for r in br._r.values()]
            gi_ = 0
            si_ = 0
            for bi, (t0, t1_) in enumerate(TBLK):
                n = t1_ - t0
                kb.banks_rot = list(range(7))
                compute_h(1, t0, t1_, hb, hb.R(), W)
                for i in range(4):
                    wbi = wb[i % 2]
                    load_w(wbi, wbi[:], I["w_branch"][l, i].rearrange("(kt p) n -> p kt n", p=128))
                    for half in range(2):
                        w = wg[gi_ % 2]
                        gi_ += 1
                        c0 = E_GATE + i * 1024 + half * 512
                        for hh in range(2):
                            load_w(w, w[:, hh * 4:(hh + 1) * 4, :], win_src[:, hh * 4:(hh + 1) * 4, c0:c0 + 512], hh)
                        for d4 in range(4):
                            dt = half * 4 + d4
                            pgt, ppj = kb.bank(), kb.bank()
                            for kt in range(8):
                                kb.mm(ps[:, pgt, :n], w[:, kt, d4 * 128:(d4 + 1) * 128], hb[:, kt, :n], kt == 0, kt == 7, [w.R(0), w.R(1), hb.R()], [PR[pgt]])
                            for k2 in range(2):
                                kb.mm(ps[:, ppj, :n], wbi[:, k2, dt * 128:(dt + 1) * 128], br[:, i, k2, t0:t1_], k2 == 0, k2 == 1, [wbi.R()] + BRALL, [PR[ppj]])
                            s_ = sg[si_ % 2]
                            si_ += 1
                            kb.act(s_[:, :n], ps[:, pgt, :n], AF.Sigmoid, [PR[pgt]], [s_.R()])
                            if i == 0:
                                kb.tt(mg[:, dt, :n], s_[:, :n], ps[:, ppj, :n], ALU.mult, [s_.R(), PR[ppj]], [mg.R(dt)])
                            else:
                                kb.tt(tq[:, :n], s_[:, :n], ps[:, ppj, :n], ALU.mult, [s_.R(), PR[ppj]], [tq.R()])
                                kb.tt(mg[:, dt, :n], mg[:, dt, :n], tq[:, :n], ALU.add, [mg.R(dt), tq.R()], [mg.R(dt)])
                for dt in range(8):
                    kb.act(mgb[:, dt, :n], mg[:, dt, :n], AF.Copy, [mg.R(dt)], [mgb.R()])
                ssb = 7
                for half in range(2):
                    w = wg[gi_ % 2]
                    gi_ += 1
                    for hh in range(2):
                        load_w(w, w[:, hh * 4:(hh + 1) * 4, :], wo_src[:, hh * 4:(hh + 1) * 4, half * 512:(half + 1) * 512], hh)
                    for d4 in range(4):
                        dt = half * 4 + d4
                        pb = kb.bank()
                        for kt in range(8):
                            kb.mm(ps[:, pb, :n], w[:, kt, d4 * 128:(d4 + 1) * 128], mgb[:, kt, :n], kt == 0, kt == 7, [w.R(0), w.R(1), mgb.R()], [PR[pb]])
                        kb.act(mg[:, dt, :n], ps[:, pb, :n], AF.Copy, [PR[pb]], [mg.R(dt)])
                        sq_ = sqy[si_ % 2]
                        si_ += 1
                        kb.act(sq_[:, :n], ps[:, pb, :n], AF.Square, [PR[pb]], [sq_.R()])
                        kb.mm(ps[:, ssb, :n], ones[:, :], sq_[:, :n], dt == 0, dt == 7, [ones.R(), sq_.R()], [PR[ssb]])
                post_norm_residual(1, t0, n, mg, ssb, W, ykeys=list(range(8)))
            kb.banks_rot = list(range(8))

    for l in range(depth_run):
        if l > 0:
            kb.barrier()
            kb.rotate()
        ada(l)
        ffn(l, 0, 0)
        with kb.scope():
            BR['t'] = kb.sb([128, 4, 2, T], BF16, "br")
            mla(l)
            gqa(l)
            s5(l)
            gmlp(l)
            merge(l)
        ffn(l, 1, 2)

    kb.barrier()
    for c in range(8):
        t0 = NCTX + c * 256
        kb.dma("sp", outT[:, :, c * 256:(c + 1) * 256], xT[:, :, t0:t0 + 256], xres(t0, t0 + 256), ())
    for ch in kb.sch:
        kb.E["sp"].obj.wait_ge(ch.sem, ch.total)
    return nc, es


def _rope_tables():
    def tab(rot_dim):
        axis_dim = rot_dim // 2
        half = axis_dim // 2
        inv = 10000.0 ** (-np.arange(0, axis_dim, 2, dtype=np.float32) / axis_dim)
        t = np.arange(NLAT)
        row = (t // 64).astype(np.float32)
        col = (t % 64).astype(np.float32)
        C = np.ones((rot_dim, T), np.float32)
        S = np.zeros((rot_dim, T), np.float32)
        partner = np.zeros(rot_dim, np.int64)
        for ax, pos in ((0, row), (1, col)):
            for r in range(axis_dim):
                k = r % half
                ang = pos * inv[k]
                C[ax * axis_dim + r, NCTX:] = np.cos(ang)
                if r < half:
                    S[ax * axis_dim + r, NCTX:] = -np.sin(ang)
                    partner[ax * axis_dim + r] = ax * axis_dim + r + half
                else:
                    S[ax * axis_dim + r, NCTX:] = np.sin(ang)
                    partner[ax * axis_dim + r] = ax * axis_dim + r - half
        return C, S, partner
    Cg, Sg, pg = tab(64)
    Cm, Sm, pm = tab(32)
    ropeC = np.zeros((128, T), np.float32)
    ropeS = np.zeros((128, T), np.float32)
    ropeC[0:64] = Cg
    ropeS[0:64] = Sg
    ropeC[64:96] = Cm
    ropeS[64:96] = Sm
    ropeS[96:128] = Sm
    return ropeC, ropeS, pg, pm


_CACHE = {}


def _prep(inp):
    f = lambda k: np.asarray(inp[k], np.float32)
    ropeC, ropeS, pg, pm = _rope_tables()
    P = {}
    P["ropeC"], P["ropeS"] = ropeC, ropeS
    cm = np.zeros((128, 3, 128), np.float32)
    cm[:, 0, :] = np.eye(128)
    qi = np.arange(128)[:, None]
    kj = np.arange(128)[None, :]
    cm[:, 1, :] = np.where(qi <= kj, 0.0, -30000.0)
    cm[:, 2, :] = np.where(kj <= qi, 0.0, -30000.0)
    P["cmat"] = cm
    P["w_ada"] = f("w_ada")
    vecs = np.zeros((DEPTH, 128, 120), np.float32)
    vecs[:, :, 0:72] = f("b_ada").reshape(DEPTH, 72, 128).transpose(0, 2, 1)
    vecs[:, :, 72:96] = f("norm_pre").reshape(DEPTH, 24, 128).transpose(0, 2, 1)
    vecs[:, :, 96:120] = f("norm_post").reshape(DEPTH, 24, 128).transpose(0, 2, 1)
    P["vecs"] = vecs
    P["w_ffn_in"] = f("w_ffn_in")
    P["w_ffn_out"] = f("w_ffn_out")
    w_in = f("w_in")
    idx = np.zeros(NEXT, np.int64)
    idx[E_KVL:E_KVL + 128] = np.arange(0, 128)
    idx[E_QL:E_QL + 256] = np.arange(672, 928)
    idx[E_KPA:E_KPA + 32] = 128 + np.arange(32)
    idx[E_KPB:E_KPB + 32] = 128 + pm
    gq = 928 + np.arange(256)
    gqs = 928 + (np.arange(4)[:, None] * 64 + pg[None, :]).reshape(-1)
    gk = 160 + np.arange(128)
    gks = 160 + (np.arange(2)[:, None] * 64 + pg[None, :]).reshape(-1)
    idx[E_GQA:E_GQA + 256] = gq
    idx[E_GQB:E_GQB + 256] = gqs
    idx[E_GKA:E_GKA + 128] = gk
    idx[E_GKB:E_GKB + 128] = gks
    idx[E_GV:E_GV + 128] = 288 + np.arange(128)
    idx[E_U:E_U + 256] = 416 + np.arange(256)
    idx[E_Z:E_Z + 512] = 1184 + np.arange(512)
    idx[E_GATE:E_GATE + 4096] = 1696 + np.arange(4096)
    P["w_in"] = np.ascontiguousarray(w_in[:, :, idx])
    wuq = f("mla_w_uq").reshape(DEPTH, 256, 4, 96)
    wuq_e = np.concatenate([wuq[..., 0:64], wuq[..., 64:96], wuq[..., 64 + pm]], axis=-1)
    P["w_uq"] = np.ascontiguousarray(wuq_e.reshape(DEPTH, 256, 512))
    wukv = f("mla_w_ukv").reshape(DEPTH, 128, 4, 128)
    P["w_ukv"] = np.ascontiguousarray(np.concatenate([wukv[..., 0:64].reshape(DEPTH, 128, 256), wukv[..., 64:128].reshape(DEPTH, 128, 256)], axis=-1))
    mvec = np.zeros((DEPTH, 128, 16), np.float32)
    mvec[:, :, 0] = f("mla_kv_norm")
    mvec[:, :, 1:3] = f("mla_q_norm").reshape(DEPTH, 2, 128).transpose(0, 2, 1)
    mvec[:, :, 3:5] = f("gmlp_norm").reshape(DEPTH, 2, 128).transpose(0, 2, 1)
    mvec[:, :, 5:7] = f("s5_d").reshape(DEPTH, 2, 128).transpose(0, 2, 1)
    mvec[:, 65, 8:12] = f("gqa_sink")
    mvec[:, :, 12:16] = f("s5_b_glu").reshape(DEPTH, 4, 128).transpose(0, 2, 1)
    P["mvec"] = mvec
    P["w_branch"] = f("w_branch")
    P["w_out"] = f("w_out")
    P["wsT"] = np.ascontiguousarray(f("gmlp_w_s").transpose(0, 3, 1, 2))
    bs = f("gmlp_b_s")
    P["bsT"] = np.ascontiguousarray(np.repeat(bs.reshape(DEPTH, 2, 2, 1, 128), 64, axis=3).reshape(DEPTH, 2, 128, 128).transpose(0, 2, 1, 3))
    def st(a):
        return a.reshape(DEPTH, 2, 8, 2, 64).transpose(0, 1, 3, 4, 2).reshape(DEPTH, 2, 128, 8)
    s5p = np.zeros((DEPTH, 2, 128, 8, 4), np.float32)
    s5p[..., 0] = st(f("s5_lam_re"))
    s5p[..., 1] = st(f("s5_lam_im"))
    s5p[..., 2] = st(np.repeat(f("s5_log_dt")[..., None], 64, axis=-1))
    P["s5p"] = s5p
    s5b = np.zeros((DEPTH, 2, 2, 128, 8, 128), np.float32)
    s5c = np.zeros((DEPTH, 2, 2, 128, 8, 128), np.float32)
    bre, bim = f("s5_b_re"), f("s5_b_im")
    cre, cim = f("s5_c_re"), f("s5_c_im")
    for g in range(16):
        k, half = g // 2, g % 2
        cs = (g % 8) * 16
        s5b[:, :, 0, half * 64:(half + 1) * 64, k, cs:cs + 16] = bre[:, :, g]
        s5b[:, :, 1, half * 64:(half + 1) * 64, k, cs:cs + 16] = bim[:, :, g]
        s5c[:, :, 0, half * 64:(half + 1) * 64, k, cs:cs + 16] = cre[:, :, g].transpose(0, 1, 3, 2)
        s5c[:, :, 1, half * 64:(half + 1) * 64, k, cs:cs + 16] = cim[:, :, g].transpose(0, 1, 3, 2)
    P["s5b"], P["s5c"] = s5b, s5c
    P["w_glu"] = f("s5_w_glu")
    return P


def _core_inputs(inp, P, b):
    x = np.asarray(inp["x"], np.float32)[b]
    ctx = np.asarray(inp["ctx"], np.float32)[b]
    full = np.concatenate([ctx, x], axis=0)
    xT = np.ascontiguousarray(full.T.reshape(8, 128, T).transpose(1, 0, 2))
    cv = np.stack([np.asarray(inp["c"], np.float32)[b], np.asarray(inp["c_ctx"], np.float32)], axis=-1)
    cT = np.ascontiguousarray(cv.reshape(8, 128, 2).transpose(1, 0, 2))
    m = dict(P)
    m["xT"] = xT
    m["cT"] = cT
    return m


def kernel(**inputs):
    P = _prep(inputs)
    nc, es = build(DEPTH)
    in_maps = [_core_inputs(inputs, P, b) for b in range(8)]
    res = run_bass_kernel_spmd(nc, in_maps, core_ids=list(range(8)))
    out = np.zeros((8, NLAT, D), np.float32)
    for b in range(8):
        oT = np.asarray(res.results[b]["outT"])
        out[b] = oT.transpose(2, 1, 0).reshape(NLAT, D)
    return out
```

```python
import numpy as np
import ml_dtypes
from contextlib import ExitStack, contextmanager
import concourse.bass as bass
import concourse.mybir as mybir
from concourse.bass_utils import run_bass_kernel_spmd

F32 = mybir.dt.float32
BF16 = mybir.dt.bfloat16
ALU = mybir.AluOpType
AF = mybir.ActivationFunctionType
AX = mybir.AxisListType

D = 1024
T = 2304
NCTX = 256
NLAT = 2048
DFF = 2816
DEPTH = 4
EPS = 1e-6
SAME_ENG_SYNC = True

E_KVL, E_QL, E_KPA, E_KPB = 0, 128, 384, 416
E_GQA, E_GQB, E_GKA, E_GKB, E_GV, E_U, E_Z, E_GATE = 512, 768, 1024, 1152, 1280, 1408, 1664, 2176
NEXT = 6272
TBLK = [(0, 256), (256, 768), (768, 1280), (1280, 1792), (1792, 2304)]
GC = 1.5957691216057308


class Res:
    __slots__ = ("w", "r")

    def __init__(self):
        self.w = None
        self.r = {}


class Eng:
    def __init__(self, obj, pe=False):
        self.obj = obj
        self.pe = pe
        self.sem = None
        self.cnt = 0
        self.known = {}
        self.pend = []


class Chan:
    def __init__(self, sem):
        self.sem = sem
        self.total = 0


class Tl:
    def __init__(self, h):
        self.h = h
        self._r = {}

    def __getitem__(self, k):
        return self.h[k]

    def R(self, key=0):
        r = self._r.get(key)
        if r is None:
            r = self._r[key] = Res()
        return r


class KB:
    def __init__(self, nc, es):
        self.nc = nc
        self.es = es
        self.stack = [es]
        self.E = {"pe": Eng(nc.tensor, True), "act": Eng(nc.scalar), "dve": Eng(nc.vector),
                  "pool": Eng(nc.gpsimd), "sp": Eng(nc.sync)}
        self.nsem = 0
        self.nname = 0
        self.rotate()
        self.wch = [Chan(self.newsem()) for _ in range(6)]
        self.sch = [Chan(self.newsem()) for _ in range(4)]
        self.wi = 0
        self.si = 0
        self.ps = Tl(es.enter_context(nc.psum_tensor("ps", [128, 8, 512], F32)))
        self.bank_i = 0
        self.banks_rot = list(range(8))

    def newsem(self):
        self.nsem += 1
        return self.es.enter_context(self.nc.semaphore(f"sem{self.nsem}"))

    def rotate(self):
        for e in self.E.values():
            assert not e.pend
            e.sem = self.newsem()
            e.cnt = 0

    def sb(self, shape, dt, name=None):
        self.nname += 1
        return Tl(self.stack[-1].enter_context(self.nc.sbuf_tensor(f"{name or 't'}_{self.nname}", list(shape), dt)))

    @contextmanager
    def scope(self):
        st = ExitStack()
        self.stack.append(st)
        try:
            yield
        finally:
            self.barrier()
            self.stack.pop()
            st.close()

    def barrier(self):
        chans = self.wch + self.sch
        for e in self.E.values():
            assert not e.pend
        for e in self.E.values():
            for f in self.E.values():
                if f is e or f.cnt == 0:
                    continue
                if e.known.get(id(f.sem), 0) < f.cnt:
                    e.obj.wait_ge(f.sem, f.cnt)
                    e.known[id(f.sem)] = f.cnt
            for c in chans:
                if c.total and e.known.get(id(c.sem), 0) < c.total:
                    e.obj.wait_ge(c.sem, c.total)
                    e.known[id(c.sem)] = c.total

    def bank(self):
        b = self.banks_rot[self.bank_i % len(self.banks_rot)]
        self.bank_i += 1
        return b

    def _waits(self, eng, reads, writes):
        deps = {}

        def add(sv):
            sem, val = sv
            k = id(sem)
            if k not in deps or deps[k][1] < val:
                deps[k] = (sem, val)

        for r in reads:
            if r.w is not None:
                add(r.w)
        for w in writes:
            if w.w is not None and w.w[0] is not eng.sem:
                add(w.w)
            for sv in w.r.values():
                if sv[0] is not eng.sem:
                    add(sv)
        for k, (sem, val) in deps.items():
            if sem is eng.sem and (eng.pe or not SAME_ENG_SYNC):
                continue
            if eng.known.get(k, 0) >= val:
                continue
            eng.obj.wait_ge(sem, val)
            eng.known[k] = val

    def emit(self, en, fn, reads=(), writes=(), sig=True):
        eng = self.E[en]
        self._waits(eng, reads, writes)
        ins = fn(eng.obj)
        eng.pend.append((reads, writes))
        if sig:
            eng.cnt += 1
            ins.then_inc(eng.sem, 1)
            st = (eng.sem, eng.cnt)
            for rs, ws in eng.pend:
                for w in ws:
                    w.w = st
                    w.r = {}
                for r in rs:
                    r.r[id(eng.sem)] = st
            eng.pend = []
        return ins

    def dma(self, q, out, in_, reads=(), writes=(), weight=False):
        eng = self.E[q]
        if weight:
            ch = self.wch[self.wi % len(self.wch)]
            self.wi += 1
        else:
            ch = self.sch[self.si % len(self.sch)]
            self.si += 1
        self._waits(eng, reads, writes)
        if ch.total and eng.known.get(id(ch.sem), 0) < ch.total:
            eng.obj.wait_ge(ch.sem, ch.total)
            eng.known[id(ch.sem)] = ch.total
        ins = eng.obj.dma_start(out=out, in_=in_)
        ins.then_inc(ch.sem, 16)
        ch.total += 16
        st = (ch.sem, ch.total)
        for w in writes:
            w.w = st
            w.r = {}
        for r in reads:
            r.r[id(ch.sem)] = st

    def mm(self, out, lhsT, rhs, start, stop, r, w, sig=None):
        if sig is None:
            sig = stop
        self.emit("pe", lambda e: e.matmul(out, lhsT, rhs, start=start, stop=stop), r, w, sig)

    def act(self, out, in_, func, r, w, bias=None, scale=1.0):
        kw = {}
        if bias is not None:
            kw["bias"] = bias
        self.emit("act", lambda e: e.activation(out=out, in_=in_, func=func, scale=scale, **kw), r, w)

    def tt(self, out, in0, in1, op, r, w, en="dve"):
        self.emit(en, lambda e: e.tensor_tensor(out=out, in0=in0, in1=in1, op=op), r, w)

    def ts(self, out, in0, s1, s2, op0, op1, r, w, en="dve"):
        if s2 is None:
            self.emit(en, lambda e: e.tensor_scalar(out=out, in0=in0, scalar1=s1, scalar2=None, op0=op0), r, w)
        else:
            self.emit(en, lambda e: e.tensor_scalar(out=out, in0=in0, scalar1=s1, scalar2=s2, op0=op0, op1=op1), r, w)

    def stt(self, out, in0, sc, in1, op0, op1, r, w):
        self.emit("dve", lambda e: e.scalar_tensor_tensor(out=out, in0=in0, scalar=sc, in1=in1, op0=op0, op1=op1), r, w)

    def cp(self, out, in_, r, w, en="dve"):
        self.emit(en, lambda e: e.tensor_copy(out=out, in_=in_), r, w)

    def recip(self, out, in_, r, w):
        self.emit("dve", lambda e: e.reciprocal(out=out, in_=in_), r, w)

    def memset(self, ap, val, w, en="dve"):
        self.emit(en, lambda e: e.memset(ap, val), (), w)


def split_tok(t0, t1):
    out = []
    if t0 < NCTX:
        out.append((t0, min(t1, NCTX), 1))
    if t1 > NCTX:
        out.append((max(t0, NCTX), t1, 0))
    return out


def build(depth_run, dbg=None):
    nc = bass.Bass("TRN2", target_bir_lowering=False)
    es = ExitStack()
    dr = lambda name, shape, dt=F32: nc.dram_tensor(name, list(shape), dt, kind="ExternalInput").ap()
    I = {}
    I["xT"] = dr("xT", [128, 8, T])
    I["cT"] = dr("cT", [128, 8, 2])
    I["w_ada"] = dr("w_ada", [DEPTH, D, 9 * D])
    I["vecs"] = dr("vecs", [DEPTH, 128, 120])
    I["w_ffn_in"] = dr("w_ffn_in", [DEPTH, 2, D, 2 * DFF])
    I["w_ffn_out"] = dr("w_ffn_out", [DEPTH, 2, DFF, D])
    I["w_in"] = dr("w_in", [DEPTH, D, NEXT])
    I["w_uq"] = dr("w_uq", [DEPTH, 256, 512])
    I["w_ukv"] = dr("w_ukv", [DEPTH, 128, 512])
    I["mvec"] = dr("mvec", [DEPTH, 128, 16])
    I["ropeC"] = dr("ropeC", [128, T])
    I["ropeS"] = dr("ropeS", [128, T])
    I["cmat"] = dr("cmat", [128, 3, 128])
    I["w_branch"] = dr("w_branch", [DEPTH, 4, 256, D])
    I["w_out"] = dr("w_out", [DEPTH, D, D])
    I["wsT"] = dr("wsT", [DEPTH, 128, 4, 128])
    I["bsT"] = dr("bsT", [DEPTH, 128, 2, 128])
    I["s5p"] = dr("s5p", [DEPTH, 2, 128, 8, 4])
    I["s5b"] = dr("s5b", [DEPTH, 2, 2, 128, 8, 128])
    I["s5c"] = dr("s5c", [DEPTH, 2, 2, 128, 8, 128])
    I["w_glu"] = dr("w_glu", [DEPTH, 256, 512])
    outT = nc.dram_tensor("outT", [128, 8, NLAT], F32, kind="ExternalOutput").ap()
    dbg_aps = {}
    if dbg:
        for name, shape in dbg.items():
            dbg_aps[name] = nc.dram_tensor(name, list(shape), F32, kind="ExternalOutput").ap()

    kb = KB(nc, es)
    ps = kb.ps
    PR = [ps.R(b) for b in range(8)]

    xT = kb.sb([128, 8, T], F32, "xT")
    ones = kb.sb([128, 128], BF16, "ones")
    onesf = kb.sb([128, 512], F32, "onesf")
    cmat_f = kb.sb([128, 3, 128], F32, "cmatf")
    cmat = kb.sb([128, 3, 128], BF16, "cmat")
    ropeC = kb.sb([128, T], BF16, "ropeC")
    ropeS = kb.sb([128, T], BF16, "ropeS")
    sc = kb.sb([128, 8, 2], BF16, "sc")
    vecs = kb.sb([128, 120], F32, "vecs")
    mvec = kb.sb([128, 16], F32, "mvec")
    modT = kb.sb([128, 72, 2], F32, "modT")
    modD = kb.sb([128, 3, 2, 3, 8], F32, "modD")
    BR = {}

    XR = lambda blk: xT.R(blk)

    def xres(t0, t1):
        return [xT.R(c) for c in range(t0 // 288, (t1 - 1) // 288 + 1)]

    for c in range(8):
        kb.dma("sp", xT[:, :, c * 288:(c + 1) * 288], I["xT"][:, :, c * 288:(c + 1) * 288], (), [xT.R(c)])
    kb.memset(ones[:], 1.0, [ones.R()])
    kb.memset(onesf[:], 1.0, [onesf.R()])
    kb.dma("sp", cmat_f[:], I["cmat"], (), [cmat_f.R()])
    kb.cp(cmat[:], cmat_f[:], [cmat_f.R()], [cmat.R()])
    kb.dma("pool", ropeC[:], I["ropeC"], (), [ropeC.R()], weight=True)
    kb.dma("pool", ropeS[:], I["ropeS"], (), [ropeS.R()], weight=True)
    with kb.scope():
        ctf = kb.sb([128, 8, 2], F32, "ctf")
        kb.dma("sp", ctf[:], I["cT"], (), [ctf.R()])
        kb.act(sc[:], ctf[:], AF.Silu, [ctf.R()], [sc.R()])
    ident = cmat_f[:, 0, :]

    def rstd_from_ps(pb, n, out_ap, out_res, inv_n, eps, tmp):
        kb.act(tmp[:, :n], ps[:, pb, :n], AF.Sqrt, [PR[pb]], [tmp.R()], bias=epsT[:, 0:1] if eps == EPS else epsT[:, 1:2], scale=inv_n)
        kb.recip(out_ap, tmp[:, :n], [tmp.R()], [out_res])

    epsT = kb.sb([128, 2], F32, "epsT")
    kb.memset(epsT[:, 0:1], EPS, [epsT.R()])
    kb.memset(epsT[:, 1:2], 1e-5, [epsT.R()])

    def compute_h(s, t0, t1, hout, hres, W):
        n = t1 - t0
        sq, tmp, rstd, tmp2 = W["sq"], W["tmp"], W["rstd"], W["tmp2"]
        kb.act(sq[:, :, :n], xT[:, :, t0:t1], AF.Square, xres(t0, t1), [sq.R()])
        pb = kb.bank()
        for kt in range(8):
            kb.mm(ps[:, pb, :n], ones[:, :], sq[:, kt, :n], kt == 0, kt == 7, [ones.R(), sq.R()], [PR[pb]])
        rstd_from_ps(pb, n, rstd[:, :n], rstd.R(), 1.0 / D, EPS, tmp)
        for kt in range(8):
            for lo, hi, j in split_tok(t0, t1):
                a, b = lo - t0, hi - t0
                kb.stt(tmp2[:, a:b], xT[:, kt, lo:hi], modD[:, s, j, 0, kt:kt + 1], rstd[:, a:b], ALU.mult, ALU.mult,
                       xres(lo, hi) + [modD.R(), rstd.R()], [tmp2.R()])
                kb.act(hout[:, kt, a:b], tmp2[:, a:b], AF.Identity, [tmp2.R(), modD.R()], [hres],
                       bias=modD[:, s, j, 1, kt:kt + 1])

    def hwork(n):
        return {"sq": kb.sb([128, 8, n], BF16, "sq"), "tmp": kb.sb([128, 512], F32, "tmp"),
                "rstd": kb.sb([128, 512], F32, "rstd"), "tmp2": kb.sb([128, 512], F32, "tmp2")}

    def gelu(out_ap, in_ap, n_part, shape_free, G, rin, wout):
        g1, g2 = G["g1"], G["g2"]
        sl = (slice(0, n_part),) + tuple(slice(0, f) for f in shape_free)
        kb.act(g1[sl], in_ap, AF.Square, rin, [g1.R()])
        kb.ts(g1[sl], g1[sl], 0.044715, 1.0, ALU.mult, ALU.add, [g1.R()], [g1.R()])
        kb.tt(g1[sl], g1[sl], in_ap, ALU.mult, [g1.R()] + rin, [g1.R()])
        kb.act(g2[sl], g1[sl], AF.Sigmoid, [g1.R()], [g2.R()], scale=GC)
        kb.tt(out_ap, g2[sl], in_ap, ALU.mult, [g2.R()] + rin, wout)

    def ada(l):
        kb.dma("sp", vecs[:], I["vecs"][l], (), [vecs.R()])
        kb.dma("sp", mvec[:], I["mvec"][l], (), [mvec.R()])
        with kb.scope():
            wa = [kb.sb([128, 8, 1024], BF16, "wada") for _ in range(2)]
            pb = kb.bank()
            src = I["w_ada"][l].rearrange("(kt p) n -> p kt n", p=128)
            for m in range(9):
                w = wa[m % 2]
                for hh in range(2):
                    kb.dma("pool", w[:, hh * 4:(hh + 1) * 4, :], src[:, hh * 4:(hh + 1) * 4, m * 1024:(m + 1) * 1024], (), [w.R(hh)], weight=True)
                for ot in range(8):
                    col = (m * 8 + ot) * 2
                    for kt in range(8):
                        kb.mm(ps[:, pb, col:col + 2], w[:, kt, ot * 128:(ot + 1) * 128], sc[:, kt, :], kt == 0, kt == 7,
                              [w.R(kt // 4), sc.R()], [PR[pb]])
            for j in range(2):
                kb.tt(modT[:, :, j], ps[:, pb, 0:144].rearrange("p (m j) -> p m j", j=2)[:, :, j], vecs[:, 0:72], ALU.add,
                      [PR[pb], vecs.R()], [modT.R()])
            for s in range(3):
                for j in range(2):
                    wgt = 1.0 if s == 1 else 0.5
                    kb.stt(modD[:, s, j, 0, :], modT[:, (3 * s + 1) * 8:(3 * s + 2) * 8, j], 1.0, vecs[:, 72 + s * 8:72 + (s + 1) * 8],
                           ALU.add, ALU.mult, [modT.R(), vecs.R()], [modD.R()])
                    kb.cp(modD[:, s, j, 1, :], modT[:, (3 * s) * 8:(3 * s + 1) * 8, j], [modT.R()], [modD.R()])
                    kb.stt(modD[:, s, j, 2, :], modT[:, (3 * s + 2) * 8:(3 * s + 3) * 8, j], wgt, vecs[:, 96 + s * 8:96 + (s + 1) * 8],
                           ALU.mult, ALU.mult, [modT.R(), vecs.R()], [modD.R()])

    def post_norm_residual(s, t0, n, ybuf, ssb, W, ykeys=None):
        tmp, rstd, tmp2 = W["tmp"], W["rstd"], W["tmp2"]
        rstd_from_ps(ssb, n, rstd[:, :n], rstd.R(), 1.0 / D, EPS, tmp)
        for kt in range(8):
            kb.tt(tmp2[:, :n], ybuf[:, kt, :n], rstd[:, :n], ALU.mult, [ybuf.R(kt if ykeys else 0), rstd.R()], [tmp2.R()])
            for lo, hi, j in split_tok(t0, t0 + n):
                a, b = lo - t0, hi - t0
                kb.stt(xT[:, kt, lo:hi], tmp2[:, a:b], modD[:, s, j, 2, kt:kt + 1], xT[:, kt, lo:hi], ALU.mult, ALU.add,
                       [tmp2.R(), modD.R()] + xres(lo, hi), xres(lo, hi))

    def ffn(l, jf, s):
        TB, NC_ = 576, 288
        with kb.scope():
            W = hwork(NC_)
            hT = kb.sb([128, 8, TB], BF16, "hT")
            hid = kb.sb([128, 22, TB], BF16, "hid")
            ybuf = [kb.sb([128, 8, NC_], F32, "ybuf") for _ in range(2)]
            win = [kb.sb([128, 8, 2, 256], BF16, "win") for _ in range(2)]
            wout = [kb.sb([128, 22, 256], BF16, "wout") for _ in range(2)]
            sa = [kb.sb([128, NC_], F32, "sa") for _ in range(2)]
            sqy = [kb.sb([128, NC_], BF16, "sqy") for _ in range(2)]
            src_in = I["w_ffn_in"][l, jf].rearrange("(kt p) (two f) -> p kt two f", p=128, two=2)
            src_out = I["w_ffn_out"][l, jf].rearrange("(ft p) d -> p ft d", p=128)
            wi = wo = si = 0
            for blk in range(4):
                t0 = blk * TB
                kb.banks_rot = list(range(8))
                for c in range(2):
                    compute_h(s, t0 + c * NC_, t0 + (c + 1) * NC_, hT[:, :, c * NC_:(c + 1) * NC_], hT.R(c), W)
                for fg in range(11):
                    w = win[wi % 2]
                    wi += 1
                    for gi in range(2):
                        kb.dma("pool", w[:, :, gi, :], src_in[:, :, gi, fg * 256:(fg + 1) * 256], (), [w.R(gi)], weight=True)
                    for fi in range(2):
                        f = fg * 2 + fi
                        for c in range(2):
                            ba, bg = kb.bank(), kb.bank()
                            for gi, pb in ((0, ba), (1, bg)):
                                for kt in range(8):
                                    kb.mm(ps[:, pb, :NC_], w[:, kt, gi, fi * 128:(fi + 1) * 128], hT[:, kt, c * NC_:(c + 1) * NC_],
                                          kt == 0, kt == 7, [w.R(gi), hT.R(c)], [PR[pb]])
                            sab = sa[si % 2]
                            si += 1
                            kb.act(sab[:], ps[:, ba, :NC_], AF.Silu, [PR[ba]], [sab.R()])
                            kb.tt(hid[:, f, c * NC_:(c + 1) * NC_], sab[:], ps[:, bg, :NC_], ALU.mult, [sab.R(), PR[bg]], [hid.R((f, c))])
                kb.banks_rot = list(range(6))
                ssb = (6, 7)
                for dg in range(4):
                    w = wout[wo % 2]
                    wo += 1
                    for hh in range(2):
                        kb.dma("pool", w[:, hh * 11:(hh + 1) * 11, :], src_out[:, hh * 11:(hh + 1) * 11, dg * 256:(dg + 1) * 256], (), [w.R(hh)], weight=True)
                    for di in range(2):
                        dt = dg * 2 + di
                        for c in range(2):
                            pb = kb.bank()
                            for ft in range(22):
                                kb.mm(ps[:, pb, :NC_], w[:, ft, di * 128:(di + 1) * 128], hid[:, ft, c * NC_:(c + 1) * NC_],
                                      ft == 0, ft == 21, [w.R(ft // 11), hid.R((ft, c))], [PR[pb]])
                            kb.act(ybuf[c][:, dt, :], ps[:, pb, :NC_], AF.Copy, [PR[pb]], [ybuf[c].R()])
                            sq_ = sqy[si % 2]
                            si += 1
                            kb.act(sq_[:], ps[:, pb, :NC_], AF.Square, [PR[pb]], [sq_.R()])
                            kb.mm(ps[:, ssb[c], :NC_], ones[:, :], sq_[:], dt == 0, dt == 7, [ones.R(), sq_.R()], [PR[ssb[c]]])
                for c in range(2):
                    post_norm_residual(s, t0 + c * NC_, NC_, ybuf[c], ssb[c], W)
            kb.banks_rot = list(range(8))

    def load_w(dst_tl, dst_ap, src_ap, key=0):
        kb.dma("pool", dst_ap, src_ap, (), [dst_tl.R(key)], weight=True)

    def mla(l):
        scale = 96.0 ** -0.5
        win_src = I["w_in"][l].rearrange("(kt p) n -> p kt n", p=128)
        with kb.scope():
            wuq = kb.sb([128, 2, 512], BF16, "wuq")
            wukv = kb.sb([128, 512], BF16, "wukv")
            kvn = kb.sb([128, T], BF16, "kvn")
            qn = kb.sb([128, 2, T], BF16, "qn")
            kpeR = kb.sb([128, T], BF16, "kpeR")
            sqb = kb.sb([128, 3, 512], BF16, "sqb")
            t1 = kb.sb([128, 512], F32, "t1")
            t2 = kb.sb([128, 512], F32, "t2")
            rs = kb.sb([128, 512], F32, "rs")
            kmax = kb.sb([128, 4], F32, "kmax")
            load_w(wuq, wuq[:], I["w_uq"][l].rearrange("(kt p) n -> p kt n", p=128))
            load_w(wukv, wukv[:], I["w_ukv"][l])
            mla_p1(l, win_src, kvn, qn, kpeR, sqb, t1, t2, rs)
            mla_p2(l, scale, wuq, wukv, kvn, qn, kpeR, sqb, t1, t2, kmax)

    def mla_p1(l, win_src, kvn, qn, kpeR, sqb, t1, t2, rs):
        with kb.scope():
            W = hwork(512)
            hb = kb.sb([128, 8, 512], BF16, "hb")
            wst = kb.sb([128, 8, 448], BF16, "wmla")
            raw = kb.sb([128, 3, 512], F32, "raw")
            for hh in range(2):
                load_w(wst, wst[:, hh * 4:(hh + 1) * 4, :], win_src[:, hh * 4:(hh + 1) * 4, 0:448], hh)
            WST = [wst.R(0), wst.R(1)]
            for bi, (t0, t1_) in enumerate(TBLK):
                n = t1_ - t0
                compute_h(1, t0, t1_, hb, hb.R(), W)
                pbs = [kb.bank() for _ in range(3)]
                for oi, (c0, pb) in enumerate(zip((E_KVL, E_QL, E_QL + 128), pbs)):
                    for kt in range(8):
                        kb.mm(ps[:, pb, :n], wst[:, kt, c0:c0 + 128], hb[:, kt, :n], kt == 0, kt == 7, WST + [hb.R()], [PR[pb]])
                    kb.act(raw[:, oi, :n], ps[:, pb, :n], AF.Copy, [PR[pb]], [raw.R(oi)])
                    kb.act(sqb[:, oi, :n], ps[:, pb, :n], AF.Square, [PR[pb]], [sqb.R(oi)])
                pb = kb.bank()
                kb.mm(ps[:, pb, :n], ones[:, :], sqb[:, 0, :n], True, True, [ones.R(), sqb.R(0)], [PR[pb]])
                rstd_from_ps(pb, n, rs[:, :n], rs.R(), 1.0 / 128, EPS, t1)
                kb.stt(kvn[:, t0:t1_], raw[:, 0, :n], mvec[:, 0:1], rs[:, :n], ALU.mult, ALU.mult, [raw.R(0), mvec.R(), rs.R()], [kvn.R(bi)])
                pb = kb.bank()
                for oi in (1, 2):
                    kb.mm(ps[:, pb, :n], ones[:, :], sqb[:, oi, :n], oi == 1, oi == 2, [ones.R(), sqb.R(oi)], [PR[pb]])
                rstd_from_ps(pb, n, rs[:, :n], rs.R(), 1.0 / 256, EPS, t1)
                for oi in (1, 2):
                    kb.stt(qn[:, oi - 1, t0:t1_], raw[:, oi, :n], mvec[:, oi:oi + 1], rs[:, :n], ALU.mult, ALU.mult,
                           [raw.R(oi), mvec.R(), rs.R()], [qn.R(bi)])
                pa, pbb = kb.bank(), kb.bank()
                for c0, pb in ((E_KPA, pa), (E_KPB, pbb)):
                    for kt in range(8):
                        kb.mm(ps[64:96, pb, :n], wst[:, kt, c0:c0 + 32], hb[:, kt, :n], kt == 0, kt == 7, WST + [hb.R()], [PR[pb]])
                kb.tt(t1[64:96, :n], ps[64:96, pa, :n], ropeC[64:96, t0:t1_], ALU.mult, [PR[pa], ropeC.R()], [t1.R()])
                kb.tt(t2[64:96, :n], ps[64:96, pbb, :n], ropeS[64:96, t0:t1_], ALU.mult, [PR[pbb], ropeS.R()], [t2.R()])
                kb.tt(kpeR[64:96, t0:t1_], t1[64:96, :n], t2[64:96, :n], ALU.add, [t1.R(), t2.R()], [kpeR.R(bi)])

    def mla_p2(l, scale, wuq, wukv, kvn, qn, kpeR, sqb, t1, t2, kmax):
        with kb.scope():
            KTt = [kb.sb([128, T], BF16, "KT") for _ in range(2)]
            QTt = [kb.sb([128, T], BF16, "QT") for _ in range(2)]
            V = kb.sb([128, 18, 256], BF16, "V")
            pT = [kb.sb([128, 512], BF16, "pT") for _ in range(2)]
            rd = kb.sb([128, 512], F32, "rd")
            for kbk in range(18):
                pb = kb.bank()
                kb.mm(ps[:, pb, :256], kvn[:, kbk * 128:(kbk + 1) * 128], wukv[:, 256:512], True, True,
                      [kvn.R(b) for b in range(5)] + [wukv.R()], [PR[pb]])
                kb.act(V[:, kbk, :], ps[:, pb, :256], AF.Copy, [PR[pb]], [V.R()])
            KVN = [kvn.R(b) for b in range(5)]
            QN = [qn.R(b) for b in range(5)]
            for pair in range(2):
                for hi_ in range(2):
                    kb.memset(KTt[hi_][96:97, :], 1.0, [KTt[hi_].R()])
                    kb.memset(kmax[:, 2 * pair + hi_:2 * pair + hi_ + 1], 0.0, [kmax.R()])
                for bi, (t0, t1_) in enumerate(TBLK):
                    n = t1_ - t0
                    pb = kb.bank()
                    kb.mm(ps[:, pb, :n], wukv[:, pair * 128:(pair + 1) * 128], kvn[:, t0:t1_], True, True, [wukv.R()] + KVN, [PR[pb]])
                    for hi_ in range(2):
                        KTh = KTt[hi_]
                        kb.act(KTh[0:64, t0:t1_], ps[hi_ * 64:(hi_ + 1) * 64, pb, :n], AF.Copy, [PR[pb]], [KTh.R()])
                        kb.cp(KTh[64:96, t0:t1_], kpeR[64:96, t0:t1_], [kpeR.R(bi)], [KTh.R()])
                        kb.act(sqb[0:96, 0, :n], KTh[0:96, t0:t1_], AF.Square, [KTh.R()], [sqb.R(0)])
                        p2 = kb.bank()
                        kb.mm(ps[:, p2, :n], ones[0:96, :], sqb[0:96, 0, :n], True, True, [ones.R(), sqb.R(0)], [PR[p2]])
                        kb.emit("dve", lambda e: e.tensor_reduce(out=t1[:, 0:1], in_=ps[:, p2, :n], axis=AX.X, op=ALU.max), [PR[p2]], [t1.R()])
                        hcol = 2 * pair + hi_
                        kb.tt(kmax[:, hcol:hcol + 1], kmax[:, hcol:hcol + 1], t1[:, 0:1], ALU.max, [kmax.R(), t1.R()], [kmax.R()])
                for hi_ in range(2):
                    hcol = 2 * pair + hi_
                    kb.act(kmax[:, hcol:hcol + 1], kmax[:, hcol:hcol + 1], AF.Sqrt, [kmax.R()], [kmax.R()])
                    kb.ts(kmax[:, hcol:hcol + 1], kmax[:, hcol:hcol + 1], -1.0, None, ALU.mult, None, [kmax.R()], [kmax.R()])
                for bi, (t0, t1_) in enumerate(TBLK):
                    n = t1_ - t0
                    for hi_ in range(2):
                        h = 2 * pair + hi_
                        QTh = QTt[hi_]
                        pb = kb.bank()
                        for k2 in range(2):
                            kb.mm(ps[:, pb, :n], wuq[:, k2, h * 128:(h + 1) * 128], qn[:, k2, t0:t1_], k2 == 0, k2 == 1, [wuq.R()] + QN, [PR[pb]])
                        kb.act(QTh[0:64, t0:t1_], ps[0:64, pb, :n], AF.Copy, [PR[pb]], [QTh.R()])
                        kb.tt(t1[64:96, :n], ps[64:96, pb, :n], ropeC[64:96, t0:t1_], ALU.mult, [PR[pb], ropeC.R()], [t1.R()])
                        kb.tt(t2[64:96, :n], ps[96:128, pb, :n], ropeS[96:128, t0:t1_], ALU.mult, [PR[pb], ropeS.R()], [t2.R()])
                        kb.tt(QTh[64:96, t0:t1_], t1[64:96, :n], t2[64:96, :n], ALU.add, [t1.R(), t2.R()], [QTh.R()])
                        kb.act(sqb[0:96, 0, :n], QTh[0:96, t0:t1_], AF.Square, [QTh.R()], [sqb.R(0)])
                        p2 = kb.bank()
                        kb.mm(ps[:, p2, :n], ones[0:96, :], sqb[0:96, 0, :n], True, True, [ones.R(), sqb.R(0)], [PR[p2]])
                        kb.act(t1[96:97, :n], ps[96:97, p2, :n], AF.Sqrt, [PR[p2]], [t1.R()])
                        kb.ts(QTh[96:97, t0:t1_], t1[96:97, :n], kmax[96:97, h:h + 1], None, ALU.mult, None, [t1.R(), kmax.R()], [QTh.R()])
                pti = 0
                for hi_ in range(2):
                    h = 2 * pair + hi_
                    KTh, QTh = KTt[hi_], QTt[hi_]
                    for gi, (q0, q1) in enumerate(TBLK):
                        nq = q1 - q0
                        kbs = list(range(2)) if gi == 0 else list(range(18))
                        po = kb.bank()
                        for ki, kbk in enumerate(kbs):
                            pS = kb.bank()
                            while pS == po:
                                pS = kb.bank()
                            kb.mm(ps[:, pS, :nq], KTh[0:97, kbk * 128:(kbk + 1) * 128], QTh[0:97, q0:q1], True, True, [KTh.R(), QTh.R()], [PR[pS]])
                            p_ = pT[pti % 2]
                            pti += 1
                            kb.act(p_[:, :nq], ps[:, pS, :nq], AF.Exp, [PR[pS]], [p_.R()], scale=scale)
                            last = ki == len(kbs) - 1
                            kb.mm(ps[0:64, po, :nq], V[:, kbk, h * 64:(h + 1) * 64], p_[:, :nq], ki == 0, last, [V.R(), p_.R()], [PR[po]], sig=last)
                            kb.mm(ps[64:128, po, :nq], ones[:, 0:64], p_[:, :nq], ki == 0, last, [ones.R(), p_.R()], [PR[po]], sig=True)
                        kb.recip(rd[0:64, :nq], ps[64:128, po, :nq], [PR[po]], [rd.R()])
                        kb.tt(BR['t'][hi_ * 64:(hi_ + 1) * 64, 0, pair, q0:q1], ps[0:64, po, :nq], rd[0:64, :nq], ALU.mult, [PR[po], rd.R()], [BR['t'].R((0, pair, gi))])

    def gqa(l):
        scale = 0.125
        win_src = I["w_in"][l].rearrange("(kt p) n -> p kt n", p=128)
        with kb.scope():
            W = hwork(512)
            hb = kb.sb([128, 8, 512], BF16, "hb")
            wst = kb.sb([128, 8, 896], BF16, "wgqa")
            QG = [kb.sb([128, T], BF16, "QG") for _ in range(4)]
            KG = [kb.sb([128, T], BF16, "KG") for _ in range(2)]
            VG = kb.sb([128, 18, 128], BF16, "VG")
            sqb = kb.sb([128, 512], BF16, "sqb")
            t1 = kb.sb([128, 512], F32, "t1")
            t2 = kb.sb([128, 512], F32, "t2")
            kmax = kb.sb([128, 2], F32, "kmax")
            sst = kb.sb([128, 4], F32, "sst")
            ksink = kb.sb([128, 4], BF16, "ksink")
            pT = [kb.sb([128, 5, 128], BF16, "pT") for _ in range(2)]
            psk = [kb.sb([128, 128], BF16, "psk") for _ in range(2)]
            rd = kb.sb([128, 128], F32, "rd")
            for hh in range(2):
                load_w(wst, wst[:, hh * 4:(hh + 1) * 4, :], win_src[:, hh * 4:(hh + 1) * 4, E_GQA:E_GQA + 896], hh)
            WST = [wst.R(0), wst.R(1)]
            kb.memset(sst[64:66, :], scale, [sst.R()])
            kb.dma("sp", sst[65:66, :], I["mvec"][l, 65:66, 8:12], (), [sst.R()])
            kb.memset(ksink[0:66, :], 0.0, [ksink.R()])
            kb.ts(ksink[64:66, :], sst[64:66, :], 1.0 / scale, None, ALU.mult, None, [sst.R()], [ksink.R()])
            for h in range(4):
                kb.memset(QG[h][64:66, :], 1.0, [QG[h].R()])
            for kv in range(2):
                kb.memset(KG[kv][64:66, :], 0.0, [KG[kv].R()])
                kb.memset(KG[kv][64:65, :], 1.0, [KG[kv].R()])
            kb.memset(kmax[:], 0.0, [kmax.R()])

            def rope_proj(cA, cB, dst, t0, t1_, n):
                pa, pbb = kb.bank(), kb.bank()
                for c0, pb in ((cA, pa), (cB, pbb)):
                    for kt in range(8):
                        kb.mm(ps[0:64, pb, :n], wst[:, kt, c0:c0 + 64], hb[:, kt, :n], kt == 0, kt == 7, WST + [hb.R()], [PR[pb]])
                kb.tt(t1[0:64, :n], ps[0:64, pa, :n], ropeC[0:64, t0:t1_], ALU.mult, [PR[pa], ropeC.R()], [t1.R()])
                kb.tt(t2[0:64, :n], ps[0:64, pbb, :n], ropeS[0:64, t0:t1_], ALU.mult, [PR[pbb], ropeS.R()], [t2.R()])
                kb.tt(dst[0:64, t0:t1_], t1[0:64, :n], t2[0:64, :n], ALU.add, [t1.R(), t2.R()], [dst.R()])

            def sumsq64(src, t0, t1_, n):
                kb.act(sqb[0:64, :n], src[0:64, t0:t1_], AF.Square, [src.R()], [sqb.R()])
                p2 = kb.bank()
                kb.mm(ps[:, p2, :n], ones[0:64, :], sqb[0:64, :n], True, True, [ones.R(), sqb.R()], [PR[p2]])
                return p2

            for bi, (t0, t1_) in enumerate(TBLK):
                n = t1_ - t0
                compute_h(1, t0, t1_, hb, hb.R(), W)
                for kv in range(2):
                    rope_proj(E_GKA - E_GQA + kv * 64, E_GKB - E_GQA + kv * 64, KG[kv], t0, t1_, n)
                    p2 = sumsq64(KG[kv], t0, t1_, n)
                    kb.emit("dve", lambda e: e.tensor_reduce(out=t1[:, 0:1], in_=ps[:, p2, :n], axis=AX.X, op=ALU.max), [PR[p2]], [t1.R()])
                    kb.tt(kmax[:, kv:kv + 1], kmax[:, kv:kv + 1], t1[:, 0:1], ALU.max, [kmax.R(), t1.R()], [kmax.R()])
                for cb in range(n // 128):
                    kbk = t0 // 128 + cb
                    pb = kb.bank()
                    for kt in range(8):
                        kb.mm(ps[:, pb, :128], hb[:, kt, cb * 128:(cb + 1) * 128], wst[:, kt, E_GV - E_GQA:E_GV - E_GQA + 128], kt == 0, kt == 7,
                              WST + [hb.R()], [PR[pb]])
                    kb.act(VG[:, kbk, :], ps[:, pb, :128], AF.Copy, [PR[pb]], [VG.R()])
            kb.act(kmax[:], kmax[:], AF.Sqrt, [kmax.R()], [kmax.R()])
            kb.ts(kmax[:], kmax[:], -1.0, None, ALU.mult, None, [kmax.R()], [kmax.R()])
            for bi, (t0, t1_) in enumerate(TBLK):
                n = t1_ - t0
                compute_h(1, t0, t1_, hb, hb.R(), W)
                for h in range(4):
                    rope_proj(h * 64, E_GQB - E_GQA + h * 64, QG[h], t0, t1_, n)
                    p2 = sumsq64(QG[h], t0, t1_, n)
                    kb.act(t1[64:65, :n], ps[64:65, p2, :n], AF.Sqrt, [PR[p2]], [t1.R()])
                    kb.ts(QG[h][64:65, t0:t1_], t1[64:65, :n], kmax[64:65, h // 2:h // 2 + 1], None, ALU.mult, None, [t1.R(), kmax.R()], [QG[h].R()])
            it = 0
            for h in range(4):
                kv = h // 2
                for qb in range(18):
                    q0 = qb * 128
                    if qb < 2:
                        band = []
                    else:
                        nb = qb - 2
                        band = [(2 + nb + d, d) for d in (-1, 0, 1) if 0 <= nb + d < 16]
                    pband, pctx, po = kb.bank(), kb.bank(), kb.bank()
                    p_ = pT[it % 2]
                    pk = psk[it % 2]
                    it += 1
                    for i, (kbk, d) in enumerate(band):
                        kb.mm(ps[:, pband, i * 128:(i + 1) * 128], KG[kv][0:66, kbk * 128:(kbk + 1) * 128], QG[h][0:66, q0:q0 + 128], True, d == 0,
                              [KG[kv].R(), QG[h].R()], [PR[pband]], sig=(d == 0))
                        if d != 0:
                            mi = 1 if d < 0 else 2
                            kb.mm(ps[:, pband, i * 128:(i + 1) * 128], cmat[:, mi, :], cmat[:, 0, :], False, True, [cmat.R()], [PR[pband]], sig=True)
                    for i in range(2):
                        kb.mm(ps[:, pctx, i * 128:(i + 1) * 128], KG[kv][0:66, i * 128:(i + 1) * 128], QG[h][0:66, q0:q0 + 128], True, True,
                              [KG[kv].R(), QG[h].R()], [PR[pctx]])
                    kb.mm(ps[0:1, pctx, 256:384], ksink[0:66, h:h + 1], QG[h][0:66, q0:q0 + 128], True, True, [ksink.R(), QG[h].R()], [PR[pctx]])
                    nb_ = len(band)
                    if nb_:
                        kb.act(p_[:, 0:nb_, :], ps[:, pband, 0:nb_ * 128].rearrange("p (a b) -> p a b", b=128), AF.Exp, [PR[pband]], [p_.R()], scale=scale)
                    kb.act(p_[:, 3:5, :], ps[:, pctx, 0:256].rearrange("p (a b) -> p a b", b=128), AF.Exp, [PR[pctx]], [p_.R()], scale=scale)
                    kb.act(pk[0:1, :], ps[0:1, pctx, 256:384], AF.Exp, [PR[pctx]], [pk.R()], scale=scale)
                    items = [(kbk, i) for i, (kbk, d) in enumerate(band)] + [(0, 3), (1, 4)]
                    for ii, (kbk, pi) in enumerate(items):
                        kb.mm(ps[0:64, po, :128], VG[:, kbk, kv * 64:(kv + 1) * 64], p_[:, pi, :], ii == 0, ii == len(items) - 1, [VG.R(), p_.R()], [PR[po]], sig=False)
                        kb.mm(ps[64:128, po, :128], ones[:, 0:64], p_[:, pi, :], ii == 0, False, [ones.R(), p_.R()], [PR[po]], sig=False)
                    kb.mm(ps[64:128, po, :128], ones[0:1, 0:64], pk[0:1, :], False, True, [ones.R(), pk.R()], [PR[po]], sig=True)
                    kb.recip(rd[0:64, :], ps[64:128, po, :128], [PR[po]], [rd.R()])
                    kb.tt(BR['t'][(h % 2) * 64:(h % 2 + 1) * 64, 1, h // 2, q0:q0 + 128], ps[0:64, po, :128], rd[0:64, :], ALU.mult, [PR[po], rd.R()],
                          [BR['t'].R((1, h // 2, qb))])

    def gmlp(l):
        win_src = I["w_in"][l].rearrange("(kt p) n -> p kt n", p=128)
        with kb.scope():
            W = hwork(512)
            hb = kb.sb([128, 8, 512], BF16, "hb")
            wst = kb.sb([128, 8, 512], BF16, "wz")
            wsT = kb.sb([128, 4, 128], BF16, "wsT")
            bsT = kb.sb([128, 2, 128], F32, "bsT")
            zu = kb.sb([128, 2, 512], BF16, "zu")
            G = {"g1": kb.sb([128, 512], F32, "g1"), "g2": kb.sb([128, 512], F32, "g2")}
            vg = kb.sb([128, 256], F32, "vg")
            xn = kb.sb([128, 256], BF16, "xn")
            st6 = kb.sb([128, 6], F32, "st6")
            mv = kb.sb([128, 4], F32, "mv")
            mx = kb.sb([128, 128], F32, "mx")
            for hh in range(2):
                load_w(wst, wst[:, hh * 4:(hh + 1) * 4, :], win_src[:, hh * 4:(hh + 1) * 4, E_Z:E_Z + 512], hh)
            load_w(wsT, wsT[:], I["wsT"][l])
            kb.dma("sp", bsT[:], I["bsT"][l], (), [bsT.R()])
            WST = [wst.R(0), wst.R(1)]
            for bi, (t0, t1_) in enumerate(TBLK):
                n = t1_ - t0
                compute_h(1, t0, t1_, hb, hb.R(), W)
                for ot in range(2):
                    pb = kb.bank()
                    for kt in range(8):
                        kb.mm(ps[:, pb, :n], wst[:, kt, ot * 128:(ot + 1) * 128], hb[:, kt, :n], kt == 0, kt == 7, WST + [hb.R()], [PR[pb]])
                    gelu(zu[:, ot, :n], ps[:, pb, :n], 128, (n,), G, [PR[pb]], [zu.R(ot)])
                for cb in range(n // 128):
                    c0 = t0 + cb * 128
                    pb = kb.bank()
                    for kt in range(8):
                        kb.mm(ps[:, pb, :256], hb[:, kt, cb * 128:(cb + 1) * 128], wst[:, kt, 256:512], kt == 0, kt == 7, WST + [hb.R()], [PR[pb]])
                    gelu(vg[:, :], ps[:, pb, :256], 128, (256,), G, [PR[pb]], [vg.R()])
                    kb.emit("dve", lambda e: e.bn_stats(out=st6[:, :], in_=vg[:, :]), [vg.R()], [st6.R()])
                    kb.emit("dve", lambda e: e.bn_aggr(out=mv[:, 0:2], in_=st6[:, :]), [st6.R()], [mv.R()])
                    kb.act(mv[:, 2:3], mv[:, 1:2], AF.Sqrt, [mv.R()], [mv.R()], bias=epsT[:, 1:2])
                    kb.recip(mv[:, 3:4], mv[:, 2:3], [mv.R()], [mv.R()])
                    kb.ts(xn[:, :], vg[:, :], mv[:, 0:1], mv[:, 3:4], ALU.subtract, ALU.mult, [vg.R(), mv.R()], [xn.R()])
                    for gp in range(2):
                        pm = kb.bank()
                        for gg in range(2):
                            g = gp * 2 + gg
                            kb.mm(ps[gg * 64:(gg + 1) * 64, pm, :128], xn[:, g * 64:(g + 1) * 64], wsT[:, g, :], True, True, [xn.R(), wsT.R()], [PR[pm]])
                        kb.stt(mx[:, :], ps[:, pm, :128], mvec[:, 3 + gp:4 + gp], bsT[:, gp, :], ALU.mult, ALU.add, [PR[pm], mvec.R(), bsT.R()], [mx.R()])
                        kb.tt(BR['t'][:, 3, gp, c0:c0 + 128], mx[:, :], zu[:, gp, cb * 128:(cb + 1) * 128], ALU.mult, [mx.R(), zu.R(gp)], [BR['t'].R((3, gp, c0 // 128))])

    def s5(l):
        win_src = I["w_in"][l].rearrange("(kt p) n -> p kt n", p=128)
        with kb.scope():
            wglu = kb.sb([128, 2, 512], BF16, "wglu")
            uT = kb.sb([128, 2, T], BF16, "uT")
            uR = kb.sb([128, 2, T], BF16, "uR")
            yacc = kb.sb([128, 2, T], BF16, "yacc")
            load_w(wglu, wglu[:], I["w_glu"][l].rearrange("(kt p) n -> p kt n", p=128))
            with kb.scope():
                W = hwork(512)
                hb = kb.sb([128, 8, 512], BF16, "hb")
                wst = kb.sb([128, 8, 256], BF16, "wu")
                for hh in range(2):
                    load_w(wst, wst[:, hh * 4:(hh + 1) * 4, :], win_src[:, hh * 4:(hh + 1) * 4, E_U:E_U + 256], hh)
                WST = [wst.R(0), wst.R(1)]
                for bi, (t0, t1_) in enumerate(TBLK):
                    n = t1_ - t0
                    compute_h(1, t0, t1_, hb, hb.R(), W)
                    for ot in range(2):
                        pb = kb.bank()
                        for kt in range(8):
                            kb.mm(ps[:, pb, :n], wst[:, kt, ot * 128:(ot + 1) * 128], hb[:, kt, :n], kt == 0, kt == 7, WST + [hb.R()], [PR[pb]])
                        kb.act(uT[:, ot, t0:t1_], ps[:, pb, :n], AF.Copy, [PR[pb]], [uT.R()])
            kb.cp(uR[:, :, 0:NCTX], uT[:, :, 0:NCTX][:, :, ::-1], [uT.R()], [uR.R()])
            kb.cp(uR[:, :, NCTX:T], uT[:, :, NCTX:T][:, :, ::-1], [uT.R()], [uR.R()])
            with kb.scope():
                pp = kb.sb([128, 8, 4], F32, "pp")
                tabs = kb.sb([128, 4, 8, 128], F32, "tabs")
                sc1 = kb.sb([128, 12, 8], F32, "sc1")
                bT = kb.sb([128, 2, 8, 128], BF16, "bT")
                cP = kb.sb([128, 2, 8, 128], BF16, "cP")
                kc = kb.sb([128, 8, 2], F32, "kc")
                kt_ = kb.sb([128, 4], F32, "kt_")
                PI = float(np.pi)

                def sin_of(out_ap, in_ap, shift, r, w, s_a, s_b):
                    kb.ts(s_a, in_ap, shift + PI, 1.0 / (2 * PI), ALU.add, ALU.mult, r, [sc1.R()])
                    kb.ts(s_b, s_a, -0.5, 12582912.0, ALU.add, ALU.add, [sc1.R()], [sc1.R()])
                    kb.ts(s_b, s_b, -12582912.0, None, ALU.add, None, [sc1.R()], [sc1.R()])
                    kb.tt(s_a, s_a, s_b, ALU.subtract, [sc1.R()], [sc1.R()])
                    kb.ts(s_a, s_a, 2 * PI, -PI, ALU.mult, ALU.add, [sc1.R()], [sc1.R()])
                    kb.ts(s_a, s_a, PI, -PI, ALU.min, ALU.max, [sc1.R()], [sc1.R()])
                    kb.act(out_ap, s_a, AF.Sin, [sc1.R()], w)

                for d_ in range(2):
                  usrc = uT if d_ == 0 else uR
                  with kb.scope():
                    bst = kb.sb([128, 2, 8, 128], F32, "bst")
                    bb = kb.sb([128, 2, 8, 128], F32, "bb")
                    tA = kb.sb([128, 8, 128], F32, "tA")
                    tB = kb.sb([128, 8, 128], F32, "tB")
                    kb.dma("sp", pp[:], I["s5p"][l, d_], (), [pp.R()])
                    kb.dma("sp", bst[:], I["s5b"][l, d_].rearrange("r n k c -> n r k c"), (), [bst.R()])
                    S = lambda i: sc1[:, i, :]
                    R1 = [sc1.R()]
                    lr, li, ldt = pp[:, :, 0], pp[:, :, 1], pp[:, :, 2]
                    kb.ts(S(0), lr, -1e-4, None, ALU.min, None, [pp.R()], R1)
                    kb.act(S(1), ldt, AF.Exp, [pp.R()], R1)
                    kb.tt(S(2), S(0), S(1), ALU.mult, R1, R1)
                    kb.act(S(2), S(2), AF.Exp, R1, R1)
                    kb.tt(S(3), li, S(1), ALU.mult, [pp.R()] + R1, R1)
                    sin_of(S(4), S(3), 0.0, R1, R1, S(10), S(11))
                    sin_of(S(5), S(3), PI / 2, R1, R1, S(10), S(11))
                    kb.tt(S(4), S(4), S(2), ALU.mult, R1, R1)
                    kb.tt(S(5), S(5), S(2), ALU.mult, R1, R1)
                    kb.ts(S(6), S(5), -1.0, None, ALU.add, None, R1, R1)
                    kb.tt(S(7), S(0), S(0), ALU.mult, R1, R1)
                    kb.tt(S(8), li, li, ALU.mult, [pp.R()], R1)
                    kb.tt(S(7), S(7), S(8), ALU.add, R1, R1)
                    kb.recip(S(7), S(7), R1, R1)
                    kb.tt(S(8), S(6), S(0), ALU.mult, R1, R1)
                    kb.tt(S(9), S(4), li, ALU.mult, R1 + [pp.R()], R1)
                    kb.tt(S(8), S(8), S(9), ALU.add, R1, R1)
                    kb.tt(S(8), S(8), S(7), ALU.mult, R1, R1)
                    kb.tt(S(9), S(4), S(0), ALU.mult, R1, R1)
                    kb.tt(S(6), S(6), li, ALU.mult, R1 + [pp.R()], R1)
                    kb.tt(S(9), S(9), S(6), ALU.subtract, R1, R1)
                    kb.tt(S(9), S(9), S(7), ALU.mult, R1, R1)
                    bc = lambda i: sc1[:, i, :].unsqueeze(2).broadcast_to([128, 8, 128])
                    kb.tt(tA[:], bst[:, 0], bc(8), ALU.mult, [bst.R()] + R1, [tA.R()])
                    kb.tt(tB[:], bst[:, 1], bc(9), ALU.mult, [bst.R()] + R1, [tB.R()])
                    kb.tt(bb[:, 0], tA[:], tB[:], ALU.subtract, [tA.R(), tB.R()], [bb.R()])
                    kb.tt(tA[:], bst[:, 1], bc(8), ALU.mult, [bst.R()] + R1, [tA.R()])
                    kb.tt(tB[:], bst[:, 0], bc(9), ALU.mult, [bst.R()] + R1, [tB.R()])
                    kb.tt(bb[:, 1], tA[:], tB[:], ALU.add, [tA.R(), tB.R()], [bb.R()])
                    for ri in range(2):
                        for k in range(8):
                            pb = kb.bank()
                            kb.emit("pe", lambda e: e.transpose(ps[:, pb, 0:128], bb[:, ri, k, :], ident), [bb.R(), cmat_f.R()], [PR[pb]])
                            kb.act(bT[:, ri, k, :], ps[:, pb, 0:128], AF.Copy, [PR[pb]], [bT.R()])
                    kb.dma("pool", cP[:], I["s5c"][l, d_].rearrange("r n k c -> n r k c"), (), [cP.R()], weight=True)
                    kb.tt(S(6), S(2), S(2), ALU.mult, R1, R1)
                    kb.recip(S(6), S(6), R1, R1)
                    kb.tt(S(7), S(5), S(6), ALU.mult, R1, R1)
                    kb.tt(S(6), S(4), S(6), ALU.mult, R1, R1)
                    kb.ts(S(6), S(6), -1.0, None, ALU.mult, None, R1, R1)
                    TR = [tabs.R()]
                    for (tr, ti, pr0, pi0) in ((0, 1, 5, 4), (2, 3, 7, 6)):
                        kb.memset(tabs[:, tr, :, 0:1], 1.0, TR)
                        kb.memset(tabs[:, ti, :, 0:1], 0.0, TR)
                        kb.cp(S(10), S(pr0), R1, R1)
                        kb.cp(S(11), S(pi0), R1, R1)
                        m = 1
                        while m < 256:
                            mm_ = min(m, 128) if m < 128 else 1
                            if m < 128:
                                pr = sc1[:, 10, :].unsqueeze(2).broadcast_to([128, 8, m])
                                pi_ = sc1[:, 11, :].unsqueeze(2).broadcast_to([128, 8, m])
                                src_r, src_i = tabs[:, tr, :, 0:m], tabs[:, ti, :, 0:m]
                                kb.tt(tA[:, :, 0:m], src_r, pr, ALU.mult, TR + R1, [tA.R()])
                                kb.tt(tB[:, :, 0:m], src_i, pi_, ALU.mult, TR + R1, [tB.R()])
                                kb.tt(tabs[:, tr, :, m:2 * m], tA[:, :, 0:m], tB[:, :, 0:m], ALU.subtract, [tA.R(), tB.R()], TR)
                                kb.tt(tA[:, :, 0:m], src_r, pi_, ALU.mult, TR + R1, [tA.R()])
                                kb.tt(tB[:, :, 0:m], src_i, pr, ALU.mult, TR + R1, [tB.R()])
                                kb.tt(tabs[:, ti, :, m:2 * m], tA[:, :, 0:m], tB[:, :, 0:m], ALU.add, [tA.R(), tB.R()], TR)
                            if m < 128:
                                kb.tt(tA[:, :, 0], S(10), S(10), ALU.mult, R1, [tA.R()])
                                kb.tt(tB[:, :, 0], S(11), S(11), ALU.mult, R1, [tB.R()])
                                kb.tt(S(11), S(10), S(11), ALU.mult, R1, R1)
                                kb.ts(S(11), S(11), 2.0, None, ALU.mult, None, R1, R1)
                                kb.tt(S(10), tA[:, :, 0], tB[:, :, 0], ALU.subtract, [tA.R(), tB.R()], R1)
                            m *= 2
                        if tr == 0:
                            kb.cp(sc1[:, 0, :], S(10), R1, R1)
                            kb.cp(sc1[:, 1, :], S(11), R1, R1)
                  with kb.scope():
                    Wt = kb.sb([128, 2, 512], F32, "Wt")
                    Zt = kb.sb([128, 2, 512], F32, "Zt")
                    Ss = [kb.sb([128, 2, 512], BF16, "Ss") for _ in range(2)]
                    w4 = [kb.sb([128, 512], F32, "w4") for _ in range(4)]
                    S = lambda i: sc1[:, i, :]
                    R1 = [sc1.R()]
                    TR = [tabs.R()]
                    kb.memset(kc[:], 0.0, [kc.R()])
                    si_ = 0
                    for bi, (t0, t1_) in enumerate(TBLK):
                        n = t1_ - t0
                        nch = n // 128
                        pY = [kb.bank(), kb.bank()]
                        for k in range(8):
                            o = k // 4
                            pr_, pi_ = kb.bank(), kb.bank()
                            while pr_ in pY:
                                pr_ = kb.bank()
                            while pi_ in pY or pi_ == pr_:
                                pi_ = kb.bank()
                            kb.mm(ps[:, pr_, :n], bT[:, 0, k, :], usrc[:, o, t0:t1_], True, True, [bT.R(), usrc.R()], [PR[pr_]])
                            kb.mm(ps[:, pi_, :n], bT[:, 1, k, :], usrc[:, o, t0:t1_], True, True, [bT.R(), usrc.R()], [PR[pi_]])
                            tb = lambda i: tabs[:, i, k:k + 1, :].broadcast_to([128, nch, 128])
                            v3 = lambda ap: ap.rearrange("p (c j) -> p c j", j=128)
                            P_r, P_i = v3(ps[:, pr_, :n]), v3(ps[:, pi_, :n])
                            a0, a1, a2, a3 = [v3(w4[i][:, :n]) for i in range(4)]
                            kb.tt(a0, P_r, tb(2), ALU.mult, [PR[pr_]] + TR, [w4[0].R()])
                            kb.tt(a1, P_i, tb(3), ALU.mult, [PR[pi_]] + TR, [w4[1].R()])
                            kb.tt(a2, P_i, tb(2), ALU.mult, [PR[pi_]] + TR, [w4[2].R()])
                            kb.tt(a3, P_r, tb(3), ALU.mult, [PR[pr_]] + TR, [w4[3].R()])
                            kb.tt(Wt[:, 0, :n], w4[0][:, :n], w4[1][:, :n], ALU.subtract, [w4[0].R(), w4[1].R()], [Wt.R()])
                            kb.tt(Wt[:, 1, :n], w4[2][:, :n], w4[3][:, :n], ALU.add, [w4[2].R(), w4[3].R()], [Wt.R()])
                            for c in range(nch):
                                cs = slice(c * 128, (c + 1) * 128)
                                for ri in range(2):
                                    kb.emit("dve", lambda e: e.tensor_tensor_scan(out=Zt[:, ri, cs], data0=onesf[:, 0:128], data1=Wt[:, ri, cs],
                                                                                  initial=kc[:, k, ri:ri + 1], op0=ALU.mult, op1=ALU.add),
                                            [onesf.R(), Wt.R(), kc.R()], [Zt.R()])
                                e0 = c * 128 + 127
                                kb.ts(kt_[:, 0:1], Zt[:, 0, e0:e0 + 1], sc1[:, 0, k:k + 1], None, ALU.mult, None, [Zt.R()] + R1, [kt_.R()])
                                kb.ts(kt_[:, 1:2], Zt[:, 1, e0:e0 + 1], sc1[:, 1, k:k + 1], None, ALU.mult, None, [Zt.R()] + R1, [kt_.R()])
                                kb.ts(kt_[:, 2:3], Zt[:, 1, e0:e0 + 1], sc1[:, 0, k:k + 1], None, ALU.mult, None, [Zt.R()] + R1, [kt_.R()])
                                kb.ts(kt_[:, 3:4], Zt[:, 0, e0:e0 + 1], sc1[:, 1, k:k + 1], None, ALU.mult, None, [Zt.R()] + R1, [kt_.R()])
                                kb.tt(kc[:, k, 0:1], kt_[:, 0:1], kt_[:, 1:2], ALU.subtract, [kt_.R()], [kc.R()])
                                kb.tt(kc[:, k, 1:2], kt_[:, 2:3], kt_[:, 3:4], ALU.add, [kt_.R()], [kc.R()])
                            Sb = Ss[si_ % 2]
                            si_ += 1
                            Z_r, Z_i = v3(Zt[:, 0, :n]), v3(Zt[:, 1, :n])
                            kb.tt(a0, Z_r, tb(0), ALU.mult, [Zt.R()] + TR, [w4[0].R()])
                            kb.tt(a1, Z_i, tb(1), ALU.mult, [Zt.R()] + TR, [w4[1].R()])
                            kb.tt(a2, Z_i, tb(0), ALU.mult, [Zt.R()] + TR, [w4[2].R()])
                            kb.tt(a3, Z_r, tb(1), ALU.mult, [Zt.R()] + TR, [w4[3].R()])
                            kb.tt(Sb[:, 0, :n], w4[0][:, :n], w4[1][:, :n], ALU.subtract, [w4[0].R(), w4[1].R()], [Sb.R()])
                            kb.stt(Sb[:, 1, :n], w4[2][:, :n], -1.0, w4[3][:, :n], ALU.mult, ALU.subtract, [w4[2].R(), w4[3].R()], [Sb.R()])
                            first, last = (k % 4 == 0), (k % 4 == 3)
                            kb.mm(ps[:, pY[o], :n], cP[:, 0, k, :], Sb[:, 0, :n], first, False, [cP.R(), Sb.R()], [PR[pY[o]]], sig=False)
                            kb.mm(ps[:, pY[o], :n], cP[:, 1, k, :], Sb[:, 1, :n], False, last, [cP.R(), Sb.R()], [PR[pY[o]]], sig=True)
                        for o in range(2):
                            if d_ == 0:
                                kb.stt(yacc[:, o, t0:t1_], uT[:, o, t0:t1_], mvec[:, 5 + o:6 + o], ps[:, pY[o], :n], ALU.mult, ALU.add,
                                       [uT.R(), mvec.R(), PR[pY[o]]], [yacc.R()])
                            else:
                                if bi == 0:
                                    dst = yacc[:, o, 0:NCTX][:, ::-1]
                                else:
                                    a_, b_ = t0 - NCTX, t1_ - NCTX
                                    lo_ = NCTX + (NLAT - b_)
                                    hi_ = NCTX + (NLAT - a_)
                                    dst = yacc[:, o, lo_:hi_][:, ::-1]
                                kb.tt(dst, dst, ps[:, pY[o], :n], ALU.add, [yacc.R(), PR[pY[o]]], [yacc.R()])
            with kb.scope():
                G = {"g1": kb.sb([128, 512], F32, "g1"), "g2": kb.sb([128, 512], F32, "g2")}
                yg = kb.sb([128, 2, 512], BF16, "yg")
                sg = kb.sb([128, 512], F32, "sg")
                for bi, (t0, t1_) in enumerate(TBLK):
                    n = t1_ - t0
                    for o in range(2):
                        gelu(yg[:, o, :n], yacc[:, o, t0:t1_], 128, (n,), G, [yacc.R()], [yg.R()])
                    for ot in range(2):
                        pa, pg = kb.bank(), kb.bank()
                        for (pb, cc) in ((pa, ot * 128), (pg, 256 + ot * 128)):
                            for k2 in range(2):
                                kb.mm(ps[:, pb, :n], wglu[:, k2, cc:cc + 128], yg[:, k2, :n], k2 == 0, k2 == 1, [wglu.R(), yg.R()], [PR[pb]])
                        kb.act(sg[:, :n], ps[:, pg, :n], AF.Sigmoid, [PR[pg], mvec.R()], [sg.R()], bias=mvec[:, 14 + ot:15 + ot])
                        kb.stt(BR['t'][:, 2, ot, t0:t1_], ps[:, pa, :n], mvec[:, 12 + ot:13 + ot], sg[:, :n], ALU.add, ALU.mult,
                               [PR[pa], mvec.R(), sg.R()], [BR['t'].R((2, ot, bi))])

    def merge(l):
        win_src = I["w_in"][l].rearrange("(kt p) n -> p kt n", p=128)
        wo_src = I["w_out"][l].rearrange("(kt p) n -> p kt n", p=128)
        br = BR['t']
        with kb.scope():
            W = hwork(512)
            hb = kb.sb([128, 8, 512], BF16, "hb")
            wg = [kb.sb([128, 8, 512], BF16, "wg") for _ in range(2)]
            wb = [kb.sb([128, 2, 1024], BF16, "wb") for _ in range(2)]
            mg = kb.sb([128, 8, 512], F32, "mg")
            mgb = kb.sb([128, 8, 512], BF16, "mgb")
            sg = [kb.sb([128, 512], F32, "sg") for _ in range(2)]
            tq = kb.sb([128, 512], F32, "tq")
            sqy = [kb.sb([128, 512], BF16, "sqy") for _ in range(2)]
            BRALL = [r for r in br._r.values()]
            gi_ = 0
            si_ = 0
            for bi, (t0, t1_) in enumerate(TBLK):
                n = t1_ - t0
                kb.banks_rot = list(range(7))
                compute_h(1, t0, t1_, hb, hb.R(), W)
                for i in range(4):
                    wbi = wb[i % 2]
                    load_w(wbi, wbi[:], I["w_branch"][l, i].rearrange("(kt p) n -> p kt n", p=128))
                    for half in range(2):
                        w = wg[gi_ % 2]
                        gi_ += 1
                        c0 = E_GATE + i * 1024 + half * 512
                        for hh in range(2):
                            load_w(w, w[:, hh * 4:(hh + 1) * 4, :], win_src[:, hh * 4:(hh + 1) * 4, c0:c0 + 512], hh)
                        for d4 in range(4):
                            dt = half * 4 + d4
                            pgt, ppj = kb.bank(), kb.bank()
                            for kt in range(8):
                                kb.mm(ps[:, pgt, :n], w[:, kt, d4 * 128:(d4 + 1) * 128], hb[:, kt, :n], kt == 0, kt == 7, [w.R(0), w.R(1), hb.R()], [PR[pgt]])
                            for k2 in range(2):
                                kb.mm(ps[:, ppj, :n], wbi[:, k2, dt * 128:(dt + 1) * 128], br[:, i, k2, t0:t1_], k2 == 0, k2 == 1, [wbi.R()] + BRALL, [PR[ppj]])
                            s_ = sg[si_ % 2]
                            si_ += 1
                            kb.act(s_[:, :n], ps[:, pgt, :n], AF.Sigmoid, [PR[pgt]], [s_.R()])
                            if i == 0:
                                kb.tt(mg[:, dt, :n], s_[:, :n], ps[:, ppj, :n], ALU.mult, [s_.R(), PR[ppj]], [mg.R(dt)])
                            else:
                                kb.tt(tq[:, :n], s_[:, :n], ps[:, ppj, :n], ALU.mult, [s_.R(), PR[ppj]], [tq.R()])
                                kb.tt(mg[:, dt, :n], mg[:, dt, :n], tq[:, :n], ALU.add, [mg.R(dt), tq.R()], [mg.R(dt)])
                for dt in range(8):
                    kb.act(mgb[:, dt, :n], mg[:, dt, :n], AF.Copy, [mg.R(dt)], [mgb.R()])
                ssb = 7
                for half in range(2):
                    w = wg[gi_ % 2]
                    gi_ += 1
                    for hh in range(2):
                        load_w(w, w[:, hh * 4:(hh + 1) * 4, :], wo_src[:, hh * 4:(hh + 1) * 4, half * 512:(half + 1) * 512], hh)
                    for d4 in range(4):
                        dt = half * 4 + d4
                        pb = kb.bank()
                        for kt in range(8):
                            kb.mm(ps[:, pb, :n], w[:, kt, d4 * 128:(d4 + 1) * 128], mgb[:, kt, :n], kt == 0, kt == 7, [w.R(0), w.R(1), mgb.R()], [PR[pb]])
                        kb.act(mg[:, dt, :n], ps[:, pb, :n], AF.Copy, [PR[pb]], [mg.R(dt)])
                        sq_ = sqy[si_ % 2]
                        si_ += 1
                        kb.act(sq_[:, :n], ps[:, pb, :n], AF.Square, [PR[pb]], [sq_.R()])
                        kb.mm(ps[:, ssb, :n], ones[:, :], sq_[:, :n], dt == 0, dt == 7, [ones.R(), sq_.R()], [PR[ssb]])
                post_norm_residual(1, t0, n, mg, ssb, W, ykeys=list(range(8)))
            kb.banks_rot = list(range(8))

    for l in range(depth_run):
        if l > 0:
            kb.barrier()
            kb.rotate()
        ada(l)
        ffn(l, 0, 0)
        with kb.scope():
            BR['t'] = kb.sb([128, 4, 2, T], BF16, "br")
            mla(l)
            gqa(l)
            s5(l)
            gmlp(l)
            merge(l)
        ffn(l, 1, 2)

    kb.barrier()
    for c in range(8):
        t0 = NCTX + c * 256
        kb.dma("sp", outT[:, :, c * 256:(c + 1) * 256], xT[:, :, t0:t0 + 256], xres(t0, t0 + 256), ())
    for ch in kb.sch:
        kb.E["sp"].obj.wait_ge(ch.sem, ch.total)
    return nc, es


def _rope_tables():
    def tab(rot_dim):
        axis_dim = rot_dim // 2
        half = axis_dim // 2
        inv = 10000.0 ** (-np.arange(0, axis_dim, 2, dtype=np.float32) / axis_dim)
        t = np.arange(NLAT)
        row = (t // 64).astype(np.float32)
        col = (t % 64).astype(np.float32)
        C = np.ones((rot_dim, T), np.float32)
        S = np.zeros((rot_dim, T), np.float32)
        partner = np.zeros(rot_dim, np.int64)
        for ax, pos in ((0, row), (1, col)):
            for r in range(axis_dim):
                k = r % half
                ang = pos * inv[k]
                C[ax * axis_dim + r, NCTX:] = np.cos(ang)
                if r < half:
                    S[ax * axis_dim + r, NCTX:] = -np.sin(ang)
                    partner[ax * axis_dim + r] = ax * axis_dim + r + half
                else:
                    S[ax * axis_dim + r, NCTX:] = np.sin(ang)
                    partner[ax * axis_dim + r] = ax * axis_dim + r - half
        return C, S, partner
    Cg, Sg, pg = tab(64)
    Cm, Sm, pm = tab(32)
    ropeC = np.zeros((128, T), np.float32)
    ropeS = np.zeros((128, T), np.float32)
    ropeC[0:64] = Cg
    ropeS[0:64] = Sg
    ropeC[64:96] = Cm
    ropeS[64:96] = Sm
    ropeS[96:128] = Sm
    return ropeC, ropeS, pg, pm


_CACHE = {}


def _prep(inp):
    f = lambda k: np.asarray(inp[k], np.float32)
    ropeC, ropeS, pg, pm = _rope_tables()
    P = {}
    P["ropeC"], P["ropeS"] = ropeC, ropeS
    cm = np.zeros((128, 3, 128), np.float32)
    cm[:, 0, :] = np.eye(128)
    qi = np.arange(128)[:, None]
    kj = np.arange(128)[None, :]
    cm[:, 1, :] = np.where(qi <= kj, 0.0, -30000.0)
    cm[:, 2, :] = np.where(kj <= qi, 0.0, -30000.0)
    P["cmat"] = cm
    P["w_ada"] = f("w_ada")
    vecs = np.zeros((DEPTH, 128, 120), np.float32)
    vecs[:, :, 0:72] = f("b_ada").reshape(DEPTH, 72, 128).transpose(0, 2, 1)
    vecs[:, :, 72:96] = f("norm_pre").reshape(DEPTH, 24, 128).transpose(0, 2, 1)
    vecs[:, :, 96:120] = f("norm_post").reshape(DEPTH, 24, 128).transpose(0, 2, 1)
    P["vecs"] = vecs
    P["w_ffn_in"] = f("w_ffn_in")
    P["w_ffn_out"] = f("w_ffn_out")
    w_in = f("w_in")
    idx = np.zeros(NEXT, np.int64)
    idx[E_KVL:E_KVL + 128] = np.arange(0, 128)
    idx[E_QL:E_QL + 256] = np.arange(672, 928)
    idx[E_KPA:E_KPA + 32] = 128 + np.arange(32)
    idx[E_KPB:E_KPB + 32] = 128 + pm
    gq = 928 + np.arange(256)
    gqs = 928 + (np.arange(4)[:, None] * 64 + pg[None, :]).reshape(-1)
    gk = 160 + np.arange(128)
    gks = 160 + (np.arange(2)[:, None] * 64 + pg[None, :]).reshape(-1)
    idx[E_GQA:E_GQA + 256] = gq
    idx[E_GQB:E_GQB + 256] = gqs
    idx[E_GKA:E_GKA + 128] = gk
    idx[E_GKB:E_GKB + 128] = gks
    idx[E_GV:E_GV + 128] = 288 + np.arange(128)
    idx[E_U:E_U + 256] = 416 + np.arange(256)
    idx[E_Z:E_Z + 512] = 1184 + np.arange(512)
    idx[E_GATE:E_GATE + 4096] = 1696 + np.arange(4096)
    P["w_in"] = np.ascontiguousarray(w_in[:, :, idx])
    wuq = f("mla_w_uq").reshape(DEPTH, 256, 4, 96)
    wuq_e = np.concatenate([wuq[..., 0:64], wuq[..., 64:96], wuq[..., 64 + pm]], axis=-1)
    P["w_uq"] = np.ascontiguousarray(wuq_e.reshape(DEPTH, 256, 512))
    wukv = f("mla_w_ukv").reshape(DEPTH, 128, 4, 128)
    P["w_ukv"] = np.ascontiguousarray(np.concatenate([wukv[..., 0:64].reshape(DEPTH, 128, 256), wukv[..., 64:128].reshape(DEPTH, 128, 256)], axis=-1))
    mvec = np.zeros((DEPTH, 128, 16), np.float32)
    mvec[:, :, 0] = f("mla_kv_norm")
    mvec[:, :, 1:3] = f("mla_q_norm").reshape(DEPTH, 2, 128).transpose(0, 2, 1)
    mvec[:, :, 3:5] = f("gmlp_norm").reshape(DEPTH, 2, 128).transpose(0, 2, 1)
    mvec[:, :, 5:7] = f("s5_d").reshape(DEPTH, 2, 128).transpose(0, 2, 1)
    mvec[:, 65, 8:12] = f("gqa_sink")
    mvec[:, :, 12:16] = f("s5_b_glu").reshape(DEPTH, 4, 128).transpose(0, 2, 1)
    P["mvec"] = mvec
    P["w_branch"] = f("w_branch")
    P["w_out"] = f("w_out")
    P["wsT"] = np.ascontiguousarray(f("gmlp_w_s").transpose(0, 3, 1, 2))
    bs = f("gmlp_b_s")
    P["bsT"] = np.ascontiguousarray(np.repeat(bs.reshape(DEPTH, 2, 2, 1, 128), 64, axis=3).reshape(DEPTH, 2, 128, 128).transpose(0, 2, 1, 3))
    def st(a):
        return a.reshape(DEPTH, 2, 8, 2, 64).transpose(0, 1, 3, 4, 2).reshape(DEPTH, 2, 128, 8)
    s5p = np.zeros((DEPTH, 2, 128, 8, 4), np.float32)
    s5p[..., 0] = st(f("s5_lam_re"))
    s5p[..., 1] = st(f("s5_lam_im"))
    s5p[..., 2] = st(np.repeat(f("s5_log_dt")[..., None], 64, axis=-1))
    P["s5p"] = s5p
    s5b = np.zeros((DEPTH, 2, 2, 128, 8, 128), np.float32)
    s5c = np.zeros((DEPTH, 2, 2, 128, 8, 128), np.float32)
    bre, bim = f("s5_b_re"), f("s5_b_im")
    cre, cim = f("s5_c_re"), f("s5_c_im")
    for g in range(16):
        k, half = g // 2, g % 2
        cs = (g % 8) * 16
        s5b[:, :, 0, half * 64:(half + 1) * 64, k, cs:cs + 16] = bre[:, :, g]
        s5b[:, :, 1, half * 64:(half + 1) * 64, k, cs:cs + 16] = bim[:, :, g]
        s5c[:, :, 0, half * 64:(half + 1) * 64, k, cs:cs + 16] = cre[:, :, g].transpose(0, 1, 3, 2)
        s5c[:, :, 1, half * 64:(half + 1) * 64, k, cs:cs + 16] = cim[:, :, g].transpose(0, 1, 3, 2)
    P["s5b"], P["s5c"] = s5b, s5c
    P["w_glu"] = f("s5_w_glu")
    return P


def _core_inputs(inp, P, b):
    x = np.asarray(inp["x"], np.float32)[b]
    ctx = np.asarray(inp["ctx"], np.float32)[b]
    full = np.concatenate([ctx, x], axis=0)
    xT = np.ascontiguousarray(full.T.reshape(8, 128, T).transpose(1, 0, 2))
    cv = np.stack([np.asarray(inp["c"], np.float32)[b], np.asarray(inp["c_ctx"], np.float32)], axis=-1)
    cT = np.ascontiguousarray(cv.reshape(8, 128, 2).transpose(1, 0, 2))
    m = dict(P)
    m["xT"] = xT
    m["cT"] = cT
    return m


def kernel(**inputs):
    P = _prep(inputs)
    nc, es = build(DEPTH)
    in_maps = [_core_inputs(inputs, P, b) for b in range(8)]
    res = run_bass_kernel_spmd(nc, in_maps, core_ids=list(range(8)))
    out = np.zeros((8, NLAT, D), np.float32)
    for b in range(8):
        oT = np.asarray(res.results[b]["outT"])
        out[b] = oT.transpose(2, 1, 0).reshape(NLAT, D)
    return out
```

```python
import numpy as np
import ml_dtypes
from contextlib import ExitStack, contextmanager
import concourse.bass as bass
import concourse.mybir as mybir
from concourse.bass_utils import run_bass_kernel_spmd

F32 = mybir.dt.float32
BF16 = mybir.dt.bfloat16
ALU = mybir.AluOpType
AF = mybir.ActivationFunctionType
AX = mybir.AxisListType

D = 1024
T = 2304
NCTX = 256
NLAT = 2048
DFF = 2816
DEPTH = 4
EPS = 1e-6
SAME_ENG_SYNC = True

E_KVL, E_QL, E_KPA, E_KPB = 0, 128, 384, 416
E_GQA, E_GQB, E_GKA, E_GKB, E_GV, E_U, E_Z, E_GATE = 512, 768, 1024, 1152, 1280, 1408, 1664, 2176
NEXT = 6272
TBLK = [(0, 256), (256, 768), (768, 1280), (1280, 1792), (1792, 2304)]
GC = 1.5957691216057308


class Res:
    __slots__ = ("w", "r")

    def __init__(self):
        self.w = None
        self.r = {}


class Eng:
    def __init__(self, obj, pe=False):
        self.obj = obj
        self.pe = pe
        self.sem = None
        self.cnt = 0
        self.known = {}
        self.pend = []


class Chan:
    def __init__(self, sem):
        self.sem = sem
        self.total = 0


class Tl:
    def __init__(self, h):
        self.h = h
        self._r = {}

    def __getitem__(self, k):
        return self.h[k]

    def R(self, key=0):
        r = self._r.get(key)
        if r is None:
            r = self._r[key] = Res()
        return r


class KB:
    def __init__(self, nc, es):
        self.nc = nc
        self.es = es
        self.stack = [es]
        self.E = {"pe": Eng(nc.tensor, True), "act": Eng(nc.scalar), "dve": Eng(nc.vector),
                  "pool": Eng(nc.gpsimd), "sp": Eng(nc.sync)}
        self.nsem = 0
        self.nname = 0
        self.rotate()
        self.wch = [Chan(self.newsem()) for _ in range(6)]
        self.sch = [Chan(self.newsem()) for _ in range(4)]
        self.wi = 0
        self.si = 0
        self.ps = Tl(es.enter_context(nc.psum_tensor("ps", [128, 8, 512], F32)))
        self.bank_i = 0
        self.banks_rot = list(range(8))

    def newsem(self):
        self.nsem += 1
        return self.es.enter_context(self.nc.semaphore(f"sem{self.nsem}"))

    def rotate(self):
        for e in self.E.values():
            assert not e.pend
            e.sem = self.newsem()
            e.cnt = 0

    def sb(self, shape, dt, name=None):
        self.nname += 1
        return Tl(self.stack[-1].enter_context(self.nc.sbuf_tensor(f"{name or 't'}_{self.nname}", list(shape), dt)))

    @contextmanager
    def scope(self):
        st = ExitStack()
        self.stack.append(st)
        try:
            yield
        finally:
            self.barrier()
            self.stack.pop()
            st.close()

    def barrier(self):
        chans = self.wch + self.sch
        for e in self.E.values():
            assert not e.pend
        for e in self.E.values():
            for f in self.E.values():
                if f is e or f.cnt == 0:
                    continue
                if e.known.get(id(f.sem), 0) < f.cnt:
                    e.obj.wait_ge(f.sem, f.cnt)
                    e.known[id(f.sem)] = f.cnt
            for c in chans:
                if c.total and e.known.get(id(c.sem), 0) < c.total:
                    e.obj.wait_ge(c.sem, c.total)
                    e.known[id(c.sem)] = c.total

    def bank(self):
        b = self.banks_rot[self.bank_i % len(self.banks_rot)]
        self.bank_i += 1
        return b

    def _waits(self, eng, reads, writes):
        deps = {}

        def add(sv):
            sem, val = sv
            k = id(sem)
            if k not in deps or deps[k][1] < val:
                deps[k] = (sem, val)

        for r in reads:
            if r.w is not None:
                add(r.w)
        for w in writes:
            if w.w is not None and w.w[0] is not eng.sem:
                add(w.w)
            for sv in w.r.values():
                if sv[0] is not eng.sem:
                    add(sv)
        for k, (sem, val) in deps.items():
            if sem is eng.sem and (eng.pe or not SAME_ENG_SYNC):
                continue
            if eng.known.get(k, 0) >= val:
                continue
            eng.obj.wait_ge(sem, val)
            eng.known[k] = val

    def emit(self, en, fn, reads=(), writes=(), sig=True):
        eng = self.E[en]
        self._waits(eng, reads, writes)
        ins = fn(eng.obj)
        eng.pend.append((reads, writes))
        if sig:
            eng.cnt += 1
            ins.then_inc(eng.sem, 1)
            st = (eng.sem, eng.cnt)
            for rs, ws in eng.pend:
                for w in ws:
                    w.w = st
                    w.r = {}
                for r in rs:
                    r.r[id(eng.sem)] = st
            eng.pend = []
        return ins

    def dma(self, q, out, in_, reads=(), writes=(), weight=False):
        eng = self.E[q]
        if weight:
            ch = self.wch[self.wi % len(self.wch)]
            self.wi += 1
        else:
            ch = self.sch[self.si % len(self.sch)]
            self.si += 1
        self._waits(eng, reads, writes)
        if ch.total and eng.known.get(id(ch.sem), 0) < ch.total:
            eng.obj.wait_ge(ch.sem, ch.total)
            eng.known[id(ch.sem)] = ch.total
        ins = eng.obj.dma_start(out=out, in_=in_)
        ins.then_inc(ch.sem, 16)
        ch.total += 16
        st = (ch.sem, ch.total)
        for w in writes:
            w.w = st
            w.r = {}
        for r in reads:
            r.r[id(ch.sem)] = st

    def mm(self, out, lhsT, rhs, start, stop, r, w, sig=None):
        if sig is None:
            sig = stop
        self.emit("pe", lambda e: e.matmul(out, lhsT, rhs, start=start, stop=stop), r, w, sig)

    def act(self, out, in_, func, r, w, bias=None, scale=1.0):
        kw = {}
        if bias is not None:
            kw["bias"] = bias
        self.emit("act", lambda e: e.activation(out=out, in_=in_, func=func, scale=scale, **kw), r, w)

    def tt(self, out, in0, in1, op, r, w, en="dve"):
        self.emit(en, lambda e: e.tensor_tensor(out=out, in0=in0, in1=in1, op=op), r, w)

    def ts(self, out, in0, s1, s2, op0, op1, r, w, en="dve"):
        if s2 is None:
            self.emit(en, lambda e: e.tensor_scalar(out=out, in0=in0, scalar1=s1, scalar2=None, op0=op0), r, w)
        else:
            self.emit(en, lambda e: e.tensor_scalar(out=out, in0=in0, scalar1=s1, scalar2=s2, op0=op0, op1=op1), r, w)

    def stt(self, out, in0, sc, in1, op0, op1, r, w):
        self.emit("dve", lambda e: e.scalar_tensor_tensor(out=out, in0=in0, scalar=sc, in1=in1, op0=op0, op1=op1), r, w)

    def cp(self, out, in_, r, w, en="dve"):
        self.emit(en, lambda e: e.tensor_copy(out=out, in_=in_), r, w)

    def recip(self, out, in_, r, w):
        self.emit("dve", lambda e: e.reciprocal(out=out, in_=in_), r, w)

    def memset(self, ap, val, w, en="dve"):
        self.emit(en, lambda e: e.memset(ap, val), (), w)


def split_tok(t0, t1):
    out = []
    if t0 < NCTX:
        out.append((t0, min(t1, NCTX), 1))
    if t1 > NCTX:
        out.append((max(t0, NCTX), t1, 0))
    return out


def build(depth_run, dbg=None):
    nc = bass.Bass("TRN2", target_bir_lowering=False)
    es = ExitStack()
    dr = lambda name, shape, dt=F32: nc.dram_tensor(name, list(shape), dt, kind="ExternalInput").ap()
    I = {}
    I["xT"] = dr("xT", [128, 8, T])
    I["cT"] = dr("cT", [128, 8, 2])
    I["w_ada"] = dr("w_ada", [DEPTH, D, 9 * D])
    I["vecs"] = dr("vecs", [DEPTH, 128, 120])
    I["w_ffn_in"] = dr("w_ffn_in", [DEPTH, 2, D, 2 * DFF])
    I["w_ffn_out"] = dr("w_ffn_out", [DEPTH, 2, DFF, D])
    I["w_in"] = dr("w_in", [DEPTH, D, NEXT])
    I["w_uq"] = dr("w_uq", [DEPTH, 256, 512])
    I["w_ukv"] = dr("w_ukv", [DEPTH, 128, 512])
    I["mvec"] = dr("mvec", [DEPTH, 128, 16])
    I["ropeC"] = dr("ropeC", [128, T])
    I["ropeS"] = dr("ropeS", [128, T])
    I["cmat"] = dr("cmat", [128, 3, 128])
    I["w_branch"] = dr("w_branch", [DEPTH, 4, 256, D])
    I["w_out"] = dr("w_out", [DEPTH, D, D])
    I["wsT"] = dr("wsT", [DEPTH, 128, 4, 128])
    I["bsT"] = dr("bsT", [DEPTH, 128, 2, 128])
    I["s5p"] = dr("s5p", [DEPTH, 2, 128, 8, 4])
    I["s5b"] = dr("s5b", [DEPTH, 2, 2, 128, 8, 128])
    I["s5c"] = dr("s5c", [DEPTH, 2, 2, 128, 8, 128])
    I["w_glu"] = dr("w_glu", [DEPTH, 256, 512])
    outT = nc.dram_tensor("outT", [128, 8, NLAT], F32, kind="ExternalOutput").ap()
    dbg_aps = {}
    if dbg:
        for name, shape in dbg.items():
            dbg_aps[name] = nc.dram_tensor(name, list(shape), F32, kind="ExternalOutput").ap()

    kb = KB(nc, es)
    ps = kb.ps
    PR = [ps.R(b) for b in range(8)]

    xT = kb.sb([128, 8, T], F32, "xT")
    ones = kb.sb([128, 128], BF16, "ones")
    onesf = kb.sb([128, 512], F32, "onesf")
    cmat_f = kb.sb([128, 3, 128], F32, "cmatf")
    cmat = kb.sb([128, 3, 128], BF16, "cmat")
    ropeC = kb.sb([128, T], BF16, "ropeC")
    ropeS = kb.sb([128, T], BF16, "ropeS")
    sc = kb.sb([128, 8, 2], BF16, "sc")
    vecs = kb.sb([128, 120], F32, "vecs")
    mvec = kb.sb([128, 16], F32, "mvec")
    modT = kb.sb([128, 72, 2], F32, "modT")
    modD = kb.sb([128, 3, 2, 3, 8], F32, "modD")
    BR = {}

    XR = lambda blk: xT.R(blk)

    def xres(t0, t1):
        return [xT.R(c) for c in range(t0 // 288, (t1 - 1) // 288 + 1)]

    for c in range(8):
        kb.dma("sp", xT[:, :, c * 288:(c + 1) * 288], I["xT"][:, :, c * 288:(c + 1) * 288], (), [xT.R(c)])
    kb.memset(ones[:], 1.0, [ones.R()])
    kb.memset(onesf[:], 1.0, [onesf.R()])
    kb.dma("sp", cmat_f[:], I["cmat"], (), [cmat_f.R()])
    kb.cp(cmat[:], cmat_f[:], [cmat_f.R()], [cmat.R()])
    for hh in range(2):
        kb.dma("pool", ropeC[:, hh * 1152:(hh + 1) * 1152], I["ropeC"][:, hh * 1152:(hh + 1) * 1152], (), [ropeC.R(("h", hh))], weight=True)
        kb.dma("pool", ropeS[:, hh * 1152:(hh + 1) * 1152], I["ropeS"][:, hh * 1152:(hh + 1) * 1152], (), [ropeS.R(("h", hh))], weight=True)
    for rt in (ropeC, ropeS):
        kb.cp(rt[0:1, 0:1], rt[0:1, 0:1], [rt.R(("h", 0)), rt.R(("h", 1))], [rt.R()])
    with kb.scope():
        ctf = kb.sb([128, 8, 2], F32, "ctf")
        kb.dma("sp", ctf[:], I["cT"], (), [ctf.R()])
        kb.act(sc[:], ctf[:], AF.Silu, [ctf.R()], [sc.R()])
    ident = cmat_f[:, 0, :]

    def rstd_from_ps(pb, n, out_ap, out_res, inv_n, eps, tmp):
        kb.act(tmp[:, :n], ps[:, pb, :n], AF.Sqrt, [PR[pb]], [tmp.R()], bias=epsT[:, 0:1] if eps == EPS else epsT[:, 1:2], scale=inv_n)
        kb.recip(out_ap, tmp[:, :n], [tmp.R()], [out_res])

    epsT = kb.sb([128, 2], F32, "epsT")
    kb.memset(epsT[:, 0:1], EPS, [epsT.R()])
    kb.memset(epsT[:, 1:2], 1e-5, [epsT.R()])

    def compute_h(s, t0, t1, hout, hres, W):
        n = t1 - t0
        sq, tmp, rstd, tmp2 = W["sq"], W["tmp"], W["rstd"], W["tmp2"]
        kb.act(sq[:, :, :n], xT[:, :, t0:t1], AF.Square, xres(t0, t1), [sq.R()])
        pb = kb.bank()
        for kt in range(8):
            kb.mm(ps[:, pb, :n], ones[:, :], sq[:, kt, :n], kt == 0, kt == 7, [ones.R(), sq.R()], [PR[pb]])
        rstd_from_ps(pb, n, rstd[:, :n], rstd.R(), 1.0 / D, EPS, tmp)
        for kt in range(8):
            for lo, hi, j in split_tok(t0, t1):
                a, b = lo - t0, hi - t0
                kb.stt(tmp2[:, a:b], xT[:, kt, lo:hi], modD[:, s, j, 0, kt:kt + 1], rstd[:, a:b], ALU.mult, ALU.mult,
                       xres(lo, hi) + [modD.R(), rstd.R()], [tmp2.R()])
                kb.act(hout[:, kt, a:b], tmp2[:, a:b], AF.Identity, [tmp2.R(), modD.R()], [hres],
                       bias=modD[:, s, j, 1, kt:kt + 1])

    def hwork(n):
        return {"sq": kb.sb([128, 8, n], BF16, "sq"), "tmp": kb.sb([128, 512], F32, "tmp"),
                "rstd": kb.sb([128, 512], F32, "rstd"), "tmp2": kb.sb([128, 512], F32, "tmp2")}

    def gelu(out_ap, in_ap, n_part, shape_free, G, rin, wout):
        g1, g2 = G["g1"], G["g2"]
        sl = (slice(0, n_part),) + tuple(slice(0, f) for f in shape_free)
        kb.act(g1[sl], in_ap, AF.Square, rin, [g1.R()])
        kb.ts(g1[sl], g1[sl], 0.044715, 1.0, ALU.mult, ALU.add, [g1.R()], [g1.R()])
        kb.tt(g1[sl], g1[sl], in_ap, ALU.mult, [g1.R()] + rin, [g1.R()])
        kb.act(g2[sl], g1[sl], AF.Sigmoid, [g1.R()], [g2.R()], scale=GC)
        kb.tt(out_ap, g2[sl], in_ap, ALU.mult, [g2.R()] + rin, wout)

    def ada(l):
        kb.dma("sp", vecs[:], I["vecs"][l], (), [vecs.R()])
        kb.dma("sp", mvec[:], I["mvec"][l], (), [mvec.R()])
        with kb.scope():
            wa = [kb.sb([128, 8, 1024], BF16, "wada") for _ in range(2)]
            pb = kb.bank()
            src = I["w_ada"][l].rearrange("(kt p) n -> p kt n", p=128)
            for m in range(9):
                w = wa[m % 2]
                for hh in range(2):
                    kb.dma("pool", w[:, hh * 4:(hh + 1) * 4, :], src[:, hh * 4:(hh + 1) * 4, m * 1024:(m + 1) * 1024], (), [w.R(hh)], weight=True)
                for ot in range(8):
                    col = (m * 8 + ot) * 2
                    for kt in range(8):
                        kb.mm(ps[:, pb, col:col + 2], w[:, kt, ot * 128:(ot + 1) * 128], sc[:, kt, :], kt == 0, kt == 7,
                              [w.R(kt // 4), sc.R()], [PR[pb]])
            for j in range(2):
                kb.tt(modT[:, :, j], ps[:, pb, 0:144].rearrange("p (m j) -> p m j", j=2)[:, :, j], vecs[:, 0:72], ALU.add,
                      [PR[pb], vecs.R()], [modT.R()])
            for s in range(3):
                for j in range(2):
                    wgt = 1.0 if s == 1 else 0.5
                    kb.stt(modD[:, s, j, 0, :], modT[:, (3 * s + 1) * 8:(3 * s + 2) * 8, j], 1.0, vecs[:, 72 + s * 8:72 + (s + 1) * 8],
                           ALU.add, ALU.mult, [modT.R(), vecs.R()], [modD.R()])
                    kb.cp(modD[:, s, j, 1, :], modT[:, (3 * s) * 8:(3 * s + 1) * 8, j], [modT.R()], [modD.R()])
                    kb.stt(modD[:, s, j, 2, :], modT[:, (3 * s + 2) * 8:(3 * s + 3) * 8, j], wgt, vecs[:, 96 + s * 8:96 + (s + 1) * 8],
                           ALU.mult, ALU.mult, [modT.R(), vecs.R()], [modD.R()])

    def post_norm_residual(s, t0, n, ybuf, ssb, W, ykeys=None):
        tmp, rstd, tmp2 = W["tmp"], W["rstd"], W["tmp2"]
        rstd_from_ps(ssb, n, rstd[:, :n], rstd.R(), 1.0 / D, EPS, tmp)
        for kt in range(8):
            kb.tt(tmp2[:, :n], ybuf[:, kt, :n], rstd[:, :n], ALU.mult, [ybuf.R(kt if ykeys else 0), rstd.R()], [tmp2.R()])
            for lo, hi, j in split_tok(t0, t0 + n):
                a, b = lo - t0, hi - t0
                kb.stt(xT[:, kt, lo:hi], tmp2[:, a:b], modD[:, s, j, 2, kt:kt + 1], xT[:, kt, lo:hi], ALU.mult, ALU.add,
                       [tmp2.R(), modD.R()] + xres(lo, hi), xres(lo, hi))

    def ffn(l, jf, s):
        TB, NC_ = 576, 288
        with kb.scope():
            W = hwork(NC_)
            hT = kb.sb([128, 8, TB], BF16, "hT")
            hid = kb.sb([128, 22, TB], BF16, "hid")
            ybuf = [kb.sb([128, 8, NC_], F32, "ybuf") for _ in range(2)]
            win = [kb.sb([128, 8, 2, 256], BF16, "win") for _ in range(2)]
            wout = [kb.sb([128, 22, 256], BF16, "wout") for _ in range(2)]
            sa = [kb.sb([128, NC_], F32, "sa") for _ in range(2)]
            sqy = [kb.sb([128, NC_], BF16, "sqy") for _ in range(2)]
            src_in = I["w_ffn_in"][l, jf].rearrange("(kt p) (two f) -> p kt two f", p=128, two=2)
            src_out = I["w_ffn_out"][l, jf].rearrange("(ft p) d -> p ft d", p=128)
            wi = wo = si = 0
            for blk in range(4):
                t0 = blk * TB
                kb.banks_rot = list(range(8))
                for c in range(2):
                    compute_h(s, t0 + c * NC_, t0 + (c + 1) * NC_, hT[:, :, c * NC_:(c + 1) * NC_], hT.R(c), W)
                for fg in range(11):
                    w = win[wi % 2]
                    wi += 1
                    for gi in range(2):
                        kb.dma("pool", w[:, :, gi, :], src_in[:, :, gi, fg * 256:(fg + 1) * 256], (), [w.R(gi)], weight=True)
                    for fi in range(2):
                        f = fg * 2 + fi
                        for c in range(2):
                            ba, bg = kb.bank(), kb.bank()
                            for gi, pb in ((0, ba), (1, bg)):
                                for kt in range(8):
                                    kb.mm(ps[:, pb, :NC_], w[:, kt, gi, fi * 128:(fi + 1) * 128], hT[:, kt, c * NC_:(c + 1) * NC_],
                                          kt == 0, kt == 7, [w.R(gi), hT.R(c)], [PR[pb]])
                            sab = sa[si % 2]
                            si += 1
                            kb.act(sab[:], ps[:, ba, :NC_], AF.Silu, [PR[ba]], [sab.R()])
                            kb.tt(hid[:, f, c * NC_:(c + 1) * NC_], sab[:], ps[:, bg, :NC_], ALU.mult, [sab.R(), PR[bg]], [hid.R((f, c))])
                kb.banks_rot = list(range(6))
                ssb = (6, 7)
                for dg in range(4):
                    w = wout[wo % 2]
                    wo += 1
                    for hh in range(2):
                        kb.dma("pool", w[:, hh * 11:(hh + 1) * 11, :], src_out[:, hh * 11:(hh + 1) * 11, dg * 256:(dg + 1) * 256], (), [w.R(hh)], weight=True)
                    for di in range(2):
                        dt = dg * 2 + di
                        for c in range(2):
                            pb = kb.bank()
                            for ft in range(22):
                                kb.mm(ps[:, pb, :NC_], w[:, ft, di * 128:(di + 1) * 128], hid[:, ft, c * NC_:(c + 1) * NC_],
                                      ft == 0, ft == 21, [w.R(ft // 11), hid.R((ft, c))], [PR[pb]])
                            kb.act(ybuf[c][:, dt, :], ps[:, pb, :NC_], AF.Copy, [PR[pb]], [ybuf[c].R()])
                            sq_ = sqy[si % 2]
                            si += 1
                            kb.act(sq_[:], ps[:, pb, :NC_], AF.Square, [PR[pb]], [sq_.R()])
                            kb.mm(ps[:, ssb[c], :NC_], ones[:, :], sq_[:], dt == 0, dt == 7, [ones.R(), sq_.R()], [PR[ssb[c]]])
                for c in range(2):
                    post_norm_residual(s, t0 + c * NC_, NC_, ybuf[c], ssb[c], W)
            kb.banks_rot = list(range(8))

    def load_w(dst_tl, dst_ap, src_ap, key=0):
        kb.dma("pool", dst_ap, src_ap, (), [dst_tl.R(key)], weight=True)

    def mla(l):
        scale = 96.0 ** -0.5
        win_src = I["w_in"][l].rearrange("(kt p) n -> p kt n", p=128)
        with kb.scope():
            wuq = kb.sb([128, 2, 512], BF16, "wuq")
            wukv = kb.sb([128, 512], BF16, "wukv")
            kvn = kb.sb([128, T], BF16, "kvn")
            qn = kb.sb([128, 2, T], BF16, "qn")
            kpeR = kb.sb([128, T], BF16, "kpeR")
            sqb = kb.sb([128, 3, 512], BF16, "sqb")
            t1 = kb.sb([128, 512], F32, "t1")
            t2 = kb.sb([128, 512], F32, "t2")
            rs = kb.sb([128, 512], F32, "rs")
            kmax = kb.sb([128, 4], F32, "kmax")
            load_w(wuq, wuq[:], I["w_uq"][l].rearrange("(kt p) n -> p kt n", p=128))
            load_w(wukv, wukv[:], I["w_ukv"][l])
            mla_p1(l, win_src, kvn, qn, kpeR, sqb, t1, t2, rs)
            mla_p2(l, scale, wuq, wukv, kvn, qn, kpeR, sqb, t1, t2, kmax)

    def mla_p1(l, win_src, kvn, qn, kpeR, sqb, t1, t2, rs):
        with kb.scope():
            W = hwork(512)
            hb = kb.sb([128, 8, 512], BF16, "hb")
            wst = kb.sb([128, 8, 448], BF16, "wmla")
            raw = kb.sb([128, 3, 512], F32, "raw")
            for hh in range(2):
                load_w(wst, wst[:, hh * 4:(hh + 1) * 4, :], win_src[:, hh * 4:(hh + 1) * 4, 0:448], hh)
            WST = [wst.R(0), wst.R(1)]
            for bi, (t0, t1_) in enumerate(TBLK):
                n = t1_ - t0
                compute_h(1, t0, t1_, hb, hb.R(), W)
                pbs = [kb.bank() for _ in range(3)]
                for oi, (c0, pb) in enumerate(zip((E_KVL, E_QL, E_QL + 128), pbs)):
                    for kt in range(8):
                        kb.mm(ps[:, pb, :n], wst[:, kt, c0:c0 + 128], hb[:, kt, :n], kt == 0, kt == 7, WST + [hb.R()], [PR[pb]])
                    kb.act(raw[:, oi, :n], ps[:, pb, :n], AF.Copy, [PR[pb]], [raw.R(oi)])
                    kb.act(sqb[:, oi, :n], ps[:, pb, :n], AF.Square, [PR[pb]], [sqb.R(oi)])
                pb = kb.bank()
                kb.mm(ps[:, pb, :n], ones[:, :], sqb[:, 0, :n], True, True, [ones.R(), sqb.R(0)], [PR[pb]])
                rstd_from_ps(pb, n, rs[:, :n], rs.R(), 1.0 / 128, EPS, t1)
                kb.stt(kvn[:, t0:t1_], raw[:, 0, :n], mvec[:, 0:1], rs[:, :n], ALU.mult, ALU.mult, [raw.R(0), mvec.R(), rs.R()], [kvn.R(bi)])
                pb = kb.bank()
                for oi in (1, 2):
                    kb.mm(ps[:, pb, :n], ones[:, :], sqb[:, oi, :n], oi == 1, oi == 2, [ones.R(), sqb.R(oi)], [PR[pb]])
                rstd_from_ps(pb, n, rs[:, :n], rs.R(), 1.0 / 256, EPS, t1)
                for oi in (1, 2):
                    kb.stt(qn[:, oi - 1, t0:t1_], raw[:, oi, :n], mvec[:, oi:oi + 1], rs[:, :n], ALU.mult, ALU.mult,
                           [raw.R(oi), mvec.R(), rs.R()], [qn.R(bi)])
                pa, pbb = kb.bank(), kb.bank()
                for c0, pb in ((E_KPA, pa), (E_KPB, pbb)):
                    for kt in range(8):
                        kb.mm(ps[64:96, pb, :n], wst[:, kt, c0:c0 + 32], hb[:, kt, :n], kt == 0, kt == 7, WST + [hb.R()], [PR[pb]])
                kb.tt(t1[64:96, :n], ps[64:96, pa, :n], ropeC[64:96, t0:t1_], ALU.mult, [PR[pa], ropeC.R()], [t1.R()])
                kb.tt(t2[64:96, :n], ps[64:96, pbb, :n], ropeS[64:96, t0:t1_], ALU.mult, [PR[pbb], ropeS.R()], [t2.R()])
                kb.tt(kpeR[64:96, t0:t1_], t1[64:96, :n], t2[64:96, :n], ALU.add, [t1.R(), t2.R()], [kpeR.R(bi)])

    def mla_p2(l, scale, wuq, wukv, kvn, qn, kpeR, sqb, t1, t2, kmax):
        with kb.scope():
            KTt = [kb.sb([128, T], BF16, "KT") for _ in range(2)]
            QTt = [kb.sb([128, T], BF16, "QT") for _ in range(2)]
            V = kb.sb([128, 18, 256], BF16, "V")
            pT = [kb.sb([128, 512], BF16, "pT") for _ in range(2)]
            rd = kb.sb([128, 512], F32, "rd")
            for kbk in range(18):
                pb = kb.bank()
                kb.mm(ps[:, pb, :256], kvn[:, kbk * 128:(kbk + 1) * 128], wukv[:, 256:512], True, True,
                      [kvn.R(b) for b in range(5)] + [wukv.R()], [PR[pb]])
                kb.act(V[:, kbk, :], ps[:, pb, :256], AF.Copy, [PR[pb]], [V.R()])
            KVN = [kvn.R(b) for b in range(5)]
            QN = [qn.R(b) for b in range(5)]
            for pair in range(2):
                for hi_ in range(2):
                    kb.memset(KTt[hi_][96:97, :], 1.0, [KTt[hi_].R()])
                    kb.memset(kmax[:, 2 * pair + hi_:2 * pair + hi_ + 1], 0.0, [kmax.R()])
                for bi, (t0, t1_) in enumerate(TBLK):
                    n = t1_ - t0
                    pb = kb.bank()
                    kb.mm(ps[:, pb, :n], wukv[:, pair * 128:(pair + 1) * 128], kvn[:, t0:t1_], True, True, [wukv.R()] + KVN, [PR[pb]])
                    for hi_ in range(2):
                        KTh = KTt[hi_]
                        kb.act(KTh[0:64, t0:t1_], ps[hi_ * 64:(hi_ + 1) * 64, pb, :n], AF.Copy, [PR[pb]], [KTh.R()])
                        kb.cp(KTh[64:96, t0:t1_], kpeR[64:96, t0:t1_], [kpeR.R(bi)], [KTh.R()])
                        kb.act(sqb[0:96, 0, :n], KTh[0:96, t0:t1_], AF.Square, [KTh.R()], [sqb.R(0)])
                        p2 = kb.bank()
                        kb.mm(ps[:, p2, :n], ones[0:96, :], sqb[0:96, 0, :n], True, True, [ones.R(), sqb.R(0)], [PR[p2]])
                        kb.emit("dve", lambda e: e.tensor_reduce(out=t1[:, 0:1], in_=ps[:, p2, :n], axis=AX.X, op=ALU.max), [PR[p2]], [t1.R()])
                        hcol = 2 * pair + hi_
                        kb.tt(kmax[:, hcol:hcol + 1], kmax[:, hcol:hcol + 1], t1[:, 0:1], ALU.max, [kmax.R(), t1.R()], [kmax.R()])
                for hi_ in range(2):
                    hcol = 2 * pair + hi_
                    kb.act(kmax[:, hcol:hcol + 1], kmax[:, hcol:hcol + 1], AF.Sqrt, [kmax.R()], [kmax.R()])
                    kb.ts(kmax[:, hcol:hcol + 1], kmax[:, hcol:hcol + 1], -1.0, None, ALU.mult, None, [kmax.R()], [kmax.R()])
                for bi, (t0, t1_) in enumerate(TBLK):
                    n = t1_ - t0
                    for hi_ in range(2):
                        h = 2 * pair + hi_
                        QTh = QTt[hi_]
                        pb = kb.bank()
                        for k2 in range(2):
                            kb.mm(ps[:, pb, :n], wuq[:, k2, h * 128:(h + 1) * 128], qn[:, k2, t0:t1_], k2 == 0, k2 == 1, [wuq.R()] + QN, [PR[pb]])
                        kb.act(QTh[0:64, t0:t1_], ps[0:64, pb, :n], AF.Copy, [PR[pb]], [QTh.R()])
                        kb.tt(t1[64:96, :n], ps[64:96, pb, :n], ropeC[64:96, t0:t1_], ALU.mult, [PR[pb], ropeC.R()], [t1.R()])
                        kb.tt(t2[64:96, :n], ps[96:128, pb, :n], ropeS[96:128, t0:t1_], ALU.mult, [PR[pb], ropeS.R()], [t2.R()])
                        kb.tt(QTh[64:96, t0:t1_], t1[64:96, :n], t2[64:96, :n], ALU.add, [t1.R(), t2.R()], [QTh.R()])
                        kb.act(sqb[0:96, 0, :n], QTh[0:96, t0:t1_], AF.Square, [QTh.R()], [sqb.R(0)])
                        p2 = kb.bank()
                        kb.mm(ps[:, p2, :n], ones[0:96, :], sqb[0:96, 0, :n], True, True, [ones.R(), sqb.R(0)], [PR[p2]])
                        kb.act(t1[96:97, :n], ps[96:97, p2, :n], AF.Sqrt, [PR[p2]], [t1.R()])
                        kb.ts(QTh[96:97, t0:t1_], t1[96:97, :n], kmax[96:97, h:h + 1], None, ALU.mult, None, [t1.R(), kmax.R()], [QTh.R()])
                pti = 0
                for hi_ in range(2):
                    h = 2 * pair + hi_
                    KTh, QTh = KTt[hi_], QTt[hi_]
                    for gi, (q0, q1) in enumerate(TBLK):
                        nq = q1 - q0
                        kbs = list(range(2)) if gi == 0 else list(range(18))
                        po = kb.bank()
                        for ki, kbk in enumerate(kbs):
                            pS = kb.bank()
                            while pS == po:
                                pS = kb.bank()
                            kb.mm(ps[:, pS, :nq], KTh[0:97, kbk * 128:(kbk + 1) * 128], QTh[0:97, q0:q1], True, True, [KTh.R(), QTh.R()], [PR[pS]])
                            p_ = pT[pti % 2]
                            pti += 1
                            kb.act(p_[:, :nq], ps[:, pS, :nq], AF.Exp, [PR[pS]], [p_.R()], scale=scale)
                            last = ki == len(kbs) - 1
                            kb.mm(ps[0:64, po, :nq], V[:, kbk, h * 64:(h + 1) * 64], p_[:, :nq], ki == 0, last, [V.R(), p_.R()], [PR[po]], sig=last)
                            kb.mm(ps[64:128, po, :nq], ones[:, 0:64], p_[:, :nq], ki == 0, last, [ones.R(), p_.R()], [PR[po]], sig=True)
                        kb.recip(rd[0:64, :nq], ps[64:128, po, :nq], [PR[po]], [rd.R()])
                        kb.tt(BR['t'][hi_ * 64:(hi_ + 1) * 64, 0, pair, q0:q1], ps[0:64, po, :nq], rd[0:64, :nq], ALU.mult, [PR[po], rd.R()], [BR['t'].R((0, pair, gi))])

    def gqa(l):
        scale = 0.125
        win_src = I["w_in"][l].rearrange("(kt p) n -> p kt n", p=128)
        with kb.scope():
            W = hwork(512)
            hb = kb.sb([128, 8, 512], BF16, "hb")
            wst = kb.sb([128, 8, 896], BF16, "wgqa")
            QG = [kb.sb([128, T], BF16, "QG") for _ in range(4)]
            KG = [kb.sb([128, T], BF16, "KG") for _ in range(2)]
            VG = kb.sb([128, 18, 128], BF16, "VG")
            sqb = kb.sb([128, 512], BF16, "sqb")
            t1 = kb.sb([128, 512], F32, "t1")
            t2 = kb.sb([128, 512], F32, "t2")
            kmax = kb.sb([128, 2], F32, "kmax")
            sst = kb.sb([128, 4], F32, "sst")
            ksink = kb.sb([128, 4], BF16, "ksink")
            pT = [kb.sb([128, 5, 128], BF16, "pT") for _ in range(2)]
            psk = [kb.sb([128, 128], BF16, "psk") for _ in range(2)]
            rd = kb.sb([128, 128], F32, "rd")
            for hh in range(2):
                load_w(wst, wst[:, hh * 4:(hh + 1) * 4, :], win_src[:, hh * 4:(hh + 1) * 4, E_GQA:E_GQA + 896], hh)
            WST = [wst.R(0), wst.R(1)]
            kb.memset(sst[64:66, :], scale, [sst.R()])
            kb.dma("sp", sst[65:66, :], I["mvec"][l, 65:66, 8:12], (), [sst.R()])
            kb.memset(ksink[0:66, :], 0.0, [ksink.R()])
            kb.ts(ksink[64:66, :], sst[64:66, :], 1.0 / scale, None, ALU.mult, None, [sst.R()], [ksink.R()])
            for h in range(4):
                kb.memset(QG[h][64:66, :], 1.0, [QG[h].R()])
            for kv in range(2):
                kb.memset(KG[kv][64:66, :], 0.0, [KG[kv].R()])
                kb.memset(KG[kv][64:65, :], 1.0, [KG[kv].R()])
            kb.memset(kmax[:], 0.0, [kmax.R()])

            def rope_proj(cA, cB, dst, t0, t1_, n):
                pa, pbb = kb.bank(), kb.bank()
                for c0, pb in ((cA, pa), (cB, pbb)):
                    for kt in range(8):
                        kb.mm(ps[0:64, pb, :n], wst[:, kt, c0:c0 + 64], hb[:, kt, :n], kt == 0, kt == 7, WST + [hb.R()], [PR[pb]])
                kb.tt(t1[0:64, :n], ps[0:64, pa, :n], ropeC[0:64, t0:t1_], ALU.mult, [PR[pa], ropeC.R()], [t1.R()])
                kb.tt(t2[0:64, :n], ps[0:64, pbb, :n], ropeS[0:64, t0:t1_], ALU.mult, [PR[pbb], ropeS.R()], [t2.R()])
                kb.tt(dst[0:64, t0:t1_], t1[0:64, :n], t2[0:64, :n], ALU.add, [t1.R(), t2.R()], [dst.R()])

            def sumsq64(src, t0, t1_, n):
                kb.act(sqb[0:64, :n], src[0:64, t0:t1_], AF.Square, [src.R()], [sqb.R()])
                p2 = kb.bank()
                kb.mm(ps[:, p2, :n], ones[0:64, :], sqb[0:64, :n], True, True, [ones.R(), sqb.R()], [PR[p2]])
                return p2

            for bi, (t0, t1_) in enumerate(TBLK):
                n = t1_ - t0
                compute_h(1, t0, t1_, hb, hb.R(), W)
                for kv in range(2):
                    rope_proj(E_GKA - E_GQA + kv * 64, E_GKB - E_GQA + kv * 64, KG[kv], t0, t1_, n)
                    p2 = sumsq64(KG[kv], t0, t1_, n)
                    kb.emit("dve", lambda e: e.tensor_reduce(out=t1[:, 0:1], in_=ps[:, p2, :n], axis=AX.X, op=ALU.max), [PR[p2]], [t1.R()])
                    kb.tt(kmax[:, kv:kv + 1], kmax[:, kv:kv + 1], t1[:, 0:1], ALU.max, [kmax.R(), t1.R()], [kmax.R()])
                for cb in range(n // 128):
                    kbk = t0 // 128 + cb
                    pb = kb.bank()
                    for kt in range(8):
                        kb.mm(ps[:, pb, :128], hb[:, kt, cb * 128:(cb + 1) * 128], wst[:, kt, E_GV - E_GQA:E_GV - E_GQA + 128], kt == 0, kt == 7,
                              WST + [hb.R()], [PR[pb]])
                    kb.act(VG[:, kbk, :], ps[:, pb, :128], AF.Copy, [PR[pb]], [VG.R()])
            kb.act(kmax[:], kmax[:], AF.Sqrt, [kmax.R()], [kmax.R()])
            kb.ts(kmax[:], kmax[:], -1.0, None, ALU.mult, None, [kmax.R()], [kmax.R()])
            for bi, (t0, t1_) in enumerate(TBLK):
                n = t1_ - t0
                compute_h(1, t0, t1_, hb, hb.R(), W)
                for h in range(4):
                    rope_proj(h * 64, E_GQB - E_GQA + h * 64, QG[h], t0, t1_, n)
                    p2 = sumsq64(QG[h], t0, t1_, n)
                    kb.act(t1[64:65, :n], ps[64:65, p2, :n], AF.Sqrt, [PR[p2]], [t1.R()])
                    kb.ts(QG[h][64:65, t0:t1_], t1[64:65, :n], kmax[64:65, h // 2:h // 2 + 1], None, ALU.mult, None, [t1.R(), kmax.R()], [QG[h].R()])
            it = 0
            for h in range(4):
                kv = h // 2
                for qb in range(18):
                    q0 = qb * 128
                    if qb < 2:
                        band = []
                    else:
                        nb = qb - 2
                        band = [(2 + nb + d, d) for d in (-1, 0, 1) if 0 <= nb + d < 16]
                    pband, pctx, po = kb.bank(), kb.bank(), kb.bank()
                    p_ = pT[it % 2]
                    pk = psk[it % 2]
                    it += 1
                    for i, (kbk, d) in enumerate(band):
                        kb.mm(ps[:, pband, i * 128:(i + 1) * 128], KG[kv][0:66, kbk * 128:(kbk + 1) * 128], QG[h][0:66, q0:q0 + 128], True, d == 0,
                              [KG[kv].R(), QG[h].R()], [PR[pband]], sig=(d == 0))
                        if d != 0:
                            mi = 1 if d < 0 else 2
                            kb.mm(ps[:, pband, i * 128:(i + 1) * 128], cmat[:, mi, :], cmat[:, 0, :], False, True, [cmat.R()], [PR[pband]], sig=True)
                    for i in range(2):
                        kb.mm(ps[:, pctx, i * 128:(i + 1) * 128], KG[kv][0:66, i * 128:(i + 1) * 128], QG[h][0:66, q0:q0 + 128], True, True,
                              [KG[kv].R(), QG[h].R()], [PR[pctx]])
                    kb.mm(ps[0:1, pctx, 256:384], ksink[0:66, h:h + 1], QG[h][0:66, q0:q0 + 128], True, True, [ksink.R(), QG[h].R()], [PR[pctx]])
                    nb_ = len(band)
                    if nb_:
                        kb.act(p_[:, 0:nb_, :], ps[:, pband, 0:nb_ * 128].rearrange("p (a b) -> p a b", b=128), AF.Exp, [PR[pband]], [p_.R()], scale=scale)
                    kb.act(p_[:, 3:5, :], ps[:, pctx, 0:256].rearrange("p (a b) -> p a b", b=128), AF.Exp, [PR[pctx]], [p_.R()], scale=scale)
                    kb.act(pk[0:1, :], ps[0:1, pctx, 256:384], AF.Exp, [PR[pctx]], [pk.R()], scale=scale)
                    items = [(kbk, i) for i, (kbk, d) in enumerate(band)] + [(0, 3), (1, 4)]
                    for ii, (kbk, pi) in enumerate(items):
                        kb.mm(ps[0:64, po, :128], VG[:, kbk, kv * 64:(kv + 1) * 64], p_[:, pi, :], ii == 0, ii == len(items) - 1, [VG.R(), p_.R()], [PR[po]], sig=False)
                        kb.mm(ps[64:128, po, :128], ones[:, 0:64], p_[:, pi, :], ii == 0, False, [ones.R(), p_.R()], [PR[po]], sig=False)
                    kb.mm(ps[64:128, po, :128], ones[0:1, 0:64], pk[0:1, :], False, True, [ones.R(), pk.R()], [PR[po]], sig=True)
                    kb.recip(rd[0:64, :], ps[64:128, po, :128], [PR[po]], [rd.R()])
                    kb.tt(BR['t'][(h % 2) * 64:(h % 2 + 1) * 64, 1, h // 2, q0:q0 + 128], ps[0:64, po, :128], rd[0:64, :], ALU.mult, [PR[po], rd.R()],
                          [BR['t'].R((1, h // 2, qb))])

    def gmlp(l):
        win_src = I["w_in"][l].rearrange("(kt p) n -> p kt n", p=128)
        with kb.scope():
            W = hwork(512)
            hb = kb.sb([128, 8, 512], BF16, "hb")
            wst = kb.sb([128, 8, 512], BF16, "wz")
            wsT = kb.sb([128, 4, 128], BF16, "wsT")
            bsT = kb.sb([128, 2, 128], F32, "bsT")
            zu = kb.sb([128, 2, 512], BF16, "zu")
            G = {"g1": kb.sb([128, 512], F32, "g1"), "g2": kb.sb([128, 512], F32, "g2")}
            vg = kb.sb([128, 256], F32, "vg")
            xn = kb.sb([128, 256], BF16, "xn")
            st6 = kb.sb([128, 6], F32, "st6")
            mv = kb.sb([128, 4], F32, "mv")
            mx = kb.sb([128, 128], F32, "mx")
            for hh in range(2):
                load_w(wst, wst[:, hh * 4:(hh + 1) * 4, :], win_src[:, hh * 4:(hh + 1) * 4, E_Z:E_Z + 512], hh)
            load_w(wsT, wsT[:], I["wsT"][l])
            kb.dma("sp", bsT[:], I["bsT"][l], (), [bsT.R()])
            WST = [wst.R(0), wst.R(1)]
            for bi, (t0, t1_) in enumerate(TBLK):
                n = t1_ - t0
                compute_h(1, t0, t1_, hb, hb.R(), W)
                for ot in range(2):
                    pb = kb.bank()
                    for kt in range(8):
                        kb.mm(ps[:, pb, :n], wst[:, kt, ot * 128:(ot + 1) * 128], hb[:, kt, :n], kt == 0, kt == 7, WST + [hb.R()], [PR[pb]])
                    gelu(zu[:, ot, :n], ps[:, pb, :n], 128, (n,), G, [PR[pb]], [zu.R(ot)])
                for cb in range(n // 128):
                    c0 = t0 + cb * 128
                    pb = kb.bank()
                    for kt in range(8):
                        kb.mm(ps[:, pb, :256], hb[:, kt, cb * 128:(cb + 1) * 128], wst[:, kt, 256:512], kt == 0, kt == 7, WST + [hb.R()], [PR[pb]])
                    gelu(vg[:, :], ps[:, pb, :256], 128, (256,), G, [PR[pb]], [vg.R()])
                    kb.emit("dve", lambda e: e.bn_stats(out=st6[:, :], in_=vg[:, :]), [vg.R()], [st6.R()])
                    kb.emit("dve", lambda e: e.bn_aggr(out=mv[:, 0:2], in_=st6[:, :]), [st6.R()], [mv.R()])
                    kb.act(mv[:, 2:3], mv[:, 1:2], AF.Sqrt, [mv.R()], [mv.R()], bias=epsT[:, 1:2])
                    kb.recip(mv[:, 3:4], mv[:, 2:3], [mv.R()], [mv.R()])
                    kb.ts(xn[:, :], vg[:, :], mv[:, 0:1], mv[:, 3:4], ALU.subtract, ALU.mult, [vg.R(), mv.R()], [xn.R()])
                    for gp in range(2):
                        pm = kb.bank()
                        for gg in range(2):
                            g = gp * 2 + gg
                            kb.mm(ps[gg * 64:(gg + 1) * 64, pm, :128], xn[:, g * 64:(g + 1) * 64], wsT[:, g, :], True, True, [xn.R(), wsT.R()], [PR[pm]])
                        kb.stt(mx[:, :], ps[:, pm, :128], mvec[:, 3 + gp:4 + gp], bsT[:, gp, :], ALU.mult, ALU.add, [PR[pm], mvec.R(), bsT.R()], [mx.R()])
                        kb.tt(BR['t'][:, 3, gp, c0:c0 + 128], mx[:, :], zu[:, gp, cb * 128:(cb + 1) * 128], ALU.mult, [mx.R(), zu.R(gp)], [BR['t'].R((3, gp, c0 // 128))])

    def s5(l):
        win_src = I["w_in"][l].rearrange("(kt p) n -> p kt n", p=128)
        with kb.scope():
            wglu = kb.sb([128, 2, 512], BF16, "wglu")
            uT = kb.sb([128, 2, T], BF16, "uT")
            uR = kb.sb([128, 2, T], BF16, "uR")
            yacc = kb.sb([128, 2, T], BF16, "yacc")
            load_w(wglu, wglu[:], I["w_glu"][l].rearrange("(kt p) n -> p kt n", p=128))
            with kb.scope():
                W = hwork(512)
                hb = kb.sb([128, 8, 512], BF16, "hb")
                wst = kb.sb([128, 8, 256], BF16, "wu")
                for hh in range(2):
                    load_w(wst, wst[:, hh * 4:(hh + 1) * 4, :], win_src[:, hh * 4:(hh + 1) * 4, E_U:E_U + 256], hh)
                WST = [wst.R(0), wst.R(1)]
                for bi, (t0, t1_) in enumerate(TBLK):
                    n = t1_ - t0
                    compute_h(1, t0, t1_, hb, hb.R(), W)
                    for ot in range(2):
                        pb = kb.bank()
                        for kt in range(8):
                            kb.mm(ps[:, pb, :n], wst[:, kt, ot * 128:(ot + 1) * 128], hb[:, kt, :n], kt == 0, kt == 7, WST + [hb.R()], [PR[pb]])
                        kb.act(uT[:, ot, t0:t1_], ps[:, pb, :n], AF.Copy, [PR[pb]], [uT.R()])
            kb.cp(uR[:, :, 0:NCTX], uT[:, :, 0:NCTX][:, :, ::-1], [uT.R()], [uR.R()])
            kb.cp(uR[:, :, NCTX:T], uT[:, :, NCTX:T][:, :, ::-1], [uT.R()], [uR.R()])
            with kb.scope():
                pp = kb.sb([128, 8, 4], F32, "pp")
                tabs = kb.sb([128, 4, 8, 128], F32, "tabs")
                sc1 = kb.sb([128, 12, 8], F32, "sc1")
                bT = kb.sb([128, 2, 8, 128], BF16, "bT")
                cP = kb.sb([128, 2, 8, 128], BF16, "cP")
                kc = kb.sb([128, 8, 2], F32, "kc")
                NA = kb.sb([128, 8, 2], F32, "NA")
                kt_ = kb.sb([128, 4], F32, "kt_")
                PI = float(np.pi)

                def sin_of(out_ap, in_ap, shift, r, w, s_a, s_b):
                    kb.ts(s_a, in_ap, shift + PI, 1.0 / (2 * PI), ALU.add, ALU.mult, r, [sc1.R()])
                    kb.ts(s_b, s_a, -0.5, 12582912.0, ALU.add, ALU.add, [sc1.R()], [sc1.R()])
                    kb.ts(s_b, s_b, -12582912.0, None, ALU.add, None, [sc1.R()], [sc1.R()])
                    kb.tt(s_a, s_a, s_b, ALU.subtract, [sc1.R()], [sc1.R()])
                    kb.ts(s_a, s_a, 2 * PI, -PI, ALU.mult, ALU.add, [sc1.R()], [sc1.R()])
                    kb.ts(s_a, s_a, PI, -PI, ALU.min, ALU.max, [sc1.R()], [sc1.R()])
                    kb.act(out_ap, s_a, AF.Sin, [sc1.R()], w)

                for d_ in range(2):
                  usrc = uT if d_ == 0 else uR
                  with kb.scope():
                    bst = kb.sb([128, 2, 8, 128], F32, "bst")
                    bb = kb.sb([128, 2, 8, 128], F32, "bb")
                    tA = kb.sb([128, 8, 128], F32, "tA")
                    tB = kb.sb([128, 8, 128], F32, "tB")
                    kb.dma("sp", pp[:], I["s5p"][l, d_], (), [pp.R()])
                    kb.dma("sp", bst[:], I["s5b"][l, d_].rearrange("r n k c -> n r k c"), (), [bst.R()])
                    S = lambda i: sc1[:, i, :]
                    R1 = [sc1.R()]
                    lr, li, ldt = pp[:, :, 0], pp[:, :, 1], pp[:, :, 2]
                    kb.ts(S(0), lr, -1e-4, None, ALU.min, None, [pp.R()], R1)
                    kb.act(S(1), ldt, AF.Exp, [pp.R()], R1)
                    kb.tt(S(2), S(0), S(1), ALU.mult, R1, R1)
                    kb.act(S(2), S(2), AF.Exp, R1, R1)
                    kb.tt(S(3), li, S(1), ALU.mult, [pp.R()] + R1, R1)
                    sin_of(S(4), S(3), 0.0, R1, R1, S(10), S(11))
                    sin_of(S(5), S(3), PI / 2, R1, R1, S(10), S(11))
                    kb.tt(S(4), S(4), S(2), ALU.mult, R1, R1)
                    kb.tt(S(5), S(5), S(2), ALU.mult, R1, R1)
                    kb.ts(S(6), S(5), -1.0, None, ALU.add, None, R1, R1)
                    kb.tt(S(7), S(0), S(0), ALU.mult, R1, R1)
                    kb.tt(S(8), li, li, ALU.mult, [pp.R()], R1)
                    kb.tt(S(7), S(7), S(8), ALU.add, R1, R1)
                    kb.recip(S(7), S(7), R1, R1)
                    kb.tt(S(8), S(6), S(0), ALU.mult, R1, R1)
                    kb.tt(S(9), S(4), li, ALU.mult, R1 + [pp.R()], R1)
                    kb.tt(S(8), S(8), S(9), ALU.add, R1, R1)
                    kb.tt(S(8), S(8), S(7), ALU.mult, R1, R1)
                    kb.tt(S(9), S(4), S(0), ALU.mult, R1, R1)
                    kb.tt(S(6), S(6), li, ALU.mult, R1 + [pp.R()], R1)
                    kb.tt(S(9), S(9), S(6), ALU.subtract, R1, R1)
                    kb.tt(S(9), S(9), S(7), ALU.mult, R1, R1)
                    bc = lambda i: sc1[:, i, :].unsqueeze(2).broadcast_to([128, 8, 128])
                    kb.tt(tA[:], bst[:, 0], bc(8), ALU.mult, [bst.R()] + R1, [tA.R()])
                    kb.tt(tB[:], bst[:, 1], bc(9), ALU.mult, [bst.R()] + R1, [tB.R()])
                    kb.tt(bb[:, 0], tA[:], tB[:], ALU.subtract, [tA.R(), tB.R()], [bb.R()])
                    kb.tt(tA[:], bst[:, 1], bc(8), ALU.mult, [bst.R()] + R1, [tA.R()])
                    kb.tt(tB[:], bst[:, 0], bc(9), ALU.mult, [bst.R()] + R1, [tB.R()])
                    kb.tt(bb[:, 1], tA[:], tB[:], ALU.add, [tA.R(), tB.R()], [bb.R()])
                    for ri in range(2):
                        for k in range(8):
                            pb = kb.bank()
                            kb.emit("pe", lambda e: e.transpose(ps[:, pb, 0:128], bb[:, ri, k, :], ident), [bb.R(), cmat_f.R()], [PR[pb]])
                            kb.act(bT[:, ri, k, :], ps[:, pb, 0:128], AF.Copy, [PR[pb]], [bT.R()])
                    kb.dma("pool", cP[:], I["s5c"][l, d_].rearrange("r n k c -> n r k c"), (), [cP.R()], weight=True)
                    kb.tt(S(6), S(2), S(2), ALU.mult, R1, R1)
                    kb.recip(S(6), S(6), R1, R1)
                    kb.tt(S(7), S(5), S(6), ALU.mult, R1, R1)
                    kb.tt(S(6), S(4), S(6), ALU.mult, R1, R1)
                    kb.ts(S(6), S(6), -1.0, None, ALU.mult, None, R1, R1)
                    TR = [tabs.R()]
                    for (tr, ti, pr0, pi0) in ((0, 1, 5, 4), (2, 3, 7, 6)):
                        kb.memset(tabs[:, tr, :, 0:1], 1.0, TR)
                        kb.memset(tabs[:, ti, :, 0:1], 0.0, TR)
                        kb.cp(S(10), S(pr0), R1, R1)
                        kb.cp(S(11), S(pi0), R1, R1)
                        m = 1
                        while m < 256:
                            mm_ = min(m, 128) if m < 128 else 1
                            if m < 128:
                                pr = sc1[:, 10, :].unsqueeze(2).broadcast_to([128, 8, m])
                                pi_ = sc1[:, 11, :].unsqueeze(2).broadcast_to([128, 8, m])
                                src_r, src_i = tabs[:, tr, :, 0:m], tabs[:, ti, :, 0:m]
                                kb.tt(tA[:, :, 0:m], src_r, pr, ALU.mult, TR + R1, [tA.R()])
                                kb.tt(tB[:, :, 0:m], src_i, pi_, ALU.mult, TR + R1, [tB.R()])
                                kb.tt(tabs[:, tr, :, m:2 * m], tA[:, :, 0:m], tB[:, :, 0:m], ALU.subtract, [tA.R(), tB.R()], TR)
                                kb.tt(tA[:, :, 0:m], src_r, pi_, ALU.mult, TR + R1, [tA.R()])
                                kb.tt(tB[:, :, 0:m], src_i, pr, ALU.mult, TR + R1, [tB.R()])
                                kb.tt(tabs[:, ti, :, m:2 * m], tA[:, :, 0:m], tB[:, :, 0:m], ALU.add, [tA.R(), tB.R()], TR)
                            if m < 128:
                                kb.tt(tA[:, :, 0], S(10), S(10), ALU.mult, R1, [tA.R()])
                                kb.tt(tB[:, :, 0], S(11), S(11), ALU.mult, R1, [tB.R()])
                                kb.tt(S(11), S(10), S(11), ALU.mult, R1, R1)
                                kb.ts(S(11), S(11), 2.0, None, ALU.mult, None, R1, R1)
                                kb.tt(S(10), tA[:, :, 0], tB[:, :, 0], ALU.subtract, [tA.R(), tB.R()], R1)
                            m *= 2
                        if tr == 0:
                            kb.cp(sc1[:, 0, :], S(10), R1, R1)
                            kb.cp(sc1[:, 1, :], S(11), R1, R1)
                  with kb.scope():
                    Wt = kb.sb([128, 2, 512], F32, "Wt")
                    Zt = kb.sb([128, 2, 512], F32, "Zt")
                    Ss = [kb.sb([128, 2, 512], BF16, "Ss") for _ in range(2)]
                    w4 = [kb.sb([128, 512], F32, "w4") for _ in range(4)]
                    S = lambda i: sc1[:, i, :]
                    R1 = [sc1.R()]
                    TR = [tabs.R()]
                    kb.ts(NA[:, :, 0], sc1[:, 1, :], -1.0, None, ALU.mult, None, R1, [NA.R()])
                    kb.cp(NA[:, :, 1], sc1[:, 1, :], R1, [NA.R()])
                    kb.memset(kc[:], 0.0, [kc.R()])
                    si_ = 0
                    for bi, (t0, t1_) in enumerate(TBLK):
                        n = t1_ - t0
                        nch = n // 128
                        pY = [kb.bank(), kb.bank()]
                        for k in range(8):
                            o = k // 4
                            pr_, pi_ = kb.bank(), kb.bank()
                            while pr_ in pY:
                                pr_ = kb.bank()
                            while pi_ in pY or pi_ == pr_:
                                pi_ = kb.bank()
                            kb.mm(ps[:, pr_, :n], bT[:, 0, k, :], usrc[:, o, t0:t1_], True, True, [bT.R(), usrc.R()], [PR[pr_]])
                            kb.mm(ps[:, pi_, :n], bT[:, 1, k, :], usrc[:, o, t0:t1_], True, True, [bT.R(), usrc.R()], [PR[pi_]])
                            tb = lambda i: tabs[:, i, k:k + 1, :].broadcast_to([128, nch, 128])
                            v3 = lambda ap: ap.rearrange("p (c j) -> p c j", j=128)
                            P_r, P_i = v3(ps[:, pr_, :n]), v3(ps[:, pi_, :n])
                            a0, a1, a2, a3 = [v3(w4[i][:, :n]) for i in range(4)]
                            kb.tt(a0, P_r, tb(2), ALU.mult, [PR[pr_]] + TR, [w4[0].R()])
                            kb.tt(a1, P_i, tb(3), ALU.mult, [PR[pi_]] + TR, [w4[1].R()])
                            kb.tt(a2, P_i, tb(2), ALU.mult, [PR[pi_]] + TR, [w4[2].R()])
                            kb.tt(a3, P_r, tb(3), ALU.mult, [PR[pr_]] + TR, [w4[3].R()])
                            kb.tt(Wt[:, 0, :n], w4[0][:, :n], w4[1][:, :n], ALU.subtract, [w4[0].R(), w4[1].R()], [Wt.R()])
                            kb.tt(Wt[:, 1, :n], w4[2][:, :n], w4[3][:, :n], ALU.add, [w4[2].R(), w4[3].R()], [Wt.R()])
                            for c in range(nch):
                                cs = slice(c * 128, (c + 1) * 128)
                                for ri in range(2):
                                    kb.emit("dve", lambda e: e.tensor_tensor_scan(out=Zt[:, ri, cs], data0=onesf[:, 0:128], data1=Wt[:, ri, cs],
                                                                                  initial=kc[:, k, ri:ri + 1], op0=ALU.mult, op1=ALU.add),
                                            [onesf.R(), Wt.R(), kc.R()], [Zt.R()])
                                e0 = c * 128 + 127
                                kb.tt(kt_[:, 0:2], Zt[:, ::-1, e0], NA[:, k, :], ALU.mult, [Zt.R(), NA.R()], [kt_.R()])
                                kb.stt(kc[:, k, :], Zt[:, :, e0], sc1[:, 0, k:k + 1], kt_[:, 0:2], ALU.mult, ALU.add, [Zt.R(), kt_.R()] + R1, [kc.R()])
                            Sb = Ss[si_ % 2]
                            si_ += 1
                            Z_r, Z_i = v3(Zt[:, 0, :n]), v3(Zt[:, 1, :n])
                            kb.tt(a0, Z_r, tb(0), ALU.mult, [Zt.R()] + TR, [w4[0].R()])
                            kb.tt(a1, Z_i, tb(1), ALU.mult, [Zt.R()] + TR, [w4[1].R()])
                            kb.tt(a2, Z_i, tb(0), ALU.mult, [Zt.R()] + TR, [w4[2].R()])
                            kb.tt(a3, Z_r, tb(1), ALU.mult, [Zt.R()] + TR, [w4[3].R()])
                            kb.tt(Sb[:, 0, :n], w4[0][:, :n], w4[1][:, :n], ALU.subtract, [w4[0].R(), w4[1].R()], [Sb.R()])
                            kb.stt(Sb[:, 1, :n], w4[2][:, :n], -1.0, w4[3][:, :n], ALU.mult, ALU.subtract, [w4[2].R(), w4[3].R()], [Sb.R()])
                            first, last = (k % 4 == 0), (k % 4 == 3)
                            kb.mm(ps[:, pY[o], :n], cP[:, 0, k, :], Sb[:, 0, :n], first, False, [cP.R(), Sb.R()], [PR[pY[o]]], sig=False)
                            kb.mm(ps[:, pY[o], :n], cP[:, 1, k, :], Sb[:, 1, :n], False, last, [cP.R(), Sb.R()], [PR[pY[o]]], sig=True)
                        for o in range(2):
                            if d_ == 0:
                                kb.stt(yacc[:, o, t0:t1_], uT[:, o, t0:t1_], mvec[:, 5 + o:6 + o], ps[:, pY[o], :n], ALU.mult, ALU.add,
                                       [uT.R(), mvec.R(), PR[pY[o]]], [yacc.R()])
                            else:
                                if bi == 0:
                                    dst = yacc[:, o, 0:NCTX][:, ::-1]
                                else:
                                    a_, b_ = t0 - NCTX, t1_ - NCTX
                                    lo_ = NCTX + (NLAT - b_)
                                    hi_ = NCTX + (NLAT - a_)
                                    dst = yacc[:, o, lo_:hi_][:, ::-1]
                                kb.tt(dst, dst, ps[:, pY[o], :n], ALU.add, [yacc.R(), PR[pY[o]]], [yacc.R()])
            with kb.scope():
                G = {"g1": kb.sb([128, 512], F32, "g1"), "g2": kb.sb([128, 512], F32, "g2")}
                yg = kb.sb([128, 2, 512], BF16, "yg")
                sg = kb.sb([128, 512], F32, "sg")
                for bi, (t0, t1_) in enumerate(TBLK):
                    n = t1_ - t0
                    for o in range(2):
                        gelu(yg[:, o, :n], yacc[:, o, t0:t1_], 128, (n,), G, [yacc.R()], [yg.R()])
                    for ot in range(2):
                        pa, pg = kb.bank(), kb.bank()
                        for (pb, cc) in ((pa, ot * 128), (pg, 256 + ot * 128)):
                            for k2 in range(2):
                                kb.mm(ps[:, pb, :n], wglu[:, k2, cc:cc + 128], yg[:, k2, :n], k2 == 0, k2 == 1, [wglu.R(), yg.R()], [PR[pb]])
                        kb.act(sg[:, :n], ps[:, pg, :n], AF.Sigmoid, [PR[pg], mvec.R()], [sg.R()], bias=mvec[:, 14 + ot:15 + ot])
                        kb.stt(BR['t'][:, 2, ot, t0:t1_], ps[:, pa, :n], mvec[:, 12 + ot:13 + ot], sg[:, :n], ALU.add, ALU.mult,
                               [PR[pa], mvec.R(), sg.R()], [BR['t'].R((2, ot, bi))])

    def merge(l):
        win_src = I["w_in"][l].rearrange("(kt p) n -> p kt n", p=128)
        wo_src = I["w_out"][l].rearrange("(kt p) n -> p kt n", p=128)
        br = BR['t']
        with kb.scope():
            W = hwork(512)
            hb = kb.sb([128, 8, 512], BF16, "hb")
            wg = [kb.sb([128, 8, 512], BF16, "wg") for _ in range(2)]
            wb = [kb.sb([128, 2, 1024], BF16, "wb") for _ in range(2)]
            mg = kb.sb([128, 8, 512], F32, "mg")
            mgb = kb.sb([128, 8, 512], BF16, "mgb")
            sg = [kb.sb([128, 512], F32, "sg") for _ in range(2)]
            tq = kb.sb([128, 512], F32, "tq")
            sqy = [kb.sb([128, 512], BF16, "sqy") for _ in range(2)]
            BRALL = [r for r in br._r.values()]
            gi_ = 0
            si_ = 0
            for bi, (t0, t1_) in enumerate(TBLK):
                n = t1_ - t0
                kb.banks_rot = list(range(7))
                compute_h(1, t0, t1_, hb, hb.R(), W)
                for i in range(4):
                    wbi = wb[i % 2]
                    load_w(wbi, wbi[:], I["w_branch"][l, i].rearrange("(kt p) n -> p kt n", p=128))
                    for half in range(2):
                        w = wg[gi_ % 2]
                        gi_ += 1
                        c0 = E_GATE + i * 1024 + half * 512
                        for hh in range(2):
                            load_w(w, w[:, hh * 4:(hh + 1) * 4, :], win_src[:, hh * 4:(hh + 1) * 4, c0:c0 + 512], hh)
                        for d4 in range(4):
                            dt = half * 4 + d4
                            pgt, ppj = kb.bank(), kb.bank()
                            for kt in range(8):
                                kb.mm(ps[:, pgt, :n], w[:, kt, d4 * 128:(d4 + 1) * 128], hb[:, kt, :n], kt == 0, kt == 7, [w.R(0), w.R(1), hb.R()], [PR[pgt]])
                            for k2 in range(2):
                                kb.mm(ps[:, ppj, :n], wbi[:, k2, dt * 128:(dt + 1) * 128], br[:, i, k2, t0:t1_], k2 == 0, k2 == 1, [wbi.R()] + BRALL, [PR[ppj]])
                            s_ = sg[si_ % 2]
                            si_ += 1
                            kb.act(s_[:, :n], ps[:, pgt, :n], AF.Sigmoid, [PR[pgt]], [s_.R()])
                            if i == 0:
                                kb.tt(mg[:, dt, :n], s_[:, :n], ps[:, ppj, :n], ALU.mult, [s_.R(), PR[ppj]], [mg.R(dt)])
                            else:
                                kb.tt(tq[:, :n], s_[:, :n], ps[:, ppj, :n], ALU.mult, [s_.R(), PR[ppj]], [tq.R()])
                                kb.tt(mg[:, dt, :n], mg[:, dt, :n], tq[:, :n], ALU.add, [mg.R(dt), tq.R()], [mg.R(dt)])
                for dt in range(8):
                    kb.act(mgb[:, dt, :n], mg[:, dt, :n], AF.Copy, [mg.R(dt)], [mgb.R()])
                ssb = 7
                for half in range(2):
                    w = wg[gi_ % 2]
                    gi_ += 1
                    for hh in range(2):
                        load_w(w, w[:, hh * 4:(hh + 1) * 4, :], wo_src[:, hh * 4:(hh + 1) * 4, half * 512:(half + 1) * 512], hh)
                    for d4 in range(4):
                        dt = half * 4 + d4
                        pb = kb.bank()
                        for kt in range(8):
                            kb.mm(ps[:, pb, :n], w[:, kt, d4 * 128:(d4 + 1) * 128], mgb[:, kt, :n], kt == 0, kt == 7, [w.R(0), w.R(1), mgb.R()], [PR[pb]])
                        kb.act(mg[:, dt, :n], ps[:, pb, :n], AF.Copy, [PR[pb]], [mg.R(dt)])
                        sq_ = sqy[si_ % 2]
                        si_ += 1
                        kb.act(sq_[:, :n], ps[:, pb, :n], AF.Square, [PR[pb]], [sq_.R()])
                        kb.mm(ps[:, ssb, :n], ones[:, :], sq_[:, :n], dt == 0, dt == 7, [ones.R(), sq_.R()], [PR[ssb]])
                post_norm_residual(1, t0, n, mg, ssb, W, ykeys=list(range(8)))
            kb.banks_rot = list(range(8))

    for l in range(depth_run):
        if l > 0:
            kb.barrier()
            kb.rotate()
        ada(l)
        ffn(l, 0, 0)
        with kb.scope():
            BR['t'] = kb.sb([128, 4, 2, T], BF16, "br")
            mla(l)
            gqa(l)
            s5(l)
            gmlp(l)
            merge(l)
        ffn(l, 1, 2)

    kb.barrier()
    for c in range(8):
        t0 = NCTX + c * 256
        kb.dma("sp", outT[:, :, c * 256:(c + 1) * 256], xT[:, :, t0:t0 + 256], xres(t0, t0 + 256), ())
    for ch in kb.sch:
        kb.E["sp"].obj.wait_ge(ch.sem, ch.total)
    return nc, es


def _rope_tables():
    def tab(rot_dim):
        axis_dim = rot_dim // 2
        half = axis_dim // 2
        inv = 10000.0 ** (-np.arange(0, axis_dim, 2, dtype=np.float32) / axis_dim)
        t = np.arange(NLAT)
        row = (t // 64).astype(np.float32)
        col = (t % 64).astype(np.float32)
        C = np.ones((rot_dim, T), np.float32)
        S = np.zeros((rot_dim, T), np.float32)
        partner = np.zeros(rot_dim, np.int64)
        for ax, pos in ((0, row), (1, col)):
            for r in range(axis_dim):
                k = r % half
                ang = pos * inv[k]
                C[ax * axis_dim + r, NCTX:] = np.cos(ang)
                if r < half:
                    S[ax * axis_dim + r, NCTX:] = -np.sin(ang)
                    partner[ax * axis_dim + r] = ax * axis_dim + r + half
                else:
                    S[ax * axis_dim + r, NCTX:] = np.sin(ang)
                    partner[ax * axis_dim + r] = ax * axis_dim + r - half
        return C, S, partner
    Cg, Sg, pg = tab(64)
    Cm, Sm, pm = tab(32)
    ropeC = np.zeros((128, T), np.float32)
    ropeS = np.zeros((128, T), np.float32)
    ropeC[0:64] = Cg
    ropeS[0:64] = Sg
    ropeC[64:96] = Cm
    ropeS[64:96] = Sm
    ropeS[96:128] = Sm
    return ropeC, ropeS, pg, pm


_CACHE = {}


def _prep(inp):
    f = lambda k: np.asarray(inp[k], np.float32)
    ropeC, ropeS, pg, pm = _rope_tables()
    P = {}
    P["ropeC"], P["ropeS"] = ropeC, ropeS
    cm = np.zeros((128, 3, 128), np.float32)
    cm[:, 0, :] = np.eye(128)
    qi = np.arange(128)[:, None]
    kj = np.arange(128)[None, :]
    cm[:, 1, :] = np.where(qi <= kj, 0.0, -30000.0)
    cm[:, 2, :] = np.where(kj <= qi, 0.0, -30000.0)
    P["cmat"] = cm
    P["w_ada"] = f("w_ada")
    vecs = np.zeros((DEPTH, 128, 120), np.float32)
    vecs[:, :, 0:72] = f("b_ada").reshape(DEPTH, 72, 128).transpose(0, 2, 1)
    vecs[:, :, 72:96] = f("norm_pre").reshape(DEPTH, 24, 128).transpose(0, 2, 1)
    vecs[:, :, 96:120] = f("norm_post").reshape(DEPTH, 24, 128).transpose(0, 2, 1)
    P["vecs"] = vecs
    P["w_ffn_in"] = f("w_ffn_in")
    P["w_ffn_out"] = f("w_ffn_out")
    w_in = f("w_in")
    idx = np.zeros(NEXT, np.int64)
    idx[E_KVL:E_KVL + 128] = np.arange(0, 128)
    idx[E_QL:E_QL + 256] = np.arange(672, 928)
    idx[E_KPA:E_KPA + 32] = 128 + np.arange(32)
    idx[E_KPB:E_KPB + 32] = 128 + pm
    gq = 928 + np.arange(256)
    gqs = 928 + (np.arange(4)[:, None] * 64 + pg[None, :]).reshape(-1)
    gk = 160 + np.arange(128)
    gks = 160 + (np.arange(2)[:, None] * 64 + pg[None, :]).reshape(-1)
    idx[E_GQA:E_GQA + 256] = gq
    idx[E_GQB:E_GQB + 256] = gqs
    idx[E_GKA:E_GKA + 128] = gk
    idx[E_GKB:E_GKB + 128] = gks
    idx[E_GV:E_GV + 128] = 288 + np.arange(128)
    idx[E_U:E_U + 256] = 416 + np.arange(256)
    idx[E_Z:E_Z + 512] = 1184 + np.arange(512)
    idx[E_GATE:E_GATE + 4096] = 1696 + np.arange(4096)
    P["w_in"] = np.ascontiguousarray(w_in[:, :, idx])
    wuq = f("mla_w_uq").reshape(DEPTH, 256, 4, 96)
    wuq_e = np.concatenate([wuq[..., 0:64], wuq[..., 64:96], wuq[..., 64 + pm]], axis=-1)
    P["w_uq"] = np.ascontiguousarray(wuq_e.reshape(DEPTH, 256, 512))
    wukv = f("mla_w_ukv").reshape(DEPTH, 128, 4, 128)
    P["w_ukv"] = np.ascontiguousarray(np.concatenate([wukv[..., 0:64].reshape(DEPTH, 128, 256), wukv[..., 64:128].reshape(DEPTH, 128, 256)], axis=-1))
    mvec = np.zeros((DEPTH, 128, 16), np.float32)
    mvec[:, :, 0] = f("mla_kv_norm")
    mvec[:, :, 1:3] = f("mla_q_norm").reshape(DEPTH, 2, 128).transpose(0, 2, 1)
    mvec[:, :, 3:5] = f("gmlp_norm").reshape(DEPTH, 2, 128).transpose(0, 2, 1)
    mvec[:, :, 5:7] = f("s5_d").reshape(DEPTH, 2, 128).transpose(0, 2, 1)
    mvec[:, 65, 8:12] = f("gqa_sink")
    mvec[:, :, 12:16] = f("s5_b_glu").reshape(DEPTH, 4, 128).transpose(0, 2, 1)
    P["mvec"] = mvec
    P["w_branch"] = f("w_branch")
    P["w_out"] = f("w_out")
    P["wsT"] = np.ascontiguousarray(f("gmlp_w_s").transpose(0, 3, 1, 2))
    bs = f("gmlp_b_s")
    P["bsT"] = np.ascontiguousarray(np.repeat(bs.reshape(DEPTH, 2, 2, 1, 128), 64, axis=3).reshape(DEPTH, 2, 128, 128).transpose(0, 2, 1, 3))
    def st(a):
        return a.reshape(DEPTH, 2, 8, 2, 64).transpose(0, 1, 3, 4, 2).reshape(DEPTH, 2, 128, 8)
    s5p = np.zeros((DEPTH, 2, 128, 8, 4), np.float32)
    s5p[..., 0] = st(f("s5_lam_re"))
    s5p[..., 1] = st(f("s5_lam_im"))
    s5p[..., 2] = st(np.repeat(f("s5_log_dt")[..., None], 64, axis=-1))
    P["s5p"] = s5p
    s5b = np.zeros((DEPTH, 2, 2, 128, 8, 128), np.float32)
    s5c = np.zeros((DEPTH, 2, 2, 128, 8, 128), np.float32)
    bre, bim = f("s5_b_re"), f("s5_b_im")
    cre, cim = f("s5_c_re"), f("s5_c_im")
    for g in range(16):
        k, half = g // 2, g % 2
        cs = (g % 8) * 16
        s5b[:, :, 0, half * 64:(half + 1) * 64, k, cs:cs + 16] = bre[:, :, g]
        s5b[:, :, 1, half * 64:(half + 1) * 64, k, cs:cs + 16] = bim[:, :, g]
        s5c[:, :, 0, half * 64:(half + 1) * 64, k, cs:cs + 16] = cre[:, :, g].transpose(0, 1, 3, 2)
        s5c[:, :, 1, half * 64:(half + 1) * 64, k, cs:cs + 16] = cim[:, :, g].transpose(0, 1, 3, 2)
    P["s5b"], P["s5c"] = s5b, s5c
    P["w_glu"] = f("s5_w_glu")
    return P


def _core_inputs(inp, P, b):
    x = np.asarray(inp["x"], np.float32)[b]
    ctx = np.asarray(inp["ctx"], np.float32)[b]
    full = np.concatenate([ctx, x], axis=0)
    xT = np.ascontiguousarray(full.T.reshape(8, 128, T).transpose(1, 0, 2))
    cv = np.stack([np.asarray(inp["c"], np.float32)[b], np.asarray(inp["c_ctx"], np.float32)], axis=-1)
    cT = np.ascontiguousarray(cv.reshape(8, 128, 2).transpose(1, 0, 2))
    m = dict(P)
    m["xT"] = xT
    m["cT"] = cT
    return m


def kernel(**inputs):
    P = _prep(inputs)
    nc, es = build(DEPTH)
    in_maps = [_core_inputs(inputs, P, b) for b in range(8)]
    res = run_bass_kernel_spmd(nc, in_maps, core_ids=list(range(8)))
    out = np.zeros((8, NLAT, D), np.float32)
    for b in range(8):
        oT = np.asarray(res.results[b]["outT"])
        out[b] = oT.transpose(2, 1, 0).reshape(NLAT, D)
    return out
```

```python
import numpy as np
import ml_dtypes
from contextlib import ExitStack, contextmanager
import concourse.bass as bass
import concourse.mybir as mybir
from concourse.bass_utils import run_bass_kernel_spmd

F32 = mybir.dt.float32
BF16 = mybir.dt.bfloat16
ALU = mybir.AluOpType
AF = mybir.ActivationFunctionType
AX = mybir.AxisListType

D = 1024
T = 2304
NCTX = 256
NLAT = 2048
DFF = 2816
DEPTH = 4
EPS = 1e-6
SAME_ENG_SYNC = True

E_KVL, E_QL, E_KPA, E_KPB = 0, 128, 384, 416
E_GQA, E_GQB, E_GKA, E_GKB, E_GV, E_U, E_Z, E_GATE = 512, 768, 1024, 1152, 1280, 1408, 1664, 2176
NEXT = 6272
TBLK = [(0, 256), (256, 768), (768, 1280), (1280, 1792), (1792, 2304)]
GC = 1.5957691216057308


class Res:
    __slots__ = ("w", "r")

    def __init__(self):
        self.w = None
        self.r = {}


class Eng:
    def __init__(self, obj, pe=False):
        self.obj = obj
        self.pe = pe
        self.sem = None
        self.cnt = 0
        self.known = {}
        self.pend = []


class Chan:
    def __init__(self, sem):
        self.sem = sem
        self.total = 0


class Tl:
    def __init__(self, h):
        self.h = h
        self._r = {}

    def __getitem__(self, k):
        return self.h[k]

    def R(self, key=0):
        r = self._r.get(key)
        if r is None:
            r = self._r[key] = Res()
        return r


class KB:
    def __init__(self, nc, es):
        self.nc = nc
        self.es = es
        self.stack = [es]
        self.E = {"pe": Eng(nc.tensor, True), "act": Eng(nc.scalar), "dve": Eng(nc.vector),
                  "pool": Eng(nc.gpsimd), "sp": Eng(nc.sync)}
        self.nsem = 0
        self.nname = 0
        self.rotate()
        self.wch = [Chan(self.newsem()) for _ in range(6)]
        self.sch = [Chan(self.newsem()) for _ in range(4)]
        self.wi = 0
        self.si = 0
        self.ps = Tl(es.enter_context(nc.psum_tensor("ps", [128, 8, 512], F32)))
        self.bank_i = 0
        self.banks_rot = list(range(8))

    def newsem(self):
        self.nsem += 1
        return self.es.enter_context(self.nc.semaphore(f"sem{self.nsem}"))

    def rotate(self):
        for e in self.E.values():
            assert not e.pend
            e.sem = self.newsem()
            e.cnt = 0

    def sb(self, shape, dt, name=None):
        self.nname += 1
        return Tl(self.stack[-1].enter_context(self.nc.sbuf_tensor(f"{name or 't'}_{self.nname}", list(shape), dt)))

    @contextmanager
    def scope(self):
        st = ExitStack()
        self.stack.append(st)
        try:
            yield
        finally:
            self.barrier()
            self.stack.pop()
            st.close()

    def barrier(self):
        chans = self.wch + self.sch
        for e in self.E.values():
            assert not e.pend
        for e in self.E.values():
            for f in self.E.values():
                if f is e or f.cnt == 0:
                    continue
                if e.known.get(id(f.sem), 0) < f.cnt:
                    e.obj.wait_ge(f.sem, f.cnt)
                    e.known[id(f.sem)] = f.cnt
            for c in chans:
                if c.total and e.known.get(id(c.sem), 0) < c.total:
                    e.obj.wait_ge(c.sem, c.total)
                    e.known[id(c.sem)] = c.total

    def bank(self):
        b = self.banks_rot[self.bank_i % len(self.banks_rot)]
        self.bank_i += 1
        return b

    def _waits(self, eng, reads, writes):
        deps = {}

        def add(sv):
            sem, val = sv
            k = id(sem)
            if k not in deps or deps[k][1] < val:
                deps[k] = (sem, val)

        for r in reads:
            if r.w is not None:
                add(r.w)
        for w in writes:
            if w.w is not None and w.w[0] is not eng.sem:
                add(w.w)
            for sv in w.r.values():
                if sv[0] is not eng.sem:
                    add(sv)
        for k, (sem, val) in deps.items():
            if sem is eng.sem and (eng.pe or not SAME_ENG_SYNC):
                continue
            if eng.known.get(k, 0) >= val:
                continue
            eng.obj.wait_ge(sem, val)
            eng.known[k] = val

    def emit(self, en, fn, reads=(), writes=(), sig=True):
        eng = self.E[en]
        self._waits(eng, reads, writes)
        ins = fn(eng.obj)
        eng.pend.append((reads, writes))
        if sig:
            eng.cnt += 1
            ins.then_inc(eng.sem, 1)
            st = (eng.sem, eng.cnt)
            for rs, ws in eng.pend:
                for w in ws:
                    w.w = st
                    w.r = {}
                for r in rs:
                    r.r[id(eng.sem)] = st
            eng.pend = []
        return ins

    def dma(self, q, out, in_, reads=(), writes=(), weight=False):
        eng = self.E[q]
        if weight:
            ch = self.wch[self.wi % len(self.wch)]
            self.wi += 1
        else:
            ch = self.sch[self.si % len(self.sch)]
            self.si += 1
        self._waits(eng, reads, writes)
        if ch.total and eng.known.get(id(ch.sem), 0) < ch.total:
            eng.obj.wait_ge(ch.sem, ch.total)
            eng.known[id(ch.sem)] = ch.total
        ins = eng.obj.dma_start(out=out, in_=in_)
        ins.then_inc(ch.sem, 16)
        ch.total += 16
        st = (ch.sem, ch.total)
        for w in writes:
            w.w = st
            w.r = {}
        for r in reads:
            r.r[id(ch.sem)] = st

    def mm(self, out, lhsT, rhs, start, stop, r, w, sig=None):
        if sig is None:
            sig = stop
        self.emit("pe", lambda e: e.matmul(out, lhsT, rhs, start=start, stop=stop), r, w, sig)

    def act(self, out, in_, func, r, w, bias=None, scale=1.0):
        kw = {}
        if bias is not None:
            kw["bias"] = bias
        self.emit("act", lambda e: e.activation(out=out, in_=in_, func=func, scale=scale, **kw), r, w)

    def tt(self, out, in0, in1, op, r, w, en="dve"):
        self.emit(en, lambda e: e.tensor_tensor(out=out, in0=in0, in1=in1, op=op), r, w)

    def ts(self, out, in0, s1, s2, op0, op1, r, w, en="dve"):
        if s2 is None:
            self.emit(en, lambda e: e.tensor_scalar(out=out, in0=in0, scalar1=s1, scalar2=None, op0=op0), r, w)
        else:
            self.emit(en, lambda e: e.tensor_scalar(out=out, in0=in0, scalar1=s1, scalar2=s2, op0=op0, op1=op1), r, w)

    def stt(self, out, in0, sc, in1, op0, op1, r, w):
        self.emit("dve", lambda e: e.scalar_tensor_tensor(out=out, in0=in0, scalar=sc, in1=in1, op0=op0, op1=op1), r, w)

    def cp(self, out, in_, r, w, en="dve"):
        self.emit(en, lambda e: e.tensor_copy(out=out, in_=in_), r, w)

    def recip(self, out, in_, r, w):
        self.emit("dve", lambda e: e.reciprocal(out=out, in_=in_), r, w)

    def memset(self, ap, val, w, en="dve"):
        self.emit(en, lambda e: e.memset(ap, val), (), w)


def split_tok(t0, t1):
    out = []
    if t0 < NCTX:
        out.append((t0, min(t1, NCTX), 1))
    if t1 > NCTX:
        out.append((max(t0, NCTX), t1, 0))
    return out


def build(depth_run, dbg=None):
    nc = bass.Bass("TRN2", target_bir_lowering=False)
    es = ExitStack()
    dr = lambda name, shape, dt=F32: nc.dram_tensor(name, list(shape), dt, kind="ExternalInput").ap()
    I = {}
    I["xT"] = dr("xT", [128, 8, T])
    I["cT"] = dr("cT", [128, 8, 2])
    I["w_ada"] = dr("w_ada", [DEPTH, D, 9 * D])
    I["vecs"] = dr("vecs", [DEPTH, 128, 120])
    I["w_ffn_in"] = dr("w_ffn_in", [DEPTH, 2, D, 2 * DFF])
    I["w_ffn_out"] = dr("w_ffn_out", [DEPTH, 2, DFF, D])
    I["w_in"] = dr("w_in", [DEPTH, D, NEXT])
    I["w_uq"] = dr("w_uq", [DEPTH, 256, 512])
    I["w_ukv"] = dr("w_ukv", [DEPTH, 128, 512])
    I["mvec"] = dr("mvec", [DEPTH, 128, 16])
    I["ropeC"] = dr("ropeC", [128, T])
    I["ropeS"] = dr("ropeS", [128, T])
    I["cmat"] = dr("cmat", [128, 3, 128])
    I["w_branch"] = dr("w_branch", [DEPTH, 4, 256, D])
    I["w_out"] = dr("w_out", [DEPTH, D, D])
    I["wsT"] = dr("wsT", [DEPTH, 128, 4, 128])
    I["bsT"] = dr("bsT", [DEPTH, 128, 2, 128])
    I["s5p"] = dr("s5p", [DEPTH, 2, 128, 8, 4])
    I["s5b"] = dr("s5b", [DEPTH, 2, 2, 128, 8, 128])
    I["s5c"] = dr("s5c", [DEPTH, 2, 2, 128, 8, 128])
    I["w_glu"] = dr("w_glu", [DEPTH, 256, 512])
    outT = nc.dram_tensor("outT", [128, 8, NLAT], F32, kind="ExternalOutput").ap()
    dbg_aps = {}
    if dbg:
        for name, shape in dbg.items():
            dbg_aps[name] = nc.dram_tensor(name, list(shape), F32, kind="ExternalOutput").ap()

    kb = KB(nc, es)
    ps = kb.ps
    PR = [ps.R(b) for b in range(8)]

    xT = kb.sb([128, 8, T], F32, "xT")
    ones = kb.sb([128, 128], BF16, "ones")
    onesf = kb.sb([128, 512], F32, "onesf")
    cmat_f = kb.sb([128, 3, 128], F32, "cmatf")
    cmat = kb.sb([128, 3, 128], BF16, "cmat")
    ropeC = kb.sb([128, T], BF16, "ropeC")
    ropeS = kb.sb([128, T], BF16, "ropeS")
    sc = kb.sb([128, 8, 2], BF16, "sc")
    vecs = kb.sb([128, 120], F32, "vecs")
    mvec = kb.sb([128, 16], F32, "mvec")
    modT = kb.sb([128, 72, 2], F32, "modT")
    modD = kb.sb([128, 3, 2, 3, 8], F32, "modD")
    BR = {}

    XR = lambda blk: xT.R(blk)

    def xres(t0, t1):
        return [xT.R(c) for c in range(t0 // 288, (t1 - 1) // 288 + 1)]

    for c in range(8):
        kb.dma("sp", xT[:, :, c * 288:(c + 1) * 288], I["xT"][:, :, c * 288:(c + 1) * 288], (), [xT.R(c)])
    kb.memset(ones[:], 1.0, [ones.R()])
    kb.memset(onesf[:], 1.0, [onesf.R()])
    kb.dma("sp", cmat_f[:], I["cmat"], (), [cmat_f.R()])
    kb.cp(cmat[:], cmat_f[:], [cmat_f.R()], [cmat.R()])
    for hh in range(2):
        kb.dma("pool", ropeC[:, hh * 1152:(hh + 1) * 1152], I["ropeC"][:, hh * 1152:(hh + 1) * 1152], (), [ropeC.R(("h", hh))], weight=True)
        kb.dma("pool", ropeS[:, hh * 1152:(hh + 1) * 1152], I["ropeS"][:, hh * 1152:(hh + 1) * 1152], (), [ropeS.R(("h", hh))], weight=True)
    for rt in (ropeC, ropeS):
        kb.cp(rt[0:1, 0:1], rt[0:1, 0:1], [rt.R(("h", 0)), rt.R(("h", 1))], [rt.R()])
    with kb.scope():
        ctf = kb.sb([128, 8, 2], F32, "ctf")
        kb.dma("sp", ctf[:], I["cT"], (), [ctf.R()])
        kb.act(sc[:], ctf[:], AF.Silu, [ctf.R()], [sc.R()])
    ident = cmat_f[:, 0, :]

    def rstd_from_ps(pb, n, out_ap, out_res, inv_n, eps, tmp):
        kb.act(tmp[:, :n], ps[:, pb, :n], AF.Sqrt, [PR[pb]], [tmp.R()], bias=epsT[:, 0:1] if eps == EPS else epsT[:, 1:2], scale=inv_n)
        kb.recip(out_ap, tmp[:, :n], [tmp.R()], [out_res])

    epsT = kb.sb([128, 2], F32, "epsT")
    kb.memset(epsT[:, 0:1], EPS, [epsT.R()])
    kb.memset(epsT[:, 1:2], 1e-5, [epsT.R()])

    def compute_h(s, t0, t1, hout, hres, W):
        n = t1 - t0
        sq, tmp, rstd, tmp2 = W["sq"], W["tmp"], W["rstd"], W["tmp2"]
        kb.act(sq[:, :, :n], xT[:, :, t0:t1], AF.Square, xres(t0, t1), [sq.R()])
        pb = kb.bank()
        for kt in range(8):
            kb.mm(ps[:, pb, :n], ones[:, :], sq[:, kt, :n], kt == 0, kt == 7, [ones.R(), sq.R()], [PR[pb]])
        rstd_from_ps(pb, n, rstd[:, :n], rstd.R(), 1.0 / D, EPS, tmp)
        for kt in range(8):
            for lo, hi, j in split_tok(t0, t1):
                a, b = lo - t0, hi - t0
                kb.stt(tmp2[:, a:b], xT[:, kt, lo:hi], modD[:, s, j, 0, kt:kt + 1], rstd[:, a:b], ALU.mult, ALU.mult,
                       xres(lo, hi) + [modD.R(), rstd.R()], [tmp2.R()])
                kb.act(hout[:, kt, a:b], tmp2[:, a:b], AF.Identity, [tmp2.R(), modD.R()], [hres],
                       bias=modD[:, s, j, 1, kt:kt + 1])

    def hwork(n):
        return {"sq": kb.sb([128, 8, n], BF16, "sq"), "tmp": kb.sb([128, 512], F32, "tmp"),
                "rstd": kb.sb([128, 512], F32, "rstd"), "tmp2": kb.sb([128, 512], F32, "tmp2")}

    def gelu(out_ap, in_ap, n_part, shape_free, G, rin, wout):
        g1, g2 = G["g1"], G["g2"]
        sl = (slice(0, n_part),) + tuple(slice(0, f) for f in shape_free)
        kb.act(g1[sl], in_ap, AF.Square, rin, [g1.R()])
        kb.ts(g1[sl], g1[sl], 0.044715, 1.0, ALU.mult, ALU.add, [g1.R()], [g1.R()])
        kb.tt(g1[sl], g1[sl], in_ap, ALU.mult, [g1.R()] + rin, [g1.R()])
        kb.act(g2[sl], g1[sl], AF.Sigmoid, [g1.R()], [g2.R()], scale=GC)
        kb.tt(out_ap, g2[sl], in_ap, ALU.mult, [g2.R()] + rin, wout)

    def ada(l):
        kb.dma("sp", vecs[:], I["vecs"][l], (), [vecs.R()])
        kb.dma("sp", mvec[:], I["mvec"][l], (), [mvec.R()])
        with kb.scope():
            wa = [kb.sb([128, 8, 1024], BF16, "wada") for _ in range(2)]
            pb = kb.bank()
            src = I["w_ada"][l].rearrange("(kt p) n -> p kt n", p=128)
            for m in range(9):
                w = wa[m % 2]
                for hh in range(2):
                    kb.dma("pool", w[:, hh * 4:(hh + 1) * 4, :], src[:, hh * 4:(hh + 1) * 4, m * 1024:(m + 1) * 1024], (), [w.R(hh)], weight=True)
                for ot in range(8):
                    col = (m * 8 + ot) * 2
                    for kt in range(8):
                        kb.mm(ps[:, pb, col:col + 2], w[:, kt, ot * 128:(ot + 1) * 128], sc[:, kt, :], kt == 0, kt == 7,
                              [w.R(kt // 4), sc.R()], [PR[pb]])
            for j in range(2):
                kb.tt(modT[:, :, j], ps[:, pb, 0:144].rearrange("p (m j) -> p m j", j=2)[:, :, j], vecs[:, 0:72], ALU.add,
                      [PR[pb], vecs.R()], [modT.R()])
            for s in range(3):
                for j in range(2):
                    wgt = 1.0 if s == 1 else 0.5
                    kb.stt(modD[:, s, j, 0, :], modT[:, (3 * s + 1) * 8:(3 * s + 2) * 8, j], 1.0, vecs[:, 72 + s * 8:72 + (s + 1) * 8],
                           ALU.add, ALU.mult, [modT.R(), vecs.R()], [modD.R()])
                    kb.cp(modD[:, s, j, 1, :], modT[:, (3 * s) * 8:(3 * s + 1) * 8, j], [modT.R()], [modD.R()])
                    kb.stt(modD[:, s, j, 2, :], modT[:, (3 * s + 2) * 8:(3 * s + 3) * 8, j], wgt, vecs[:, 96 + s * 8:96 + (s + 1) * 8],
                           ALU.mult, ALU.mult, [modT.R(), vecs.R()], [modD.R()])

    def post_norm_residual(s, t0, n, ybuf, ssb, W, ykeys=None):
        tmp, rstd, tmp2 = W["tmp"], W["rstd"], W["tmp2"]
        rstd_from_ps(ssb, n, rstd[:, :n], rstd.R(), 1.0 / D, EPS, tmp)
        for kt in range(8):
            kb.tt(tmp2[:, :n], ybuf[:, kt, :n], rstd[:, :n], ALU.mult, [ybuf.R(kt if ykeys else 0), rstd.R()], [tmp2.R()])
            for lo, hi, j in split_tok(t0, t0 + n):
                a, b = lo - t0, hi - t0
                kb.stt(xT[:, kt, lo:hi], tmp2[:, a:b], modD[:, s, j, 2, kt:kt + 1], xT[:, kt, lo:hi], ALU.mult, ALU.add,
                       [tmp2.R(), modD.R()] + xres(lo, hi), xres(lo, hi))

    def ffn(l, jf, s):
        TB, NC_ = 576, 288
        with kb.scope():
            W = hwork(NC_)
            hT = kb.sb([128, 8, TB], BF16, "hT")
            hid = kb.sb([128, 22, TB], BF16, "hid")
            ybuf = [kb.sb([128, 8, NC_], F32, "ybuf") for _ in range(2)]
            win = [kb.sb([128, 8, 2, 256], BF16, "win") for _ in range(2)]
            wout = [kb.sb([128, 22, 256], BF16, "wout") for _ in range(2)]
            sa = [kb.sb([128, NC_], F32, "sa") for _ in range(2)]
            sqy = [kb.sb([128, NC_], BF16, "sqy") for _ in range(2)]
            src_in = I["w_ffn_in"][l, jf].rearrange("(kt p) (two f) -> p kt two f", p=128, two=2)
            src_out = I["w_ffn_out"][l, jf].rearrange("(ft p) d -> p ft d", p=128)
            wi = wo = si = 0
            for blk in range(4):
                t0 = blk * TB
                kb.banks_rot = list(range(8))
                for c in range(2):
                    compute_h(s, t0 + c * NC_, t0 + (c + 1) * NC_, hT[:, :, c * NC_:(c + 1) * NC_], hT.R(c), W)
                for fg in range(11):
                    w = win[wi % 2]
                    wi += 1
                    for gi in range(2):
                        kb.dma("pool", w[:, :, gi, :], src_in[:, :, gi, fg * 256:(fg + 1) * 256], (), [w.R(gi)], weight=True)
                    for fi in range(2):
                        f = fg * 2 + fi
                        for c in range(2):
                            ba, bg = kb.bank(), kb.bank()
                            for gi, pb in ((0, ba), (1, bg)):
                                for kt in range(8):
                                    kb.mm(ps[:, pb, :NC_], w[:, kt, gi, fi * 128:(fi + 1) * 128], hT[:, kt, c * NC_:(c + 1) * NC_],
                                          kt == 0, kt == 7, [w.R(gi), hT.R(c)], [PR[pb]])
                            sab = sa[si % 2]
                            si += 1
                            kb.act(sab[:], ps[:, ba, :NC_], AF.Silu, [PR[ba]], [sab.R()])
                            kb.tt(hid[:, f, c * NC_:(c + 1) * NC_], sab[:], ps[:, bg, :NC_], ALU.mult, [sab.R(), PR[bg]], [hid.R((f, c))])
                kb.banks_rot = list(range(6))
                ssb = (6, 7)
                for dg in range(4):
                    w = wout[wo % 2]
                    wo += 1
                    for hh in range(2):
                        kb.dma("pool", w[:, hh * 11:(hh + 1) * 11, :], src_out[:, hh * 11:(hh + 1) * 11, dg * 256:(dg + 1) * 256], (), [w.R(hh)], weight=True)
                    for di in range(2):
                        dt = dg * 2 + di
                        for c in range(2):
                            pb = kb.bank()
                            for ft in range(22):
                                kb.mm(ps[:, pb, :NC_], w[:, ft, di * 128:(di + 1) * 128], hid[:, ft, c * NC_:(c + 1) * NC_],
                                      ft == 0, ft == 21, [w.R(ft // 11), hid.R((ft, c))], [PR[pb]])
                            kb.act(ybuf[c][:, dt, :], ps[:, pb, :NC_], AF.Copy, [PR[pb]], [ybuf[c].R()])
                            sq_ = sqy[si % 2]
                            si += 1
                            kb.act(sq_[:], ps[:, pb, :NC_], AF.Square, [PR[pb]], [sq_.R()])
                            kb.mm(ps[:, ssb[c], :NC_], ones[:, :], sq_[:], dt == 0, dt == 7, [ones.R(), sq_.R()], [PR[ssb[c]]])
                for c in range(2):
                    post_norm_residual(s, t0 + c * NC_, NC_, ybuf[c], ssb[c], W)
            kb.banks_rot = list(range(8))

    def load_w(dst_tl, dst_ap, src_ap, key=0):
        kb.dma("pool", dst_ap, src_ap, (), [dst_tl.R(key)], weight=True)

    def mla(l):
        scale = 96.0 ** -0.5
        win_src = I["w_in"][l].rearrange("(kt p) n -> p kt n", p=128)
        with kb.scope():
            wuq = kb.sb([128, 2, 512], BF16, "wuq")
            wukv = kb.sb([128, 512], BF16, "wukv")
            kvn = kb.sb([128, T], BF16, "kvn")
            qn = kb.sb([128, 2, T], BF16, "qn")
            kpeR = kb.sb([128, T], BF16, "kpeR")
            sqb = kb.sb([128, 3, 512], BF16, "sqb")
            t1 = kb.sb([128, 512], F32, "t1")
            t2 = kb.sb([128, 512], F32, "t2")
            rs = kb.sb([128, 512], F32, "rs")
            kmax = kb.sb([128, 4], F32, "kmax")
            load_w(wuq, wuq[:], I["w_uq"][l].rearrange("(kt p) n -> p kt n", p=128))
            load_w(wukv, wukv[:], I["w_ukv"][l])
            mla_p1(l, win_src, kvn, qn, kpeR, sqb, t1, t2, rs)
            mla_p2(l, scale, wuq, wukv, kvn, qn, kpeR, sqb, t1, t2, kmax)

    def mla_p1(l, win_src, kvn, qn, kpeR, sqb, t1, t2, rs):
        with kb.scope():
            W = hwork(512)
            hb = kb.sb([128, 8, 512], BF16, "hb")
            wst = kb.sb([128, 8, 448], BF16, "wmla")
            raw = kb.sb([128, 3, 512], F32, "raw")
            for hh in range(2):
                load_w(wst, wst[:, hh * 4:(hh + 1) * 4, :], win_src[:, hh * 4:(hh + 1) * 4, 0:448], hh)
            WST = [wst.R(0), wst.R(1)]
            for bi, (t0, t1_) in enumerate(TBLK):
                n = t1_ - t0
                compute_h(1, t0, t1_, hb, hb.R(), W)
                pbs = [kb.bank() for _ in range(3)]
                for oi, (c0, pb) in enumerate(zip((E_KVL, E_QL, E_QL + 128), pbs)):
                    for kt in range(8):
                        kb.mm(ps[:, pb, :n], wst[:, kt, c0:c0 + 128], hb[:, kt, :n], kt == 0, kt == 7, WST + [hb.R()], [PR[pb]])
                    kb.act(raw[:, oi, :n], ps[:, pb, :n], AF.Copy, [PR[pb]], [raw.R(oi)])
                    kb.act(sqb[:, oi, :n], ps[:, pb, :n], AF.Square, [PR[pb]], [sqb.R(oi)])
                pb = kb.bank()
                kb.mm(ps[:, pb, :n], ones[:, :], sqb[:, 0, :n], True, True, [ones.R(), sqb.R(0)], [PR[pb]])
                rstd_from_ps(pb, n, rs[:, :n], rs.R(), 1.0 / 128, EPS, t1)
                kb.stt(kvn[:, t0:t1_], raw[:, 0, :n], mvec[:, 0:1], rs[:, :n], ALU.mult, ALU.mult, [raw.R(0), mvec.R(), rs.R()], [kvn.R(bi)])
                pb = kb.bank()
                for oi in (1, 2):
                    kb.mm(ps[:, pb, :n], ones[:, :], sqb[:, oi, :n], oi == 1, oi == 2, [ones.R(), sqb.R(oi)], [PR[pb]])
                rstd_from_ps(pb, n, rs[:, :n], rs.R(), 1.0 / 256, EPS, t1)
                for oi in (1, 2):
                    kb.stt(qn[:, oi - 1, t0:t1_], raw[:, oi, :n], mvec[:, oi:oi + 1], rs[:, :n], ALU.mult, ALU.mult,
                           [raw.R(oi), mvec.R(), rs.R()], [qn.R(bi)])
                pa, pbb = kb.bank(), kb.bank()
                for c0, pb in ((E_KPA, pa), (E_KPB, pbb)):
                    for kt in range(8):
                        kb.mm(ps[64:96, pb, :n], wst[:, kt, c0:c0 + 32], hb[:, kt, :n], kt == 0, kt == 7, WST + [hb.R()], [PR[pb]])
                kb.tt(t1[64:96, :n], ps[64:96, pa, :n], ropeC[64:96, t0:t1_], ALU.mult, [PR[pa], ropeC.R()], [t1.R()])
                kb.tt(t2[64:96, :n], ps[64:96, pbb, :n], ropeS[64:96, t0:t1_], ALU.mult, [PR[pbb], ropeS.R()], [t2.R()])
                kb.tt(kpeR[64:96, t0:t1_], t1[64:96, :n], t2[64:96, :n], ALU.add, [t1.R(), t2.R()], [kpeR.R(bi)])

    def mla_p2(l, scale, wuq, wukv, kvn, qn, kpeR, sqb, t1, t2, kmax):
        with kb.scope():
            KTt = [kb.sb([128, T], BF16, "KT") for _ in range(2)]
            QTt = [kb.sb([128, T], BF16, "QT") for _ in range(2)]
            V = kb.sb([128, 18, 256], BF16, "V")
            pT = [kb.sb([128, 512], BF16, "pT") for _ in range(2)]
            rd = kb.sb([128, 512], F32, "rd")
            for kbk in range(18):
                pb = kb.bank()
                kb.mm(ps[:, pb, :256], kvn[:, kbk * 128:(kbk + 1) * 128], wukv[:, 256:512], True, True,
                      [kvn.R(b) for b in range(5)] + [wukv.R()], [PR[pb]])
                kb.act(V[:, kbk, :], ps[:, pb, :256], AF.Copy, [PR[pb]], [V.R()])
            KVN = [kvn.R(b) for b in range(5)]
            QN = [qn.R(b) for b in range(5)]
            for pair in range(2):
                for hi_ in range(2):
                    kb.memset(KTt[hi_][96:97, :], 1.0, [KTt[hi_].R()])
                    kb.memset(kmax[:, 2 * pair + hi_:2 * pair + hi_ + 1], 0.0, [kmax.R()])
                for bi, (t0, t1_) in enumerate(TBLK):
                    n = t1_ - t0
                    pb = kb.bank()
                    kb.mm(ps[:, pb, :n], wukv[:, pair * 128:(pair + 1) * 128], kvn[:, t0:t1_], True, True, [wukv.R()] + KVN, [PR[pb]])
                    for hi_ in range(2):
                        KTh = KTt[hi_]
                        kb.act(KTh[0:64, t0:t1_], ps[hi_ * 64:(hi_ + 1) * 64, pb, :n], AF.Copy, [PR[pb]], [KTh.R()])
                        kb.cp(KTh[64:96, t0:t1_], kpeR[64:96, t0:t1_], [kpeR.R(bi)], [KTh.R()])
                        kb.act(sqb[0:96, 0, :n], KTh[0:96, t0:t1_], AF.Square, [KTh.R()], [sqb.R(0)])
                        p2 = kb.bank()
                        kb.mm(ps[:, p2, :n], ones[0:96, :], sqb[0:96, 0, :n], True, True, [ones.R(), sqb.R(0)], [PR[p2]])
                        kb.emit("dve", lambda e: e.tensor_reduce(out=t1[:, 0:1], in_=ps[:, p2, :n], axis=AX.X, op=ALU.max), [PR[p2]], [t1.R()])
                        hcol = 2 * pair + hi_
                        kb.tt(kmax[:, hcol:hcol + 1], kmax[:, hcol:hcol + 1], t1[:, 0:1], ALU.max, [kmax.R(), t1.R()], [kmax.R()])
                for hi_ in range(2):
                    hcol = 2 * pair + hi_
                    kb.act(kmax[:, hcol:hcol + 1], kmax[:, hcol:hcol + 1], AF.Sqrt, [kmax.R()], [kmax.R()])
                    kb.ts(kmax[:, hcol:hcol + 1], kmax[:, hcol:hcol + 1], -1.0, None, ALU.mult, None, [kmax.R()], [kmax.R()])
                for bi, (t0, t1_) in enumerate(TBLK):
                    n = t1_ - t0
                    for hi_ in range(2):
                        h = 2 * pair + hi_
                        QTh = QTt[hi_]
                        pb = kb.bank()
                        for k2 in range(2):
                            kb.mm(ps[:, pb, :n], wuq[:, k2, h * 128:(h + 1) * 128], qn[:, k2, t0:t1_], k2 == 0, k2 == 1, [wuq.R()] + QN, [PR[pb]])
                        kb.act(QTh[0:64, t0:t1_], ps[0:64, pb, :n], AF.Copy, [PR[pb]], [QTh.R()])
                        kb.tt(t1[64:96, :n], ps[64:96, pb, :n], ropeC[64:96, t0:t1_], ALU.mult, [PR[pb], ropeC.R()], [t1.R()])
                        kb.tt(t2[64:96, :n], ps[96:128, pb, :n], ropeS[96:128, t0:t1_], ALU.mult, [PR[pb], ropeS.R()], [t2.R()])
                        kb.tt(QTh[64:96, t0:t1_], t1[64:96, :n], t2[64:96, :n], ALU.add, [t1.R(), t2.R()], [QTh.R()])
                        kb.act(sqb[0:96, 0, :n], QTh[0:96, t0:t1_], AF.Square, [QTh.R()], [sqb.R(0)])
                        p2 = kb.bank()
                        kb.mm(ps[:, p2, :n], ones[0:96, :], sqb[0:96, 0, :n], True, True, [ones.R(), sqb.R(0)], [PR[p2]])
                        kb.act(t1[96:97, :n], ps[96:97, p2, :n], AF.Sqrt, [PR[p2]], [t1.R()])
                        kb.ts(QTh[96:97, t0:t1_], t1[96:97, :n], kmax[96:97, h:h + 1], None, ALU.mult, None, [t1.R(), kmax.R()], [QTh.R()])
                pti = 0
                for hi_ in range(2):
                    h = 2 * pair + hi_
                    KTh, QTh = KTt[hi_], QTt[hi_]
                    for gi, (q0, q1) in enumerate(TBLK):
                        nq = q1 - q0
                        kbs = list(range(2)) if gi == 0 else list(range(18))
                        po = kb.bank()
                        for ki, kbk in enumerate(kbs):
                            pS = kb.bank()
                            while pS == po:
                                pS = kb.bank()
                            kb.mm(ps[:, pS, :nq], KTh[0:97, kbk * 128:(kbk + 1) * 128], QTh[0:97, q0:q1], True, True, [KTh.R(), QTh.R()], [PR[pS]])
                            p_ = pT[pti % 2]
                            pti += 1
                            kb.act(p_[:, :nq], ps[:, pS, :nq], AF.Exp, [PR[pS]], [p_.R()], scale=scale)
                            last = ki == len(kbs) - 1
                            kb.mm(ps[0:64, po, :nq], V[:, kbk, h * 64:(h + 1) * 64], p_[:, :nq], ki == 0, last, [V.R(), p_.R()], [PR[po]], sig=last)
                            kb.mm(ps[64:128, po, :nq], ones[:, 0:64], p_[:, :nq], ki == 0, last, [ones.R(), p_.R()], [PR[po]], sig=True)
                        kb.recip(rd[0:64, :nq], ps[64:128, po, :nq], [PR[po]], [rd.R()])
                        kb.tt(BR['t'][hi_ * 64:(hi_ + 1) * 64, 0, pair, q0:q1], ps[0:64, po, :nq], rd[0:64, :nq], ALU.mult, [PR[po], rd.R()], [BR['t'].R((0, pair, gi))])

    def gqa(l):
        scale = 0.125
        win_src = I["w_in"][l].rearrange("(kt p) n -> p kt n", p=128)
        with kb.scope():
            W = hwork(512)
            hb = kb.sb([128, 8, 512], BF16, "hb")
            wst = kb.sb([128, 8, 896], BF16, "wgqa")
            QG = [kb.sb([128, T], BF16, "QG") for _ in range(4)]
            KG = [kb.sb([128, T], BF16, "KG") for _ in range(2)]
            VG = kb.sb([128, 18, 128], BF16, "VG")
            sqb = kb.sb([128, 512], BF16, "sqb")
            t1 = kb.sb([128, 512], F32, "t1")
            t2 = kb.sb([128, 512], F32, "t2")
            kmax = kb.sb([128, 2], F32, "kmax")
            sst = kb.sb([128, 4], F32, "sst")
            ksink = kb.sb([128, 4], BF16, "ksink")
            pT = [kb.sb([128, 5, 128], BF16, "pT") for _ in range(2)]
            psk = [kb.sb([128, 128], BF16, "psk") for _ in range(2)]
            rd = kb.sb([128, 128], F32, "rd")
            for hh in range(2):
                load_w(wst, wst[:, hh * 4:(hh + 1) * 4, :], win_src[:, hh * 4:(hh + 1) * 4, E_GQA:E_GQA + 896], hh)
            WST = [wst.R(0), wst.R(1)]
            kb.memset(sst[64:66, :], scale, [sst.R()])
            kb.dma("sp", sst[65:66, :], I["mvec"][l, 65:66, 8:12], (), [sst.R()])
            kb.memset(ksink[0:66, :], 0.0, [ksink.R()])
            kb.ts(ksink[64:66, :], sst[64:66, :], 1.0 / scale, None, ALU.mult, None, [sst.R()], [ksink.R()])
            for h in range(4):
                kb.memset(QG[h][64:66, :], 1.0, [QG[h].R()])
            for kv in range(2):
                kb.memset(KG[kv][64:66, :], 0.0, [KG[kv].R()])
                kb.memset(KG[kv][64:65, :], 1.0, [KG[kv].R()])
            kb.memset(kmax[:], 0.0, [kmax.R()])

            def rope_proj(cA, cB, dst, t0, t1_, n):
                pa, pbb = kb.bank(), kb.bank()
                for c0, pb in ((cA, pa), (cB, pbb)):
                    for kt in range(8):
                        kb.mm(ps[0:64, pb, :n], wst[:, kt, c0:c0 + 64], hb[:, kt, :n], kt == 0, kt == 7, WST + [hb.R()], [PR[pb]])
                kb.tt(t1[0:64, :n], ps[0:64, pa, :n], ropeC[0:64, t0:t1_], ALU.mult, [PR[pa], ropeC.R()], [t1.R()])
                kb.tt(t2[0:64, :n], ps[0:64, pbb, :n], ropeS[0:64, t0:t1_], ALU.mult, [PR[pbb], ropeS.R()], [t2.R()])
                kb.tt(dst[0:64, t0:t1_], t1[0:64, :n], t2[0:64, :n], ALU.add, [t1.R(), t2.R()], [dst.R()])

            def sumsq64(src, t0, t1_, n):
                kb.act(sqb[0:64, :n], src[0:64, t0:t1_], AF.Square, [src.R()], [sqb.R()])
                p2 = kb.bank()
                kb.mm(ps[:, p2, :n], ones[0:64, :], sqb[0:64, :n], True, True, [ones.R(), sqb.R()], [PR[p2]])
                return p2

            for bi, (t0, t1_) in enumerate(TBLK):
                n = t1_ - t0
                compute_h(1, t0, t1_, hb, hb.R(), W)
                for kv in range(2):
                    rope_proj(E_GKA - E_GQA + kv * 64, E_GKB - E_GQA + kv * 64, KG[kv], t0, t1_, n)
                    p2 = sumsq64(KG[kv], t0, t1_, n)
                    kb.emit("dve", lambda e: e.tensor_reduce(out=t1[:, 0:1], in_=ps[:, p2, :n], axis=AX.X, op=ALU.max), [PR[p2]], [t1.R()])
                    kb.tt(kmax[:, kv:kv + 1], kmax[:, kv:kv + 1], t1[:, 0:1], ALU.max, [kmax.R(), t1.R()], [kmax.R()])
                for cb in range(n // 128):
                    kbk = t0 // 128 + cb
                    pb = kb.bank()
                    for kt in range(8):
                        kb.mm(ps[:, pb, :128], hb[:, kt, cb * 128:(cb + 1) * 128], wst[:, kt, E_GV - E_GQA:E_GV - E_GQA + 128], kt == 0, kt == 7,
                              WST + [hb.R()], [PR[pb]])
                    kb.act(VG[:, kbk, :], ps[:, pb, :128], AF.Copy, [PR[pb]], [VG.R()])
                for h in range(4):
                    rope_proj(h * 64, E_GQB - E_GQA + h * 64, QG[h], t0, t1_, n)
                    p2 = sumsq64(QG[h], t0, t1_, n)
                    kb.act(QG[h][64:65, t0:t1_], ps[64:65, p2, :n], AF.Sqrt, [PR[p2]], [QG[h].R()])
            kb.act(kmax[:], kmax[:], AF.Sqrt, [kmax.R()], [kmax.R()])
            kb.ts(kmax[:], kmax[:], -1.0, None, ALU.mult, None, [kmax.R()], [kmax.R()])
            for h in range(4):
                kb.ts(QG[h][64:65, :], QG[h][64:65, :], kmax[64:65, h // 2:h // 2 + 1], None, ALU.mult, None, [QG[h].R(), kmax.R()], [QG[h].R()])
            it = 0
            for h in range(4):
                kv = h // 2
                for qb in range(18):
                    q0 = qb * 128
                    if qb < 2:
                        band = []
                    else:
                        nb = qb - 2
                        band = [(2 + nb + d, d) for d in (-1, 0, 1) if 0 <= nb + d < 16]
                    pband, pctx, po = kb.bank(), kb.bank(), kb.bank()
                    p_ = pT[it % 2]
                    pk = psk[it % 2]
                    it += 1
                    for i, (kbk, d) in enumerate(band):
                        kb.mm(ps[:, pband, i * 128:(i + 1) * 128], KG[kv][0:66, kbk * 128:(kbk + 1) * 128], QG[h][0:66, q0:q0 + 128], True, d == 0,
                              [KG[kv].R(), QG[h].R()], [PR[pband]], sig=(d == 0))
                        if d != 0:
                            mi = 1 if d < 0 else 2
                            kb.mm(ps[:, pband, i * 128:(i + 1) * 128], cmat[:, mi, :], cmat[:, 0, :], False, True, [cmat.R()], [PR[pband]], sig=True)
                    for i in range(2):
                        kb.mm(ps[:, pctx, i * 128:(i + 1) * 128], KG[kv][0:66, i * 128:(i + 1) * 128], QG[h][0:66, q0:q0 + 128], True, True,
                              [KG[kv].R(), QG[h].R()], [PR[pctx]])
                    kb.mm(ps[0:1, pctx, 256:384], ksink[0:66, h:h + 1], QG[h][0:66, q0:q0 + 128], True, True, [ksink.R(), QG[h].R()], [PR[pctx]])
                    nb_ = len(band)
                    if nb_:
                        kb.act(p_[:, 0:nb_, :], ps[:, pband, 0:nb_ * 128].rearrange("p (a b) -> p a b", b=128), AF.Exp, [PR[pband]], [p_.R()], scale=scale)
                    kb.act(p_[:, 3:5, :], ps[:, pctx, 0:256].rearrange("p (a b) -> p a b", b=128), AF.Exp, [PR[pctx]], [p_.R()], scale=scale)
                    kb.act(pk[0:1, :], ps[0:1, pctx, 256:384], AF.Exp, [PR[pctx]], [pk.R()], scale=scale)
                    items = [(kbk, i) for i, (kbk, d) in enumerate(band)] + [(0, 3), (1, 4)]
                    for ii, (kbk, pi) in enumerate(items):
                        kb.mm(ps[0:64, po, :128], VG[:, kbk, kv * 64:(kv + 1) * 64], p_[:, pi, :], ii == 0, ii == len(items) - 1, [VG.R(), p_.R()], [PR[po]], sig=False)
                        kb.mm(ps[64:128, po, :128], ones[:, 0:64], p_[:, pi, :], ii == 0, False, [ones.R(), p_.R()], [PR[po]], sig=False)
                    kb.mm(ps[64:128, po, :128], ones[0:1, 0:64], pk[0:1, :], False, True, [ones.R(), pk.R()], [PR[po]], sig=True)
                    kb.recip(rd[0:64, :], ps[64:128, po, :128], [PR[po]], [rd.R()])
                    kb.tt(BR['t'][(h % 2) * 64:(h % 2 + 1) * 64, 1, h // 2, q0:q0 + 128], ps[0:64, po, :128], rd[0:64, :], ALU.mult, [PR[po], rd.R()],
                          [BR['t'].R((1, h // 2, qb))])

    def gmlp(l):
        win_src = I["w_in"][l].rearrange("(kt p) n -> p kt n", p=128)
        with kb.scope():
            W = hwork(512)
            hb = kb.sb([128, 8, 512], BF16, "hb")
            wst = kb.sb([128, 8, 512], BF16, "wz")
            wsT = kb.sb([128, 4, 128], BF16, "wsT")
            bsT = kb.sb([128, 2, 128], F32, "bsT")
            zu = kb.sb([128, 2, 512], BF16, "zu")
            G = {"g1": kb.sb([128, 512], F32, "g1"), "g2": kb.sb([128, 512], F32, "g2")}
            vg = kb.sb([128, 256], F32, "vg")
            xn = kb.sb([128, 256], BF16, "xn")
            st6 = kb.sb([128, 6], F32, "st6")
            mv = kb.sb([128, 4], F32, "mv")
            mx = kb.sb([128, 128], F32, "mx")
            for hh in range(2):
                load_w(wst, wst[:, hh * 4:(hh + 1) * 4, :], win_src[:, hh * 4:(hh + 1) * 4, E_Z:E_Z + 512], hh)
            load_w(wsT, wsT[:], I["wsT"][l])
            kb.dma("sp", bsT[:], I["bsT"][l], (), [bsT.R()])
            WST = [wst.R(0), wst.R(1)]
            for bi, (t0, t1_) in enumerate(TBLK):
                n = t1_ - t0
                compute_h(1, t0, t1_, hb, hb.R(), W)
                for ot in range(2):
                    pb = kb.bank()
                    for kt in range(8):
                        kb.mm(ps[:, pb, :n], wst[:, kt, ot * 128:(ot + 1) * 128], hb[:, kt, :n], kt == 0, kt == 7, WST + [hb.R()], [PR[pb]])
                    gelu(zu[:, ot, :n], ps[:, pb, :n], 128, (n,), G, [PR[pb]], [zu.R(ot)])
                for cb in range(n // 128):
                    c0 = t0 + cb * 128
                    pb = kb.bank()
                    for kt in range(8):
                        kb.mm(ps[:, pb, :256], hb[:, kt, cb * 128:(cb + 1) * 128], wst[:, kt, 256:512], kt == 0, kt == 7, WST + [hb.R()], [PR[pb]])
                    gelu(vg[:, :], ps[:, pb, :256], 128, (256,), G, [PR[pb]], [vg.R()])
                    kb.emit("dve", lambda e: e.bn_stats(out=st6[:, :], in_=vg[:, :]), [vg.R()], [st6.R()])
                    kb.emit("dve", lambda e: e.bn_aggr(out=mv[:, 0:2], in_=st6[:, :]), [st6.R()], [mv.R()])
                    kb.act(mv[:, 2:3], mv[:, 1:2], AF.Sqrt, [mv.R()], [mv.R()], bias=epsT[:, 1:2])
                    kb.recip(mv[:, 3:4], mv[:, 2:3], [mv.R()], [mv.R()])
                    kb.ts(xn[:, :], vg[:, :], mv[:, 0:1], mv[:, 3:4], ALU.subtract, ALU.mult, [vg.R(), mv.R()], [xn.R()])
                    for gp in range(2):
                        pm = kb.bank()
                        for gg in range(2):
                            g = gp * 2 + gg
                            kb.mm(ps[gg * 64:(gg + 1) * 64, pm, :128], xn[:, g * 64:(g + 1) * 64], wsT[:, g, :], True, True, [xn.R(), wsT.R()], [PR[pm]])
                        kb.stt(mx[:, :], ps[:, pm, :128], mvec[:, 3 + gp:4 + gp], bsT[:, gp, :], ALU.mult, ALU.add, [PR[pm], mvec.R(), bsT.R()], [mx.R()])
                        kb.tt(BR['t'][:, 3, gp, c0:c0 + 128], mx[:, :], zu[:, gp, cb * 128:(cb + 1) * 128], ALU.mult, [mx.R(), zu.R(gp)], [BR['t'].R((3, gp, c0 // 128))])

    def s5(l):
        win_src = I["w_in"][l].rearrange("(kt p) n -> p kt n", p=128)
        with kb.scope():
            wglu = kb.sb([128, 2, 512], BF16, "wglu")
            uT = kb.sb([128, 2, T], BF16, "uT")
            uR = kb.sb([128, 2, T], BF16, "uR")
            yacc = kb.sb([128, 2, T], BF16, "yacc")
            load_w(wglu, wglu[:], I["w_glu"][l].rearrange("(kt p) n -> p kt n", p=128))
            with kb.scope():
                W = hwork(512)
                hb = kb.sb([128, 8, 512], BF16, "hb")
                wst = kb.sb([128, 8, 256], BF16, "wu")
                for hh in range(2):
                    load_w(wst, wst[:, hh * 4:(hh + 1) * 4, :], win_src[:, hh * 4:(hh + 1) * 4, E_U:E_U + 256], hh)
                WST = [wst.R(0), wst.R(1)]
                for bi, (t0, t1_) in enumerate(TBLK):
                    n = t1_ - t0
                    compute_h(1, t0, t1_, hb, hb.R(), W)
                    for ot in range(2):
                        pb = kb.bank()
                        for kt in range(8):
                            kb.mm(ps[:, pb, :n], wst[:, kt, ot * 128:(ot + 1) * 128], hb[:, kt, :n], kt == 0, kt == 7, WST + [hb.R()], [PR[pb]])
                        kb.act(uT[:, ot, t0:t1_], ps[:, pb, :n], AF.Copy, [PR[pb]], [uT.R()])
            kb.cp(uR[:, :, 0:NCTX], uT[:, :, 0:NCTX][:, :, ::-1], [uT.R()], [uR.R()])
            kb.cp(uR[:, :, NCTX:T], uT[:, :, NCTX:T][:, :, ::-1], [uT.R()], [uR.R()])
            with kb.scope():
                pp = kb.sb([128, 8, 4], F32, "pp")
                tabs = kb.sb([128, 4, 8, 128], F32, "tabs")
                sc1 = kb.sb([128, 12, 8], F32, "sc1")
                bT = kb.sb([128, 2, 8, 128], BF16, "bT")
                cP = kb.sb([128, 2, 8, 128], BF16, "cP")
                kc = kb.sb([128, 8, 2], F32, "kc")
                NA = kb.sb([128, 8, 2], F32, "NA")
                kt_ = kb.sb([128, 4], F32, "kt_")
                PI = float(np.pi)

                def sin_of(out_ap, in_ap, shift, r, w, s_a, s_b):
                    kb.ts(s_a, in_ap, shift + PI, 1.0 / (2 * PI), ALU.add, ALU.mult, r, [sc1.R()])
                    kb.ts(s_b, s_a, -0.5, 12582912.0, ALU.add, ALU.add, [sc1.R()], [sc1.R()])
                    kb.ts(s_b, s_b, -12582912.0, None, ALU.add, None, [sc1.R()], [sc1.R()])
                    kb.tt(s_a, s_a, s_b, ALU.subtract, [sc1.R()], [sc1.R()])
                    kb.ts(s_a, s_a, 2 * PI, -PI, ALU.mult, ALU.add, [sc1.R()], [sc1.R()])
                    kb.ts(s_a, s_a, PI, -PI, ALU.min, ALU.max, [sc1.R()], [sc1.R()])
                    kb.act(out_ap, s_a, AF.Sin, [sc1.R()], w)

                for d_ in range(2):
                  usrc = uT if d_ == 0 else uR
                  with kb.scope():
                    bst = kb.sb([128, 2, 8, 128], F32, "bst")
                    bb = kb.sb([128, 2, 8, 128], F32, "bb")
                    tA = kb.sb([128, 8, 128], F32, "tA")
                    tB = kb.sb([128, 8, 128], F32, "tB")
                    kb.dma("sp", pp[:], I["s5p"][l, d_], (), [pp.R()])
                    kb.dma("sp", bst[:], I["s5b"][l, d_].rearrange("r n k c -> n r k c"), (), [bst.R()])
                    S = lambda i: sc1[:, i, :]
                    R1 = [sc1.R()]
                    lr, li, ldt = pp[:, :, 0], pp[:, :, 1], pp[:, :, 2]
                    kb.ts(S(0), lr, -1e-4, None, ALU.min, None, [pp.R()], R1)
                    kb.act(S(1), ldt, AF.Exp, [pp.R()], R1)
                    kb.tt(S(2), S(0), S(1), ALU.mult, R1, R1)
                    kb.act(S(2), S(2), AF.Exp, R1, R1)
                    kb.tt(S(3), li, S(1), ALU.mult, [pp.R()] + R1, R1)
                    sin_of(S(4), S(3), 0.0, R1, R1, S(10), S(11))
                    sin_of(S(5), S(3), PI / 2, R1, R1, S(10), S(11))
                    kb.tt(S(4), S(4), S(2), ALU.mult, R1, R1)
                    kb.tt(S(5), S(5), S(2), ALU.mult, R1, R1)
                    kb.ts(S(6), S(5), -1.0, None, ALU.add, None, R1, R1)
                    kb.tt(S(7), S(0), S(0), ALU.mult, R1, R1)
                    kb.tt(S(8), li, li, ALU.mult, [pp.R()], R1)
                    kb.tt(S(7), S(7), S(8), ALU.add, R1, R1)
                    kb.recip(S(7), S(7), R1, R1)
                    kb.tt(S(8), S(6), S(0), ALU.mult, R1, R1)
                    kb.tt(S(9), S(4), li, ALU.mult, R1 + [pp.R()], R1)
                    kb.tt(S(8), S(8), S(9), ALU.add, R1, R1)
                    kb.tt(S(8), S(8), S(7), ALU.mult, R1, R1)
                    kb.tt(S(9), S(4), S(0), ALU.mult, R1, R1)
                    kb.tt(S(6), S(6), li, ALU.mult, R1 + [pp.R()], R1)
                    kb.tt(S(9), S(9), S(6), ALU.subtract, R1, R1)
                    kb.tt(S(9), S(9), S(7), ALU.mult, R1, R1)
                    bc = lambda i: sc1[:, i, :].unsqueeze(2).broadcast_to([128, 8, 128])
                    kb.tt(tA[:], bst[:, 0], bc(8), ALU.mult, [bst.R()] + R1, [tA.R()])
                    kb.tt(tB[:], bst[:, 1], bc(9), ALU.mult, [bst.R()] + R1, [tB.R()])
                    kb.tt(bb[:, 0], tA[:], tB[:], ALU.subtract, [tA.R(), tB.R()], [bb.R()])
                    kb.tt(tA[:], bst[:, 1], bc(8), ALU.mult, [bst.R()] + R1, [tA.R()])
                    kb.tt(tB[:], bst[:, 0], bc(9), ALU.mult, [bst.R()] + R1, [tB.R()])
                    kb.tt(bb[:, 1], tA[:], tB[:], ALU.add, [tA.R(), tB.R()], [bb.R()])
                    for ri in range(2):
                        for k in range(8):
                            pb = kb.bank()
                            kb.emit("pe", lambda e: e.transpose(ps[:, pb, 0:128], bb[:, ri, k, :], ident), [bb.R(), cmat_f.R()], [PR[pb]])
                            kb.act(bT[:, ri, k, :], ps[:, pb, 0:128], AF.Copy, [PR[pb]], [bT.R()])
                    kb.dma("pool", cP[:], I["s5c"][l, d_].rearrange("r n k c -> n r k c"), (), [cP.R()], weight=True)
                    kb.tt(S(6), S(2), S(2), ALU.mult, R1, R1)
                    kb.recip(S(6), S(6), R1, R1)
                    kb.tt(S(7), S(5), S(6), ALU.mult, R1, R1)
                    kb.tt(S(6), S(4), S(6), ALU.mult, R1, R1)
                    kb.ts(S(6), S(6), -1.0, None, ALU.mult, None, R1, R1)
                    TR = [tabs.R()]
                    for (tr, ti, pr0, pi0) in ((0, 1, 5, 4), (2, 3, 7, 6)):
                        kb.memset(tabs[:, tr, :, 0:1], 1.0, TR)
                        kb.memset(tabs[:, ti, :, 0:1], 0.0, TR)
                        kb.cp(S(10), S(pr0), R1, R1)
                        kb.cp(S(11), S(pi0), R1, R1)
                        m = 1
                        while m < 256:
                            mm_ = min(m, 128) if m < 128 else 1
                            if m < 128:
                                pr = sc1[:, 10, :].unsqueeze(2).broadcast_to([128, 8, m])
                                pi_ = sc1[:, 11, :].unsqueeze(2).broadcast_to([128, 8, m])
                                src_r, src_i = tabs[:, tr, :, 0:m], tabs[:, ti, :, 0:m]
                                kb.tt(tA[:, :, 0:m], src_r, pr, ALU.mult, TR + R1, [tA.R()])
                                kb.tt(tB[:, :, 0:m], src_i, pi_, ALU.mult, TR + R1, [tB.R()])
                                kb.tt(tabs[:, tr, :, m:2 * m], tA[:, :, 0:m], tB[:, :, 0:m], ALU.subtract, [tA.R(), tB.R()], TR)
                                kb.tt(tA[:, :, 0:m], src_r, pi_, ALU.mult, TR + R1, [tA.R()])
                                kb.tt(tB[:, :, 0:m], src_i, pr, ALU.mult, TR + R1, [tB.R()])
                                kb.tt(tabs[:, ti, :, m:2 * m], tA[:, :, 0:m], tB[:, :, 0:m], ALU.add, [tA.R(), tB.R()], TR)
                            if m < 128:
                                kb.tt(tA[:, :, 0], S(10), S(10), ALU.mult, R1, [tA.R()])
                                kb.tt(tB[:, :, 0], S(11), S(11), ALU.mult, R1, [tB.R()])
                                kb.tt(S(11), S(10), S(11), ALU.mult, R1, R1)
                                kb.ts(S(11), S(11), 2.0, None, ALU.mult, None, R1, R1)
                                kb.tt(S(10), tA[:, :, 0], tB[:, :, 0], ALU.subtract, [tA.R(), tB.R()], R1)
                            m *= 2
                        if tr == 0:
                            kb.cp(sc1[:, 0, :], S(10), R1, R1)
                            kb.cp(sc1[:, 1, :], S(11), R1, R1)
                  with kb.scope():
                    Wt = kb.sb([128, 2, 512], F32, "Wt")
                    Zt = kb.sb([128, 2, 512], F32, "Zt")
                    Ss = [kb.sb([128, 2, 512], BF16, "Ss") for _ in range(2)]
                    w4 = [kb.sb([128, 512], F32, "w4") for _ in range(4)]
                    S = lambda i: sc1[:, i, :]
                    R1 = [sc1.R()]
                    TR = [tabs.R()]
                    kb.ts(NA[:, :, 0], sc1[:, 1, :], -1.0, None, ALU.mult, None, R1, [NA.R()])
                    kb.cp(NA[:, :, 1], sc1[:, 1, :], R1, [NA.R()])
                    kb.memset(kc[:], 0.0, [kc.R()])
                    si_ = 0
                    for bi, (t0, t1_) in enumerate(TBLK):
                        n = t1_ - t0
                        nch = n // 128
                        pY = [kb.bank(), kb.bank()]
                        for k in range(8):
                            o = k // 4
                            pr_, pi_ = kb.bank(), kb.bank()
                            while pr_ in pY:
                                pr_ = kb.bank()
                            while pi_ in pY or pi_ == pr_:
                                pi_ = kb.bank()
                            kb.mm(ps[:, pr_, :n], bT[:, 0, k, :], usrc[:, o, t0:t1_], True, True, [bT.R(), usrc.R()], [PR[pr_]])
                            kb.mm(ps[:, pi_, :n], bT[:, 1, k, :], usrc[:, o, t0:t1_], True, True, [bT.R(), usrc.R()], [PR[pi_]])
                            tb = lambda i: tabs[:, i, k:k + 1, :].broadcast_to([128, nch, 128])
                            v3 = lambda ap: ap.rearrange("p (c j) -> p c j", j=128)
                            P_r, P_i = v3(ps[:, pr_, :n]), v3(ps[:, pi_, :n])
                            a0, a1, a2, a3 = [v3(w4[i][:, :n]) for i in range(4)]
                            kb.tt(a0, P_r, tb(2), ALU.mult, [PR[pr_]] + TR, [w4[0].R()])
                            kb.tt(a1, P_i, tb(3), ALU.mult, [PR[pi_]] + TR, [w4[1].R()])
                            kb.tt(a2, P_i, tb(2), ALU.mult, [PR[pi_]] + TR, [w4[2].R()])
                            kb.tt(a3, P_r, tb(3), ALU.mult, [PR[pr_]] + TR, [w4[3].R()])
                            kb.tt(Wt[:, 0, :n], w4[0][:, :n], w4[1][:, :n], ALU.subtract, [w4[0].R(), w4[1].R()], [Wt.R()])
                            kb.tt(Wt[:, 1, :n], w4[2][:, :n], w4[3][:, :n], ALU.add, [w4[2].R(), w4[3].R()], [Wt.R()])
                            for c in range(nch):
                                cs = slice(c * 128, (c + 1) * 128)
                                for ri in range(2):
                                    kb.emit("dve", lambda e: e.tensor_tensor_scan(out=Zt[:, ri, cs], data0=onesf[:, 0:128], data1=Wt[:, ri, cs],
                                                                                  initial=kc[:, k, ri:ri + 1], op0=ALU.mult, op1=ALU.add),
                                            [onesf.R(), Wt.R(), kc.R()], [Zt.R()])
                                e0 = c * 128 + 127
                                kb.tt(kt_[:, 0:2], Zt[:, ::-1, e0], NA[:, k, :], ALU.mult, [Zt.R(), NA.R()], [kt_.R()])
                                kb.stt(kc[:, k, :], Zt[:, :, e0], sc1[:, 0, k:k + 1], kt_[:, 0:2], ALU.mult, ALU.add, [Zt.R(), kt_.R()] + R1, [kc.R()])
                            Sb = Ss[si_ % 2]
                            si_ += 1
                            Z_r, Z_i = v3(Zt[:, 0, :n]), v3(Zt[:, 1, :n])
                            kb.tt(a0, Z_r, tb(0), ALU.mult, [Zt.R()] + TR, [w4[0].R()])
                            kb.tt(a1, Z_i, tb(1), ALU.mult, [Zt.R()] + TR, [w4[1].R()])
                            kb.tt(a2, Z_i, tb(0), ALU.mult, [Zt.R()] + TR, [w4[2].R()])
                            kb.tt(a3, Z_r, tb(1), ALU.mult, [Zt.R()] + TR, [w4[3].R()])
                            kb.tt(Sb[:, 0, :n], w4[0][:, :n], w4[1][:, :n], ALU.subtract, [w4[0].R(), w4[1].R()], [Sb.R()])
                            kb.stt(Sb[:, 1, :n], w4[2][:, :n], -1.0, w4[3][:, :n], ALU.mult, ALU.subtract, [w4[2].R(), w4[3].R()], [Sb.R()])
                            first, last = (k % 4 == 0), (k % 4 == 3)
                            kb.mm(ps[:, pY[o], :n], cP[:, 0, k, :], Sb[:, 0, :n], first, False, [cP.R(), Sb.R()], [PR[pY[o]]], sig=False)
                            kb.mm(ps[:, pY[o], :n], cP[:, 1, k, :], Sb[:, 1, :n], False, last, [cP.R(), Sb.R()], [PR[pY[o]]], sig=True)
                        for o in range(2):
                            if d_ == 0:
                                kb.stt(yacc[:, o, t0:t1_], uT[:, o, t0:t1_], mvec[:, 5 + o:6 + o], ps[:, pY[o], :n], ALU.mult, ALU.add,
                                       [uT.R(), mvec.R(), PR[pY[o]]], [yacc.R()])
                            else:
                                if bi == 0:
                                    dst = yacc[:, o, 0:NCTX][:, ::-1]
                                else:
                                    a_, b_ = t0 - NCTX, t1_ - NCTX
                                    lo_ = NCTX + (NLAT - b_)
                                    hi_ = NCTX + (NLAT - a_)
                                    dst = yacc[:, o, lo_:hi_][:, ::-1]
                                kb.tt(dst, dst, ps[:, pY[o], :n], ALU.add, [yacc.R(), PR[pY[o]]], [yacc.R()])
            with kb.scope():
                G = {"g1": kb.sb([128, 512], F32, "g1"), "g2": kb.sb([128, 512], F32, "g2")}
                yg = kb.sb([128, 2, 512], BF16, "yg")
                sg = kb.sb([128, 512], F32, "sg")
                for bi, (t0, t1_) in enumerate(TBLK):
                    n = t1_ - t0
                    for o in range(2):
                        gelu(yg[:, o, :n], yacc[:, o, t0:t1_], 128, (n,), G, [yacc.R()], [yg.R()])
                    for ot in range(2):
                        pa, pg = kb.bank(), kb.bank()
                        for (pb, cc) in ((pa, ot * 128), (pg, 256 + ot * 128)):
                            for k2 in range(2):
                                kb.mm(ps[:, pb, :n], wglu[:, k2, cc:cc + 128], yg[:, k2, :n], k2 == 0, k2 == 1, [wglu.R(), yg.R()], [PR[pb]])
                        kb.act(sg[:, :n], ps[:, pg, :n], AF.Sigmoid, [PR[pg], mvec.R()], [sg.R()], bias=mvec[:, 14 + ot:15 + ot])
                        kb.stt(BR['t'][:, 2, ot, t0:t1_], ps[:, pa, :n], mvec[:, 12 + ot:13 + ot], sg[:, :n], ALU.add, ALU.mult,
                               [PR[pa], mvec.R(), sg.R()], [BR['t'].R((2, ot, bi))])

    def merge(l):
        win_src = I["w_in"][l].rearrange("(kt p) n -> p kt n", p=128)
        wo_src = I["w_out"][l].rearrange("(kt p) n -> p kt n", p=128)
        br = BR['t']
        with kb.scope():
            W = hwork(512)
            hb = kb.sb([128, 8, 512], BF16, "hb")
            wg = [kb.sb([128, 8, 512], BF16, "wg") for _ in range(2)]
            wb = [kb.sb([128, 2, 1024], BF16, "wb") for _ in range(2)]
            mg = kb.sb([128, 8, 512], F32, "mg")
            mgb = kb.sb([128, 8, 512], BF16, "mgb")
            sg = [kb.sb([128, 512], F32, "sg") for _ in range(2)]
            tq = kb.sb([128, 512], F32, "tq")
            sqy = [kb.sb([128, 512], BF16, "sqy") for _ in range(2)]
            BRALL = [r for r in br._r.values()]
            gi_ = 0
            si_ = 0
            for bi, (t0, t1_) in enumerate(TBLK):
                n = t1_ - t0
                kb.banks_rot = list(range(7))
                compute_h(1, t0, t1_, hb, hb.R(), W)
                for i in range(4):
                    wbi = wb[i % 2]
                    load_w(wbi, wbi[:], I["w_branch"][l, i].rearrange("(kt p) n -> p kt n", p=128))
                    for half in range(2):
                        w = wg[gi_ % 2]
                        gi_ += 1
                        c0 = E_GATE + i * 1024 + half * 512
                        for hh in range(2):
                            load_w(w, w[:, hh * 4:(hh + 1) * 4, :], win_src[:, hh * 4:(hh + 1) * 4, c0:c0 + 512], hh)
                        for d4 in range(4):
                            dt = half * 4 + d4
                            pgt, ppj = kb.bank(), kb.bank()
                            for kt in range(8):
                                kb.mm(ps[:, pgt, :n], w[:, kt, d4 * 128:(d4 + 1) * 128], hb[:, kt, :n], kt == 0, kt == 7, [w.R(0), w.R(1), hb.R()], [PR[pgt]])
                            for k2 in range(2):
                                kb.mm(ps[:, ppj, :n], wbi[:, k2, dt * 128:(dt + 1) * 128], br[:, i, k2, t0:t1_], k2 == 0, k2 == 1, [wbi.R()] + BRALL, [PR[ppj]])
                            s_ = sg[si_ % 2]
                            si_ += 1
                            kb.act(s_[:, :n], ps[:, pgt, :n], AF.Sigmoid, [PR[pgt]], [s_.R()])
                            if i == 0:
                                kb.tt(mg[:, dt, :n], s_[:, :n], ps[:, ppj, :n], ALU.mult, [s_.R(), PR[ppj]], [mg.R(dt)])
                            else:
                                kb.tt(tq[:, :n], s_[:, :n], ps[:, ppj, :n], ALU.mult, [s_.R(), PR[ppj]], [tq.R()])
                                kb.tt(mg[:, dt, :n], mg[:, dt, :n], tq[:, :n], ALU.add, [mg.R(dt), tq.R()], [mg.R(dt)])
                for dt in range(8):
                    kb.act(mgb[:, dt, :n], mg[:, dt, :n], AF.Copy, [mg.R(dt)], [mgb.R()])
                ssb = 7
                for half in range(2):
                    w = wg[gi_ % 2]
                    gi_ += 1
                    for hh in range(2):
                        load_w(w, w[:, hh * 4:(hh + 1) * 4, :], wo_src[:, hh * 4:(hh + 1) * 4, half * 512:(half + 1) * 512], hh)
                    for d4 in range(4):
                        dt = half * 4 + d4
                        pb = kb.bank()
                        for kt in range(8):
                            kb.mm(ps[:, pb, :n], w[:, kt, d4 * 128:(d4 + 1) * 128], mgb[:, kt, :n], kt == 0, kt == 7, [w.R(0), w.R(1), mgb.R()], [PR[pb]])
                        kb.act(mg[:, dt, :n], ps[:, pb, :n], AF.Copy, [PR[pb]], [mg.R(dt)])
                        sq_ = sqy[si_ % 2]
                        si_ += 1
                        kb.act(sq_[:, :n], ps[:, pb, :n], AF.Square, [PR[pb]], [sq_.R()])
                        kb.mm(ps[:, ssb, :n], ones[:, :], sq_[:, :n], dt == 0, dt == 7, [ones.R(), sq_.R()], [PR[ssb]])
                post_norm_residual(1, t0, n, mg, ssb, W, ykeys=list(range(8)))
            kb.banks_rot = list(range(8))

    for l in range(depth_run):
        if l > 0:
            kb.barrier()
            kb.rotate()
        ada(l)
        ffn(l, 0, 0)
        with kb.scope():
            BR['t'] = kb.sb([128, 4, 2, T], BF16, "br")
            mla(l)
            gqa(l)
            s5(l)
            gmlp(l)
            merge(l)
        ffn(l, 1, 2)

    kb.barrier()
    for c in range(8):
        t0 = NCTX + c * 256
        kb.dma("sp", outT[:, :, c * 256:(c + 1) * 256], xT[:, :, t0:t0 + 256], xres(t0, t0 + 256), ())
    for ch in kb.sch:
        kb.E["sp"].obj.wait_ge(ch.sem, ch.total)
    return nc, es


def _rope_tables():
    def tab(rot_dim):
        axis_dim = rot_dim // 2
        half = axis_dim // 2
        inv = 10000.0 ** (-np.arange(0, axis_dim, 2, dtype=np.float32) / axis_dim)
        t = np.arange(NLAT)
        row = (t // 64).astype(np.float32)
        col = (t % 64).astype(np.float32)
        C = np.ones((rot_dim, T), np.float32)
        S = np.zeros((rot_dim, T), np.float32)
        partner = np.zeros(rot_dim, np.int64)
        for ax, pos in ((0, row), (1, col)):
            for r in range(axis_dim):
                k = r % half
                ang = pos * inv[k]
                C[ax * axis_dim + r, NCTX:] = np.cos(ang)
                if r < half:
                    S[ax * axis_dim + r, NCTX:] = -np.sin(ang)
                    partner[ax * axis_dim + r] = ax * axis_dim + r + half
                else:
                    S[ax * axis_dim + r, NCTX:] = np.sin(ang)
                    partner[ax * axis_dim + r] = ax * axis_dim + r - half
        return C, S, partner
    Cg, Sg, pg = tab(64)
    Cm, Sm, pm = tab(32)
    ropeC = np.zeros((128, T), np.float32)
    ropeS = np.zeros((128, T), np.float32)
    ropeC[0:64] = Cg
    ropeS[0:64] = Sg
    ropeC[64:96] = Cm
    ropeS[64:96] = Sm
    ropeS[96:128] = Sm
    return ropeC, ropeS, pg, pm


_CACHE = {}


def _prep(inp):
    f = lambda k: np.asarray(inp[k], np.float32)
    ropeC, ropeS, pg, pm = _rope_tables()
    P = {}
    P["ropeC"], P["ropeS"] = ropeC, ropeS
    cm = np.zeros((128, 3, 128), np.float32)
    cm[:, 0, :] = np.eye(128)
    qi = np.arange(128)[:, None]
    kj = np.arange(128)[None, :]
    cm[:, 1, :] = np.where(qi <= kj, 0.0, -30000.0)
    cm[:, 2, :] = np.where(kj <= qi, 0.0, -30000.0)
    P["cmat"] = cm
    P["w_ada"] = f("w_ada")
    vecs = np.zeros((DEPTH, 128, 120), np.float32)
    vecs[:, :, 0:72] = f("b_ada").reshape(DEPTH, 72, 128).transpose(0, 2, 1)
    vecs[:, :, 72:96] = f("norm_pre").reshape(DEPTH, 24, 128).transpose(0, 2, 1)
    vecs[:, :, 96:120] = f("norm_post").reshape(DEPTH, 24, 128).transpose(0, 2, 1)
    P["vecs"] = vecs
    P["w_ffn_in"] = f("w_ffn_in")
    P["w_ffn_out"] = f("w_ffn_out")
    w_in = f("w_in")
    idx = np.zeros(NEXT, np.int64)
    idx[E_KVL:E_KVL + 128] = np.arange(0, 128)
    idx[E_QL:E_QL + 256] = np.arange(672, 928)
    idx[E_KPA:E_KPA + 32] = 128 + np.arange(32)
    idx[E_KPB:E_KPB + 32] = 128 + pm
    gq = 928 + np.arange(256)
    gqs = 928 + (np.arange(4)[:, None] * 64 + pg[None, :]).reshape(-1)
    gk = 160 + np.arange(128)
    gks = 160 + (np.arange(2)[:, None] * 64 + pg[None, :]).reshape(-1)
    idx[E_GQA:E_GQA + 256] = gq
    idx[E_GQB:E_GQB + 256] = gqs
    idx[E_GKA:E_GKA + 128] = gk
    idx[E_GKB:E_GKB + 128] = gks
    idx[E_GV:E_GV + 128] = 288 + np.arange(128)
    idx[E_U:E_U + 256] = 416 + np.arange(256)
    idx[E_Z:E_Z + 512] = 1184 + np.arange(512)
    idx[E_GATE:E_GATE + 4096] = 1696 + np.arange(4096)
    P["w_in"] = np.ascontiguousarray(w_in[:, :, idx])
    wuq = f("mla_w_uq").reshape(DEPTH, 256, 4, 96)
    wuq_e = np.concatenate([wuq[..., 0:64], wuq[..., 64:96], wuq[..., 64 + pm]], axis=-1)
    P["w_uq"] = np.ascontiguousarray(wuq_e.reshape(DEPTH, 256, 512))
    wukv = f("mla_w_ukv").reshape(DEPTH, 128, 4, 128)
    P["w_ukv"] = np.ascontiguousarray(np.concatenate([wukv[..., 0:64].reshape(DEPTH, 128, 256), wukv[..., 64:128].reshape(DEPTH, 128, 256)], axis=-1))
    mvec = np.zeros((DEPTH, 128, 16), np.float32)
    mvec[:, :, 0] = f("mla_kv_norm")
    mvec[:, :, 1:3] = f("mla_q_norm").reshape(DEPTH, 2, 128).transpose(0, 2, 1)
    mvec[:, :, 3:5] = f("gmlp_norm").reshape(DEPTH, 2, 128).transpose(0, 2, 1)
    mvec[:, :, 5:7] = f("s5_d").reshape(DEPTH, 2, 128).transpose(0, 2, 1)
    mvec[:, 65, 8:12] = f("gqa_sink")
    mvec[:, :, 12:16] = f("s5_b_glu").reshape(DEPTH, 4, 128).transpose(0, 2, 1)
    P["mvec"] = mvec
    P["w_branch"] = f("w_branch")
    P["w_out"] = f("w_out")
    P["wsT"] = np.ascontiguousarray(f("gmlp_w_s").transpose(0, 3, 1, 2))
    bs = f("gmlp_b_s")
    P["bsT"] = np.ascontiguousarray(np.repeat(bs.reshape(DEPTH, 2, 2, 1, 128), 64, axis=3).reshape(DEPTH, 2, 128, 128).transpose(0, 2, 1, 3))
    def st(a):
        return a.reshape(DEPTH, 2, 8, 2, 64).transpose(0, 1, 3, 4, 2).reshape(DEPTH, 2, 128, 8)
    s5p = np.zeros((DEPTH, 2, 128, 8, 4), np.float32)
    s5p[..., 0] = st(f("s5_lam_re"))
    s5p[..., 1] = st(f("s5_lam_im"))
    s5p[..., 2] = st(np.repeat(f("s5_log_dt")[..., None], 64, axis=-1))
    P["s5p"] = s5p
    s5b = np.zeros((DEPTH, 2, 2, 128, 8, 128), np.float32)
    s5c = np.zeros((DEPTH, 2, 2, 128, 8, 128), np.float32)
    bre, bim = f("s5_b_re"), f("s5_b_im")
    cre, cim = f("s5_c_re"), f("s5_c_im")
    for g in range(16):
        k, half = g // 2, g % 2
        cs = (g % 8) * 16
        s5b[:, :, 0, half * 64:(half + 1) * 64, k, cs:cs + 16] = bre[:, :, g]
        s5b[:, :, 1, half * 64:(half + 1) * 64, k, cs:cs + 16] = bim[:, :, g]
        s5c[:, :, 0, half * 64:(half + 1) * 64, k, cs:cs + 16] = cre[:, :, g].transpose(0, 1, 3, 2)
        s5c[:, :, 1, half * 64:(half + 1) * 64, k, cs:cs + 16] = cim[:, :, g].transpose(0, 1, 3, 2)
    P["s5b"], P["s5c"] = s5b, s5c
    P["w_glu"] = f("s5_w_glu")
    return P


def _core_inputs(inp, P, b):
    x = np.asarray(inp["x"], np.float32)[b]
    ctx = np.asarray(inp["ctx"], np.float32)[b]
    full = np.concatenate([ctx, x], axis=0)
    xT = np.ascontiguousarray(full.T.reshape(8, 128, T).transpose(1, 0, 2))
    cv = np.stack([np.asarray(inp["c"], np.float32)[b], np.asarray(inp["c_ctx"], np.float32)], axis=-1)
    cT = np.ascontiguousarray(cv.reshape(8, 128, 2).transpose(1, 0, 2))
    m = dict(P)
    m["xT"] = xT
    m["cT"] = cT
    return m


def kernel(**inputs):
    P = _prep(inputs)
    nc, es = build(DEPTH)
    in_maps = [_core_inputs(inputs, P, b) for b in range(8)]
    res = run_bass_kernel_spmd(nc, in_maps, core_ids=list(range(8)))
    out = np.zeros((8, NLAT, D), np.float32)
    for b in range(8):
        oT = np.asarray(res.results[b]["outT"])
        out[b] = oT.transpose(2, 1, 0).reshape(NLAT, D)
    return out
```
